# Optimizing a Trainium2 kernel written in Bass

```python
import math
import jax, jax.numpy as jnp
from jax import lax
import numpy as np

D_MODEL = 1024
BATCH = 32
SEQ = 256
DEPTH = 1
DEC_BATCH = 2
DEC_SEQ = 2048
PAST_LEN = 256

GRID_W = 64
MIX_W = D_MODEL
DN_W = MIX_W // 2
CV_W = MIX_W - DN_W
DN_HEADS = 4
DN_DK = DN_W // DN_HEADS
DN_DV = DN_DK
SHORT_CONV = 3
CHUNK = 64
CONV_K = 31
D_FF = 2816
N_MOD = 9
EPS = 1e-6
SPLITS = (3 * DN_W, 3 * DN_W + 2 * DN_HEADS, 3 * DN_W + 4 * DN_HEADS, 4 * DN_W + 4 * DN_HEADS)
IN_COLS = 4 * DN_W + 4 * DN_HEADS + 2 * CV_W

kernel_name = "hybrid_gdn_conformer_diffusion_step"


def _rmsnorm(x, g):
    xf = x.astype(jnp.float32)
    y = xf * lax.rsqrt(jnp.mean(xf * xf, axis=-1, keepdims=True) + EPS)
    return (y * g.astype(jnp.float32)).astype(x.dtype)


def _layernorm(x, g, b):
    xf = x.astype(jnp.float32)
    mu = jnp.mean(xf, axis=-1, keepdims=True)
    var = jnp.mean(jnp.square(xf - mu), axis=-1, keepdims=True)
    y = (xf - mu) * lax.rsqrt(var + EPS) * g.astype(jnp.float32) + b.astype(jnp.float32)
    return y.astype(x.dtype)


def _l2norm(x):
    return x * lax.rsqrt(jnp.sum(x * x, axis=-1, keepdims=True) + EPS)


def _dwconv(x, w, n_rows):
    b, t, ch = x.shape
    k = w.shape[0]
    xr = x.reshape(b * n_rows, t // n_rows, ch)
    y = lax.conv_general_dilated(xr, w.astype(x.dtype)[:, None, :], window_strides=(1,),
                                 padding=[(k // 2, k // 2)],
                                 dimension_numbers=("NWC", "WIO", "NWC"),
                                 feature_group_count=ch)
    return y.reshape(b, t, ch)


def _gated_delta_chunked(q, k, v, g, beta, s0):
    b, t, h, dk = q.shape
    dv = v.shape[-1]
    n = t // CHUNK

    def blocks(a):
        a = a.reshape(b, n, CHUNK, h, *a.shape[3:])
        return jnp.moveaxis(a, (1, 3), (0, 2))

    qc, kc, vc, bc = blocks(q), blocks(k), blocks(v), blocks(beta)
    gc = jnp.cumsum(blocks(g), axis=-1)
    idx = jnp.arange(CHUNK)
    incl = idx[:, None] >= idx[None, :]
    strict = idx[:, None] > idx[None, :]
    decay = jnp.exp(jnp.where(incl, gc[..., :, None] - gc[..., None, :], -jnp.inf))
    kb = kc * bc[..., None]
    m = jnp.where(strict, jnp.einsum("nbhcd,nbhed->nbhce", kb, kc) * decay, 0.0)
    eye = jnp.eye(CHUNK, dtype=jnp.float32)
    tinv = lax.linalg.triangular_solve(m + eye, jnp.broadcast_to(eye, m.shape), left_side=True,
                                       lower=True, unit_diagonal=True)
    u = tinv @ (vc * bc[..., None])
    w = tinv @ (kb * jnp.exp(gc)[..., None])
    qk = jnp.einsum("nbhcd,nbhed->nbhce", qc, kc) * decay
    q_dec = qc * jnp.exp(gc)[..., None]
    k_dec = kc * jnp.exp(gc[..., -1:] - gc)[..., None]
    g_last = jnp.exp(gc[..., -1])

    def step(s, xs):
        u_n, w_n, qk_n, qd_n, kd_n, gl_n = xs
        v_new = u_n - w_n @ s
        o = qd_n @ s + qk_n @ v_new
        s = s * gl_n[..., None, None] + jnp.swapaxes(kd_n, -1, -2) @ v_new
        return s, o

    s_fin, o = lax.scan(step, s0, (u, w, qk, q_dec, k_dec, g_last))
    o = jnp.moveaxis(o, (0, 2), (1, 3)).reshape(b, t, h, dv)
    return o, s_fin


def _bidir_gated_delta(q, k, v, g, beta, s0):
    rev = lambda a: jnp.flip(a, axis=1)
    o_f, s_f = _gated_delta_chunked(q, k, v, g[:, :, 0], beta[:, :, 0], s0[:, 0])
    o_b, s_b = _gated_delta_chunked(rev(q), rev(k), rev(v), rev(g[:, :, 1]), rev(beta[:, :, 1]), s0[:, 1])
    return o_f + rev(o_b), jnp.stack([s_f, s_b], axis=1)


def _token_mixer(h, s0, n_rows, w_in, dn_conv_w, dn_a_log, dn_dt_bias, dn_norm_g,
                 cv_dw_w, cv_dw_b, cv_ln_g, cv_ln_b, w_out):
    b, t, _ = h.shape
    proj = h @ w_in
    qkv, a, bb, z, glu = jnp.split(proj, SPLITS, axis=-1)
    qkv = jax.nn.silu(_dwconv(qkv, dn_conv_w, n_rows)).astype(jnp.float32)
    qkv = qkv.reshape(b, t, 3, DN_HEADS, DN_DK)
    q = _l2norm(qkv[:, :, 0]) * (DN_DK ** -0.5)
    k = _l2norm(qkv[:, :, 1])
    v = qkv[:, :, 2]
    a = a.astype(jnp.float32).reshape(b, t, 2, DN_HEADS)
    g = -jnp.exp(dn_a_log.astype(jnp.float32)) * jax.nn.softplus(a + dn_dt_bias.astype(jnp.float32))
    beta = jax.nn.sigmoid(bb.astype(jnp.float32).reshape(b, t, 2, DN_HEADS))
    o, s = _bidir_gated_delta(q, k, v, g, beta, s0.astype(jnp.float32))
    o = _rmsnorm(o, dn_norm_g) * jax.nn.silu(z.astype(jnp.float32).reshape(b, t, DN_HEADS, DN_DV))
    o = o.reshape(b, t, DN_W).astype(h.dtype)
    gv, gg = jnp.split(glu, 2, axis=-1)
    cv = gv * jax.nn.sigmoid(gg)
    cv = _dwconv(cv, cv_dw_w, n_rows) + cv_dw_b
    cv = jax.nn.silu(_layernorm(cv, cv_ln_g, cv_ln_b))
    y = jnp.concatenate([o, cv.astype(h.dtype)], axis=-1) @ w_out
    return y, s


def _swiglu(h, w_gu, w_down):
    gt, up = jnp.split(h @ w_gu, 2, axis=-1)
    return (jax.nn.silu(gt) * up) @ w_down


def _modulation(cond, w_mod, b_mod):
    m = jax.nn.silu(cond) @ w_mod + b_mod
    return m.reshape(cond.shape[0], 1, N_MOD, D_MODEL)


def _layer(x, mod, s0, n_rows, norm_g, ffn1_w_in, ffn1_w_out, w_in, dn_conv_w, dn_a_log,
           dn_dt_bias, dn_norm_g, cv_dw_w, cv_dw_b, cv_ln_g, cv_ln_b, w_out, ffn2_w_in, ffn2_w_out):
    def pre(i, xx):
        return _rmsnorm(xx, norm_g[2 * i]) * (1.0 + mod[:, :, 3 * i + 1]) + mod[:, :, 3 * i]

    def post(i, yy):
        return mod[:, :, 3 * i + 2] * _rmsnorm(yy, norm_g[2 * i + 1])

    x = x + 0.5 * post(0, _swiglu(pre(0, x), ffn1_w_in, ffn1_w_out))
    y, s = _token_mixer(pre(1, x), s0, n_rows, w_in, dn_conv_w, dn_a_log, dn_dt_bias, dn_norm_g,
                        cv_dw_w, cv_dw_b, cv_ln_g, cv_ln_b, w_out)
    x = x + post(1, y)
    x = x + 0.5 * post(2, _swiglu(pre(2, x), ffn2_w_in, ffn2_w_out))
    return x, s


def setup_inputs(seed: int = 0) -> dict:
    key = jax.random.key(seed)
    ks = jax.random.split(key, 24)
    f32 = jnp.float32
    nrm = lambda k, shape, scale: scale * jax.random.normal(k, shape, f32)
    dt = jnp.exp(jax.random.uniform(ks[13], (DEPTH, 2, DN_HEADS), f32, math.log(1e-3), math.log(1e-1)))
    return {
        "x_prompt": nrm(ks[0], (BATCH, SEQ, D_MODEL), 1.0),
        "x_sample": nrm(ks[1], (DEC_BATCH, DEC_SEQ, D_MODEL), 1.0),
        "state_delta": nrm(ks[2], (DEC_BATCH, DEPTH, 2, DN_HEADS, DN_DK, DN_DV), 0.1),
        "c": nrm(ks[3], (DEC_BATCH, D_MODEL), 1.0),
        "c_ctx": nrm(ks[4], (D_MODEL,), 1.0),
        "w_mod": nrm(ks[5], (DEPTH, D_MODEL, N_MOD * D_MODEL), 0.3 * D_MODEL ** -0.5),
        "b_mod": nrm(ks[6], (DEPTH, N_MOD * D_MODEL), 0.02),
        "norm_g": 1.0 + nrm(ks[7], (DEPTH, 6, D_MODEL), 0.02),
        "ffn1_w_in": nrm(ks[8], (DEPTH, D_MODEL, 2 * D_FF), D_MODEL ** -0.5),
        "ffn1_w_out": nrm(ks[9], (DEPTH, D_FF, D_MODEL), D_FF ** -0.5),
        "w_in": nrm(ks[10], (DEPTH, D_MODEL, IN_COLS), D_MODEL ** -0.5),
        "dn_conv_w": nrm(ks[11], (DEPTH, SHORT_CONV, 3 * DN_W), SHORT_CONV ** -0.5),
        "dn_a_log": jnp.log(jax.random.uniform(ks[12], (DEPTH, 2, DN_HEADS), f32, 1.0, 16.0)),
        "dn_dt_bias": dt + jnp.log(-jnp.expm1(-dt)),
        "dn_norm_g": 1.0 + nrm(ks[14], (DEPTH, DN_DV), 0.02),
        "cv_dw_w": nrm(ks[15], (DEPTH, CONV_K, CV_W), CONV_K ** -0.5),
        "cv_dw_b": nrm(ks[16], (DEPTH, CV_W), 0.02),
        "cv_ln_g": 1.0 + nrm(ks[17], (DEPTH, CV_W), 0.02),
        "cv_ln_b": nrm(ks[18], (DEPTH, CV_W), 0.02),
        "w_out": nrm(ks[19], (DEPTH, MIX_W, D_MODEL), MIX_W ** -0.5),
        "ffn2_w_in": nrm(ks[20], (DEPTH, D_MODEL, 2 * D_FF), D_MODEL ** -0.5),
        "ffn2_w_out": nrm(ks[21], (DEPTH, D_FF, D_MODEL), D_FF ** -0.5),
    }


def reference(x_prompt, x_sample, state_delta, c, c_ctx, w_mod, b_mod, norm_g, ffn1_w_in,
              ffn1_w_out, w_in, dn_conv_w, dn_a_log, dn_dt_bias, dn_norm_g, cv_dw_w, cv_dw_b,
              cv_ln_g, cv_ln_b, w_out, ffn2_w_in, ffn2_w_out):
    grid_rows = x_sample.shape[1] // GRID_W
    y_p, y_s = x_prompt, x_sample
    s_zero = jnp.zeros((x_prompt.shape[0], 2, DN_HEADS, DN_DK, DN_DV), jnp.float32)
    new_states = []
    for l in range(DEPTH):
        p = dict(norm_g=norm_g[l], ffn1_w_in=ffn1_w_in[l], ffn1_w_out=ffn1_w_out[l], w_in=w_in[l],
                 dn_conv_w=dn_conv_w[l], dn_a_log=dn_a_log[l], dn_dt_bias=dn_dt_bias[l],
                 dn_norm_g=dn_norm_g[l], cv_dw_w=cv_dw_w[l], cv_dw_b=cv_dw_b[l], cv_ln_g=cv_ln_g[l],
                 cv_ln_b=cv_ln_b[l], w_out=w_out[l], ffn2_w_in=ffn2_w_in[l], ffn2_w_out=ffn2_w_out[l])
        mod_ctx = _modulation(c_ctx[None, :], w_mod[l], b_mod[l])
        mod_lat = _modulation(c, w_mod[l], b_mod[l])
        y_p, s_ctx = _layer(y_p, mod_ctx, s_zero, 1, **p)
        new_states.append(s_ctx.astype(x_prompt.dtype))
        y_s, _ = _layer(y_s, mod_lat, state_delta[:, l], grid_rows, **p)
    new_state_delta = jnp.stack(new_states, axis=1)
    return (y_p, y_s, new_state_delta)
```

```python
import numpy as np
import concourse.bass as bass
import concourse.mybir as mybir
from concourse.bass_utils import run_bass_kernel_spmd
from contextlib import ExitStack

F32 = mybir.dt.float32
BF16 = mybir.dt.bfloat16
AF = mybir.ActivationFunctionType
ALU = mybir.AluOpType
AX = mybir.AxisListType

D = 1024
DC = 8
T = 2048
NT = T // 128
NG = T // 512
DFF = 2816
FC = DFF // 128
EPS = 1e-6
IN_COLS = 3088
NMASK = 16


class Buf:
    __slots__ = ("name", "w", "rs", "rdma", "frozen")

    def __init__(self, name, frozen=False):
        self.name = name
        self.w = None
        self.rs = {}
        self.rdma = []
        self.frozen = frozen


class Op:
    __slots__ = ("eng", "fn", "deps", "is_dma", "sig", "idx", "sem", "semval", "n")


class Prog:
    def __init__(self, nc, es, n_dma_sems=20):
        self.nc = nc
        self.ops = []
        self.engs = {"pe": nc.tensor, "act": nc.scalar, "dve": nc.vector,
                     "pool": nc.gpsimd, "sp": nc.sync}
        self.esem = {e: es.enter_context(nc.semaphore("es_" + e)) for e in self.engs}
        self.dsem = [es.enter_context(nc.semaphore("ds%d" % i)) for i in range(n_dma_sems)]
        self.dcnt = [0] * n_dma_sems
        self.dlast = [None] * n_dma_sems
        self.dnext = 0
        self.last = {e: None for e in self.engs}

    def _mk(self, eng, fn, reads, writes, is_dma):
        o = Op()
        o.eng, o.fn, o.is_dma, o.sig, o.idx = eng, fn, is_dma, False, 0
        o.sem = None
        o.semval = 0
        o.n = len(self.ops)
        deps = {}

        def add(d, raw):
            if d is None or d is o:
                return
            if (not d.is_dma) and (not is_dma) and d.eng == eng and not raw:
                return
            deps[d.n] = d

        for r in reads:
            add(r.w, True)
        for w in writes:
            add(w.w, False)
            for d in w.rs.values():
                add(d, False)
            for d in w.rdma:
                add(d, False)
        if is_dma:
            k = self.dnext
            self.dnext = (self.dnext + 1) % len(self.dsem)
            if self.dlast[k] is not None:
                deps[self.dlast[k].n] = self.dlast[k]
            self.dcnt[k] += 1
            o.sem = self.dsem[k]
            o.semval = 16 * self.dcnt[k]
            self.dlast[k] = o
        o.deps = list(deps.values())
        for d in o.deps:
            d.sig = True
        for w in writes:
            w.w = o
            w.rs = {}
            w.rdma = []
        for r in reads:
            if r.frozen:
                continue
            if is_dma:
                r.rdma.append(o)
            else:
                r.rs[eng] = o
        self.ops.append(o)
        self.last[eng] = o
        return o

    def op(self, eng, fn, reads=(), writes=()):
        return self._mk(eng, fn, reads, writes, False)

    def dma(self, q, fn, reads=(), writes=()):
        return self._mk(q, fn, reads, writes, True)

    def barrier(self):
        lasts = [o for o in self.last.values() if o is not None]
        dl = [o for o in self.dlast if o is not None]
        for e in self.engs:
            o = Op()
            o.eng, o.fn, o.is_dma, o.sig, o.idx = e, None, False, False, 0
            o.sem, o.semval, o.n = None, 0, len(self.ops)
            o.deps = [d for d in lasts + dl]
            for d in o.deps:
                d.sig = True
            self.ops.append(o)

    def emit(self):
        cnt = {e: 0 for e in self.engs}
        for o in self.ops:
            if (not o.is_dma) and o.sig and o.fn is not None:
                cnt[o.eng] += 1
                o.idx = cnt[o.eng]
        waited = {e: {} for e in self.engs}
        nwait = 0
        for o in self.ops:
            E = self.engs[o.eng]
            wt = waited[o.eng]
            for d in o.deps:
                if d.is_dma:
                    sem, val, key = d.sem, d.semval, id(d.sem)
                else:
                    if d.fn is None:
                        continue
                    sem, val, key = self.esem[d.eng], d.idx, d.eng
                if wt.get(key, 0) < val:
                    E.wait_ge(sem, val)
                    wt[key] = val
                    nwait += 1
            if o.fn is None:
                continue
            ins = o.fn()
            if o.is_dma:
                ins.then_inc(o.sem, 16)
            elif o.sig:
                ins.then_inc(self.esem[o.eng], 1)
        sp = self.engs["sp"]
        for k, d in enumerate(self.dlast):
            if d is not None:
                sp.wait_ge(d.sem, d.semval)
        self.stats = (len(self.ops), nwait, dict(cnt))


CF_ID, CF_A1, CF_A2, CF_A3, CF_A4, CF_ONE, CF_M512 = range(7)
NCF = 7
CB_ID, CB_MEAN, CB_ONE, CB_NBD16, CB_NOFF16, CB_NOFF32, CB_NOFF64 = range(7)
NCB = 7
NEGBIG = -30000.0
DN_WARM = 0


def make_consts():
    k = np.arange(128)[:, None]
    x = np.arange(128)[None, :]
    cf = np.zeros((128, NCF, 128), np.float32)
    cf[:, CF_ID] = (k == x)
    cf[:, CF_A1] = (k <= x)
    cf[:, CF_A2] = (k > x)
    cf[:, CF_A3] = (k >= x)
    cf[:, CF_A4] = (k < x)
    cf[:, CF_ONE] = 1.0
    cf[:, CF_M512] = 1.0 / 512.0
    cb = np.zeros((128, NCB, 128), np.float32)
    cb[:, CB_ID] = (k == x)
    cb[:, CB_MEAN] = 1.0 / 1024.0
    cb[:, CB_ONE] = 1.0
    cb[:, CB_NBD16] = -1.0 * (k // 16 == x // 16)
    for idx, b in ((CB_NOFF16, 16), (CB_NOFF32, 32), (CB_NOFF64, 64)):
        cb[:, idx] = -1.0 * ((k // (2 * b) == x // (2 * b)) & (k // b != x // b))
    return cf, cb


def gate_cums(P, nc, ntl, gB, EX, GATE, GSRC, cstf, CSTF, psum, PB, pbank):
    plan = [(CF_A1, 0), (CF_A2, 0), (CF_ONE, 0), (CF_A3, 4), (CF_A4, 4), (CF_ONE, 4)]
    for t in range(ntl):
        for i, (m, c0) in enumerate(plan):
            col = t * 24 + i * 4
            P.op("pe", lambda t=t, m=m, c0=c0, col=col: nc.tensor.matmul(
                psum[pbank][:, col:col + 4], cstf[:, m, :], gB[:, t, c0:c0 + 4], start=True, stop=True),
                reads=[CSTF, GSRC], writes=[PB[pbank]])
    P.op("act", lambda: nc.scalar.activation(out=EX[:].rearrange("p t c -> p (t c)"), in_=psum[pbank][:, 0:ntl * 24],
                                             func=AF.Exp), reads=[PB[pbank]], writes=[GATE])


def deltanet(P, nc, es, ntl, qkT, QKB, vT, VB, gB, bB, lnB, EX, GATE, cstf, cstb, CSTF, CST,
             psum, PB, S0_d, carry, CARRY, st_d, tmps, TMPS, finish_tile, sq3=None):
    sb = lambda name, shape, dty: es.enter_context(nc.sbuf_tensor("dn_" + name, shape, dty))
    HP = 2
    IDB = cstb[:, CB_ID, :]
    IDF = cstf[:, CF_ID, :]

    class Set:
        pass

    sets = []
    for d in range(2):
        S = Set()
        S.pairs = []
        for hp_ in range(2):
            Q = Set()
            pt = lambda name: sb("%s%d_%d" % (name, d, hp_), [128, 2, HP, 128], BF16)
            Q.NCt, Q.NCn, Q.P2, Q.P4, Q.RA, Q.RB = pt("NCt"), pt("NCn"), pt("P2"), pt("P4"), pt("RA"), pt("RB")
            Q.rhsE = Q.P4[:].rearrange("p v h d -> p (v h d)").bitcast(F32).rearrange("p (h d) -> p h d", h=HP)
            Q.EMi = Q.RA[:, 0]
            Q.Ers = Q.RA[:, 1]
            Q.ErC = sb("ErC%d_%d" % (d, hp_), [128, HP, 128], BF16)
            Q.B = {n: Buf("dn%d_%d_%s" % (d, hp_, n)) for n in ["NCt", "NCn", "P2", "P4", "RA", "RB", "ErC"]}
            Q.B["rhsE"] = Q.B["P4"]
            Q.B["EMi"] = Q.B["RA"]
            Q.B["Ers"] = Q.B["RA"]
            Q.bank = 2 * d + hp_
            S.pairs.append(Q)
        if d == 0 and sq3 is not None:
            S.ktok, S.vtok, S.kg = [q[:].rearrange("p (h d) -> p h d", h=4) for q in sq3]
        else:
            S.ktok = sb("ktok%d" % d, [128, 4, 128], BF16)[:]
            S.vtok = sb("vtok%d" % d, [128, 4, 128], BF16)[:]
            S.kg = sb("kg%d" % d, [128, 4, 128], BF16)[:]
        S.Yt = sb("Yt%d" % d, [128, 4, 128], BF16)
        S.kdec = [sb("kdec%d_%d" % (d, q), [128, 4, 128], BF16) for q in range(2)]
        S.QKt = [sb("QKt%d_%d" % (d, q), [128, 4, 128], BF16) for q in range(2)]
        S.Wt = [sb("Wt%d_%d" % (d, q), [128, 4, 128], BF16) for q in range(2)]
        S.bu = [sb("bu%d_%d" % (d, q), [128, 4, 128], BF16) for q in range(2)]
        S.vnew = sb("vnew%d" % d, [128, 4, 128], BF16)
        S.Sm = sb("Sm%d" % d, [128, 4, 128], F32)
        S.Sb = sb("Sb%d" % d, [128, 4, 128], BF16)
        S.B = {n: Buf("dn%d_%s" % (d, n)) for n in ["ktok", "vtok", "kg", "Yt", "vnew", "Sm", "Sb"]}
        for n in ("kdec", "QKt", "Wt", "bu"):
            for q in range(2):
                S.B[n + str(q)] = Buf("dn%d_%s%d" % (d, n, q))
        sets.append(S)

    def bc_h(ap2d, nh):
        return ap2d.unsqueeze(1).broadcast_to([128, nh, 128])

    def bc_c(ap2d):
        nh = ap2d.shape[1]
        return ap2d.unsqueeze(2).broadcast_to([128, nh, 128])

    def ps3(b, nh=4):
        return psum[b][:, 0:nh * 128].rearrange("p (h d) -> p h d", h=nh)

    def ps4(b):
        return psum[b][:].rearrange("p (v h d) -> p v h d", v=2, h=HP)

    MASKS = {0: (CF_A2, CF_A1, CF_A1, CF_A4),
             1: (CF_A4, CF_A3, CF_A3, CF_A2)}

    def mm(b, col, lhsT, rhs, start, stop, reads):
        P.op("pe", lambda: nc.tensor.matmul(psum[b][:, col * 128:(col + 1) * 128], lhsT, rhs, start=start, stop=stop),
             reads=reads, writes=[PB[b]])

    def inst_pre(t, d, q):
        S = sets[d]
        B = S.B
        ts = slice(t * 128, (t + 1) * 128)
        ex0 = d * 12
        b = S.pairs[0].bank
        for h in range(4):
            mm(b, h, qkT[:, 4 + h, ts], IDB, True, True, [QKB, CST])
        P.op("act", lambda: nc.scalar.copy(out=S.ktok, in_=ps3(b)), reads=[PB[b]], writes=[B["ktok"]])
        P.op("pool", lambda: nc.gpsimd.tensor_tensor(out=S.kg, in0=S.ktok, in1=bc_c(EX[:, t, ex0:ex0 + 4]),
                                                     op=ALU.mult), reads=[B["ktok"], GATE], writes=[B["kg"]])
        P.op("pool", lambda: nc.gpsimd.tensor_tensor(out=S.kdec[q][:], in0=S.ktok, in1=bc_c(EX[:, t, ex0 + 4:ex0 + 8]),
                                                     op=ALU.mult), reads=[B["ktok"], GATE], writes=[B["kdec%d" % q]])
        yield
        b2 = S.pairs[1].bank
        for h in range(4):
            mm(b2, h, vT[:, h, ts], IDB, True, True, [VB, CST])
        P.op("act", lambda: nc.scalar.copy(out=S.vtok, in_=ps3(b2)), reads=[PB[b2]], writes=[B["vtok"]])
        yield

    def inst_post(t, d, q):
        S = sets[d]
        B = S.B
        gc4 = slice(d * 4, d * 4 + 4)
        b = S.pairs[0].bank
        for h in range(4):
            mm(b, h, S.Yt[:, h, :], S.vtok[:, h, :], True, True, [B["Yt"], B["vtok"]])
        P.op("dve", lambda: nc.vector.tensor_tensor(out=S.bu[q][:], in0=ps3(b), in1=bc_c(bB[:, t, gc4]), op=ALU.mult),
             reads=[PB[b], GATE], writes=[B["bu%d" % q]])
        yield
        b2 = S.pairs[1].bank
        for h in range(4):
            mm(b2, h, S.kg[:, h, :], S.Yt[:, h, :], True, True, [B["kg"], B["Yt"]])
        P.op("act", lambda: nc.scalar.copy(out=S.Wt[q][:], in_=ps3(b2)), reads=[PB[b2]], writes=[B["Wt%d" % q]])
        yield

    def pair(t, d, hp, q):
        SI = sets[d]
        S = SI.pairs[hp]
        B = dict(S.B)
        B["QKt"] = SI.B["QKt%d" % q]
        B["Yt"] = SI.B["Yt"]
        ts = slice(t * 128, (t + 1) * 128)
        m_el, m_er, m_incl, m_strict = MASKS[d]
        hs = slice(hp * HP, (hp + 1) * HP)
        gcol = d * 4 + hp * HP

        def nb():
            return S.bank

        P.op("pool", lambda: nc.gpsimd.tensor_tensor(
            out=S.rhsE, in0=bc_h(cstf[:, m_er, :], HP), in1=bc_c(gB[:, t, gcol:gcol + HP]), op=ALU.mult),
            reads=[CSTF, GATE], writes=[B["rhsE"]])
        b = nb()
        rE = S.rhsE.rearrange("p h d -> p (h d)")
        P.op("pe", lambda: nc.tensor.matmul(psum[b][:, 0:256], cstf[:, m_el, :], rE, start=True, stop=True),
             reads=[CSTF, B["rhsE"]], writes=[PB[b]])
        P.op("act", lambda: nc.scalar.activation(out=S.EMi, in_=ps4(b)[:, 0], func=AF.Exp),
             reads=[PB[b]], writes=[B["EMi"]])
        for h in range(HP):
            P.op("act", lambda h=h: nc.scalar.activation(out=S.Ers[:, h, :], in_=ps4(b)[:, 0, h, :], func=AF.Exp,
                                                         bias=lnB[:, t, gcol + h:gcol + h + 1], scale=1.0),
                 reads=[PB[b], GATE], writes=[B["Ers"]])
        P.op("dve", lambda: nc.vector.tensor_tensor(out=S.EMi, in0=S.EMi, in1=bc_h(cstf[:, m_incl, :], HP),
                                                    op=ALU.mult), reads=[B["EMi"], CSTF], writes=[B["EMi"]])
        P.op("dve", lambda: nc.vector.tensor_tensor(out=S.Ers, in0=S.Ers, in1=bc_h(cstf[:, m_strict, :], HP),
                                                    op=ALU.mult), reads=[B["Ers"], CSTF], writes=[B["Ers"]])
        P.op("pool", lambda: nc.gpsimd.tensor_tensor(out=S.ErC[:], in0=S.Ers, in1=bc_h(cstb[:, CB_NBD16, :], HP),
                                                     op=ALU.mult), reads=[B["Ers"], CST], writes=[B["ErC"]])
        yield
        b1 = nb()
        for h in range(HP):
            mm(b1, h, qkT[:, 4 + hp * HP + h, ts], qkT[:, 4 + hp * HP + h, ts], True, True, [QKB])
            mm(b1, HP + h, qkT[:, 4 + hp * HP + h, ts], qkT[:, hp * HP + h, ts], True, True, [QKB])
        P.op("dve", lambda: nc.vector.tensor_tensor(out=S.NCt[:, 0], in0=ps4(b1)[:, 0], in1=S.Ers, op=ALU.mult),
             reads=[PB[b1], B["Ers"]], writes=[B["NCt"]])
        P.op("dve", lambda: nc.vector.tensor_tensor(out=S.NCt[:, 1], in0=ps4(b1)[:, 0], in1=S.ErC[:], op=ALU.mult),
             reads=[PB[b1], B["ErC"]], writes=[B["NCt"]])
        P.op("dve", lambda: nc.vector.tensor_tensor(out=SI.QKt[q][:, hs, :], in0=ps4(b1)[:, 1], in1=S.EMi, op=ALU.mult),
             reads=[PB[b1], B["EMi"]], writes=[B["QKt"]])
        P.op("pool", lambda: nc.gpsimd.tensor_tensor(out=S.RB[:, 0], in0=S.NCt[:, 1], in1=bc_h(IDB, HP), op=ALU.add),
             reads=[B["NCt"], CST], writes=[B["RB"]])
        yield
        b = nb()
        for v in range(2):
            for h in range(HP):
                mm(b, v * HP + h, S.NCt[:, v, h, :], IDB, True, True, [B["NCt"], CST])
        P.op("act", lambda b=b: nc.scalar.copy(out=S.NCn[:], in_=ps4(b)), reads=[PB[b]], writes=[B["NCn"]])
        P.op("pool", lambda: nc.gpsimd.tensor_tensor(out=S.RB[:, 1], in0=S.NCn[:, 1], in1=bc_h(IDB, HP), op=ALU.add),
             reads=[B["NCn"], CST], writes=[B["RB"]])
        yield
        Ct, Cn = S.NCt[:, 1], S.NCn[:, 1]
        Ntt, Nnn = S.NCt[:, 0], S.NCn[:, 0]

        def level(dst, dname, terms_t, terms_n, reads, mask=None, only_t=False, out_ap=None, add=None):
            b = nb()
            for _ in range(DN_WARM):
                P.op("pe", lambda: nc.tensor.matmul(psum[b][:], IDB, qkT[:, 0, 0:512], start=True, stop=True),
                     reads=[CST, QKB], writes=[PB[b]])
            for v, terms in ((0, terms_t), (1, terms_n)):
                if only_t and v == 1:
                    continue
                for h in range(HP):
                    n = len(terms)
                    for k, (l, r) in enumerate(terms):
                        lh = l if l is IDB else l[:, h, :]
                        rh = r if r is IDB else r[:, h, :]
                        P.op("pe", lambda lh=lh, rh=rh, k=k, n=n, v=v, h=h: nc.tensor.matmul(
                            psum[b][:, (v * HP + h) * 128:(v * HP + h + 1) * 128], lh, rh, start=(k == 0),
                            stop=(k == n - 1)), reads=reads + [CST], writes=[PB[b]])
            src = ps4(b)[:, 0] if only_t else ps4(b)
            o = out_ap if out_ap is not None else (dst[:, 0] if only_t else dst[:])
            if add is not None:
                P.op("dve", lambda: nc.vector.tensor_tensor(out=o, in0=src, in1=add[:], op=ALU.add),
                     reads=[PB[b]] + reads, writes=[B[dname]])
            elif mask is None:
                P.op("act", lambda: nc.scalar.copy(out=o, in_=src), reads=[PB[b]], writes=[B[dname]])
            else:
                mk = cstb[:, mask, :]
                mb = mk.unsqueeze(1).broadcast_to([128, HP, 128]) if only_t else \
                    mk.unsqueeze(1).unsqueeze(1).broadcast_to([128, 2, HP, 128])
                P.op("dve", lambda: nc.vector.tensor_tensor(out=o, in0=src, in1=mb, op=ALU.mult),
                     reads=[PB[b], CST], writes=[B[dname]])

        P2t, P2n = S.P2[:, 0], S.P2[:, 1]
        P4t, P4n = S.P4[:, 0], S.P4[:, 1]
        RAt, RAn = S.RA[:, 0], S.RA[:, 1]
        RBt, RBn = S.RB[:, 0], S.RB[:, 1]
        rd = [B["NCt"], B["NCn"], B["P2"], B["P4"], B["RA"], B["RB"]]
        level(S.P2, "P2", [(Cn, Ct)], [(Ct, Cn)], rd)
        yield
        level(S.RA, "RA", [(IDB, RBt), (RBn, P2t)], [(IDB, RBn), (P2t, RBn)], rd)
        yield
        level(S.P4, "P4", [(P2n, P2t)], [(P2t, P2n)], rd)
        yield
        level(S.RB, "RB", [(IDB, RAt), (RAn, P4t)], [(IDB, RAn), (P4t, RAn)], rd)
        yield
        level(S.P2, "P2", [(P4n, P4t)], [], rd, only_t=True)
        yield
        level(S.RA, "RA", [(IDB, RBt), (RBn, P2t)], [(IDB, RBn), (P2t, RBn)], rd)
        yield
        level(S.P4, "P4", [(Nnn, RAt)], [(Ntt, RAn)], rd, mask=CB_NOFF16)
        yield
        level(S.RB, "RB", [(RAn, P4t)], [(RAt, P4n)], rd, add=S.RA)
        yield
        level(S.P4, "P4", [(Nnn, RBt)], [(Ntt, RBn)], rd, mask=CB_NOFF32)
        yield
        level(S.RA, "RA", [(RBn, P4t)], [(RBt, P4n)], rd, add=S.RB)
        yield
        level(S.P4, "P4", [(Nnn, RAt)], [], rd, mask=CB_NOFF64, only_t=True)
        yield
        level(None, "Yt", [(RAn, P4t)], [], rd, only_t=True, out_ap=SI.Yt[:, hs, :], add=RAt)
        yield

    def scan_step(t, d, q, first, slot_start, slot_end):
        S = sets[d]
        B = S.B
        ts = slice(t * 128, (t + 1) * 128)
        gc4 = slice(d * 4, d * 4 + 4)
        ex0 = d * 12
        pa, pq, po, pS = 4, 5, 6, 7
        v3 = lambda ap: ap.rearrange("p (h d) -> p h d", h=4)
        Sm2 = S.Sm[:].rearrange("p h d -> p (h d)")
        if first:
            P.dma("sp", lambda: nc.sync.dma_start(out=Sm2, in_=S0_d[d]), writes=[B["Sm"]])
            P.op("act", lambda: nc.scalar.copy(out=S.Sb[:], in_=S.Sm[:]), reads=[B["Sm"]], writes=[B["Sb"]])
        elif slot_start:
            P.op("dve", lambda: nc.vector.tensor_scalar(out=S.Sm[:], in0=S.Sm[:], scalar1=carry[:, 0:1], scalar2=None,
                                                        op0=ALU.mult), reads=[B["Sm"], CARRY], writes=[B["Sm"]])
            P.op("act", lambda: nc.scalar.copy(out=S.Sb[:], in_=S.Sm[:]), reads=[B["Sm"]], writes=[B["Sb"]])
        for h in range(4):
            mm(pa, h, S.Wt[q][:, h, :], S.Sb[:, h, :], True, True, [B["Wt%d" % q], B["Sb"]])
        for h in range(4):
            mm(pq, h, qkT[:, h, ts], S.Sb[:, h, :], True, True, [QKB, B["Sb"]])
        yield
        tA, TA = tmps[0], TMPS[0]
        P.op("dve", lambda: nc.vector.tensor_tensor(out=v3(tA[:]), in0=ps3(pa), in1=bc_c(bB[:, t, gc4]), op=ALU.mult),
             reads=[PB[pa], GATE], writes=[TA])
        P.op("pool", lambda: nc.gpsimd.tensor_tensor(out=S.vnew[:], in0=S.bu[q][:], in1=v3(tA[:]), op=ALU.subtract),
             reads=[B["bu%d" % q], TA], writes=[B["vnew"]])
        yield
        for h in range(4):
            mm(po, h, S.QKt[q][:, h, :], S.vnew[:, h, :], True, True, [B["QKt%d" % q], B["vnew"]])
        for h in range(4):
            mm(pS, h, S.kdec[q][:, h, :], S.vnew[:, h, :], True, True, [B["kdec%d" % q], B["vnew"]])
        yield
        tB, TB = tmps[1], TMPS[1]
        P.op("pool", lambda: nc.gpsimd.tensor_tensor(out=v3(tB[:]), in0=S.Sm[:], in1=bc_c(EX[:, t, ex0 + 8:ex0 + 12]),
                                                     op=ALU.mult), reads=[B["Sm"], GATE], writes=[TB])
        P.op("dve", lambda: nc.vector.tensor_tensor(out=S.Sm[:], in0=ps3(pS), in1=v3(tB[:]), op=ALU.add),
             reads=[PB[pS], TB], writes=[B["Sm"]])
        P.op("act", lambda: nc.scalar.copy(out=S.Sb[:], in_=S.Sm[:]), reads=[B["Sm"]], writes=[B["Sb"]])
        if slot_end:
            slot = t // 2
            P.dma("sp", lambda: nc.sync.dma_start(out=st_d[slot, d], in_=Sm2), reads=[B["Sm"]])
        yield
        finish_tile(t, d, pq, po, EX[:, t, ex0:ex0 + 4])
        yield

    def run(gens):
        gens = list(gens)
        while gens:
            for g in list(gens):
                try:
                    next(g)
                except StopIteration:
                    gens.remove(g)

    scan_lock = [None]

    def locked_scan(key, g):
        while scan_lock[0] is not None and scan_lock[0] != key:
            yield
        scan_lock[0] = key
        yield from g
        scan_lock[0] = None

    def rr(gens):
        gens = list(gens)
        while gens:
            for g in list(gens):
                try:
                    next(g)
                except StopIteration:
                    gens.remove(g)
                yield

    def dir_driver(d):
        prev_scan = None
        for s in range(ntl):
            t = s if d == 0 else ntl - 1 - s
            q = s % 2
            work = [pair(t, d, 0, q), pair(t, d, 1, q), inst_pre(t, d, q)]
            if prev_scan is not None:
                work.append(prev_scan)
            yield from rr(work)
            yield from rr([inst_post(t, d, q)])
            if d == 0:
                st, en = (t % 2 == 0), (t % 2 == 1)
            else:
                st, en = (t % 2 == 1), (t % 2 == 0)
            prev_scan = locked_scan((d, s), scan_step(t, d, q, s == 0, st, en))
        yield from prev_scan

    g0, g1 = dir_driver(0), dir_driver(1)
    for _ in range(20):
        next(g0)
    run([g0, g1])


def build(debug=None, stages=("ffn1", "mixer", "ffn2")):
    nc = bass.Bass("TRN2", target_bir_lowering=False)
    dt = nc.dram_tensor
    x_d = dt("x", [T, D], F32, kind="ExternalInput").ap()
    pv_d = dt("pvec", [384, 128], F32, kind="ExternalInput").ap()
    wmod_d = dt("w_mod", [D, 9 * D], F32, kind="ExternalInput").ap()
    f1i_d = dt("ffn1_w_in", [D, 2 * DFF], F32, kind="ExternalInput").ap()
    f1o_d = dt("ffn1_w_out", [DFF, D], F32, kind="ExternalInput").ap()
    f2i_d = dt("ffn2_w_in", [D, 2 * DFF], F32, kind="ExternalInput").ap()
    f2o_d = dt("ffn2_w_out", [DFF, D], F32, kind="ExternalInput").ap()
    cf_d = dt("cstf", [128, NCF, 128], F32, kind="ExternalInput").ap()
    cb_d = dt("cstb", [128, NCB, 128], F32, kind="ExternalInput").ap()
    win_d = dt("w_in", [D, IN_COLS], F32, kind="ExternalInput").ap()
    wout_d = dt("w_out", [D, D], F32, kind="ExternalInput").ap()
    gc_d = dt("gconst", [128, 16], F32, kind="ExternalInput").ap()
    nl_d = dt("nlink", [128, 8], F32, kind="ExternalInput").ap()
    lk_d = dt("link32", [128, 32], F32, kind="ExternalInput").ap()
    ca_d = dt("carry", [128, 1], F32, kind="ExternalInput").ap()
    s0_d = dt("s0", [2, 128, 512], F32, kind="ExternalInput").ap()
    y_d = dt("y", [T, D], F32, kind="ExternalOutput").ap()
    st_d = dt("st", [8, 2, 128, 512], F32, kind="ExternalOutput").ap()
    dbg_d = None
    if debug:
        dbg_d = dt("dbg", [128, 8, T], F32, kind="ExternalOutput").ap()

    es = ExitStack()
    with es:
        P = Prog(nc, es)
        sb = lambda name, shape, dty: es.enter_context(nc.sbuf_tensor(name, shape, dty))
        ps = lambda name: es.enter_context(nc.psum_tensor(name, [128, 512], F32))

        xT = sb("xT", [128, DC, T], F32)
        XB = [[Buf("x%d_%d" % (c, g)) for g in range(NG)] for c in range(DC)]
        cstf = sb("cstf_s", [128, NCF, 128], F32)
        IDFB = Buf("cstf", frozen=True)
        CSTF = IDFB
        cstb = sb("cstb_s", [128, NCB, 128], BF16)
        CST = Buf("cst", frozen=True)
        smalls = sb("smalls", [128, 64], F32)
        SML = Buf("smalls", frozen=True)
        gconst, nlink, link32, carry = smalls[:, 0:16], smalls[:, 16:24], smalls[:, 24:56], smalls[:, 56:57]
        pT = sb("pT", [128, 384], F32)
        PT = Buf("pT", frozen=True)
        sc = sb("silu_c", [128, DC], BF16)
        SC = Buf("silu_c", frozen=True)
        WMR = [Buf("wmr%d" % k) for k in range(3)]
        bgbuf = {}
        modT = sb("modT", [128, 72], F32)
        MOD = Buf("mod", frozen=True)
        ab = sb("ab", [128, 3 * 3 * DC], F32)
        AB = Buf("ab", frozen=True)
        psum = [ps("ps%d" % i) for i in range(8)]
        PB = [Buf("ps%d" % i) for i in range(8)]
        rstd = sb("rstd", [128, 512], F32)
        RS = Buf("rstd")
        sqs = [sb("sq%d" % i, [128, 512], BF16) for i in range(3)]
        SQS = [Buf("sq%d" % i) for i in range(3)]
        tmpf = [sb("tmpf%d" % i, [128, 512], F32) for i in range(2)]
        TF = [Buf("tmpf%d" % i) for i in range(2)]
        sqi = [0]

        def nsq():
            sqi[0] = (sqi[0] + 1) % 3
            return sqs[sqi[0]], SQS[sqi[0]]

        IDF = cstf[:, CF_ID, :]
        IDB = cstb[:, CB_ID, :]
        MEANB = cstb[:, CB_MEAN, :]
        ONEB = cstb[:, CB_ONE, :]

        PC_COND, PC_BMOD, PC_NG = 0, 8, 80
        PC_DNC, PC_CVW, PC_CVB, PC_LNG, PC_LNB, PC_DNG = 128, 164, 288, 292, 296, 300

        P.dma("sp", lambda: nc.sync.dma_start(out=cstf[:], in_=cf_d[:, :, :]), writes=[IDFB])
        P.dma("pool", lambda: nc.gpsimd.dma_start(out=cstb[:], in_=cb_d[:, :, :]), writes=[CST])
        P.dma("sp", lambda: nc.sync.dma_start(out=smalls[:, 0:16], in_=gc_d[:, :]), writes=[SML])
        P.dma("sp", lambda: nc.sync.dma_start(out=smalls[:, 16:24], in_=nl_d[:, :]), writes=[SML])
        P.dma("sp", lambda: nc.sync.dma_start(out=smalls[:, 24:56], in_=lk_d[:, :]), writes=[SML])
        P.dma("sp", lambda: nc.sync.dma_start(out=smalls[:, 56:57], in_=ca_d[:, :]), writes=[SML])

        with ExitStack() as es0:
            sb0 = lambda name, shape, dty: es0.enter_context(nc.sbuf_tensor(name, shape, dty))
            pst = sb0("pstage", [128, 3, 128], F32)
            PST = Buf("pstage")
            P.dma("sp", lambda: nc.sync.dma_start(out=pst[:], in_=pv_d.rearrange("(k p) f -> p k f", p=128)),
                  writes=[PST])
            for k in range(3):
                P.op("pe", lambda k=k: nc.tensor.transpose(out=psum[0][:, k * 128:(k + 1) * 128], in_=pst[:, k, :],
                                                           identity=IDF), reads=[PST, IDFB], writes=[PB[0]])
            P.op("act", lambda: nc.scalar.copy(out=pT[:], in_=psum[0][:, 0:384]), reads=[PB[0]], writes=[PT])

            stage = [sb0("stage%d" % i, [128, D], F32) for i in range(2)]
            STG = [Buf("stage%d" % i) for i in range(2)]
            for t in range(NT):
                s = t % 2
                P.dma("sp", lambda t=t, s=s: nc.sync.dma_start(out=stage[s][:], in_=x_d[t * 128:(t + 1) * 128, :]),
                      writes=[STG[s]])
                for half in range(2):
                    pb = 1 + (2 * t + half) % 2
                    for cc in range(4):
                        c = half * 4 + cc
                        P.op("pe", lambda s=s, c=c, cc=cc, pb=pb: nc.tensor.transpose(
                            out=psum[pb][:, cc * 128:(cc + 1) * 128], in_=stage[s][:, c * 128:(c + 1) * 128],
                            identity=IDF), reads=[STG[s], IDFB], writes=[PB[pb]])
                    outap = xT[:, half * 4:half * 4 + 4, t * 128:(t + 1) * 128]
                    inap = psum[pb][:].rearrange("p (c t) -> p c t", c=4)
                    wr = [XB[half * 4 + cc][t // 4] for cc in range(4)]
                    if half == 0:
                        P.op("act", lambda outap=outap, inap=inap: nc.scalar.copy(out=outap, in_=inap),
                             reads=[PB[pb]], writes=wr)
                    else:
                        P.op("dve", lambda outap=outap, inap=inap: nc.vector.tensor_copy(out=outap, in_=inap),
                             reads=[PB[pb]], writes=wr)

            P.op("act", lambda: nc.scalar.activation(out=sc[:], in_=pT[:, PC_COND:PC_COND + 8], func=AF.Silu),
                 reads=[PT], writes=[SC])
            wm = [sb0("wm%d" % i, [128, DC, 512], BF16) for i in range(2)]
            WM = [Buf("wm%d" % i) for i in range(2)]
            for pc in range(18):
                s = pc % 2
                P.dma("pool", lambda pc=pc, s=s: nc.gpsimd.dma_start(
                    out=wm[s][:], in_=wmod_d[:, pc * 512:(pc + 1) * 512].rearrange("(k p) f -> p k f", p=128)),
                    writes=[WM[s]])
                for mm in range(4):
                    col = pc * 4 + mm
                    for kc in range(DC):
                        P.op("pe", lambda s=s, mm=mm, kc=kc, col=col: nc.tensor.matmul(
                            psum[3][:, col:col + 1], wm[s][:, kc, mm * 128:(mm + 1) * 128], sc[:, kc:kc + 1],
                            start=(kc == 0), stop=(kc == DC - 1)), reads=[WM[s], SC], writes=[PB[3]])
            P.op("dve", lambda: nc.vector.tensor_tensor(out=modT[:], in0=psum[3][:, 0:72],
                                                        in1=pT[:, PC_BMOD:PC_BMOD + 72], op=ALU.add),
                 reads=[PB[3], PT], writes=[MOD])

            def mod_ab(i, do_ab, do_g):
                rw = 0.5 if i != 1 else 1.0
                a_ap = ab[:, i * 24:i * 24 + 8]
                b_ap = ab[:, i * 24 + 8:i * 24 + 16]
                g_ap = ab[:, i * 24 + 16:i * 24 + 24]
                if do_ab:
                    P.op("dve", lambda: nc.vector.scalar_tensor_tensor(
                        out=a_ap, in0=modT[:, (3 * i + 1) * 8:(3 * i + 2) * 8], scalar=1.0,
                        in1=pT[:, PC_NG + 16 * i:PC_NG + 16 * i + 8], op0=ALU.add, op1=ALU.mult),
                        reads=[MOD, PT], writes=[AB])
                    P.op("dve", lambda: nc.vector.tensor_copy(out=b_ap, in_=modT[:, 3 * i * 8:3 * i * 8 + 8]),
                         reads=[MOD], writes=[AB])
                if do_g:
                    P.op("dve", lambda: nc.vector.scalar_tensor_tensor(
                        out=g_ap, in0=modT[:, (3 * i + 2) * 8:(3 * i + 3) * 8], scalar=rw,
                        in1=pT[:, PC_NG + 16 * i + 8:PC_NG + 16 * i + 16], op0=ALU.mult, op1=ALU.mult),
                        reads=[MOD, PT], writes=[AB])

            for i_ in range(3):
                mod_ab(i_, True, True)
            P.barrier()

        def mod_rest():
            wmr = bgbuf["wmr"]
            cols = list(range(16, 72))
            SKEW = 2

            def load(idx):
                col = cols[idx]
                k = idx % 3
                P.dma("pool", lambda: nc.gpsimd.dma_start(
                    out=wmr[k][:], in_=wmod_d[:, col * 128:(col + 1) * 128].rearrange("(k p) f -> p k f", p=128)),
                    writes=[WMR[k]])

            for idx in range(SKEW):
                load(idx)
            for idx, col in enumerate(cols):
                if idx + SKEW < len(cols):
                    load(idx + SKEW)
                k = idx % 3
                pbk = 3 if col < 24 else 2
                for kc in range(DC):
                    P.op("pe", lambda k=k, kc=kc, col=col, pbk=pbk: nc.tensor.matmul(
                        psum[pbk][:, col:col + 1], wmr[k][:, kc, :], sc[:, kc:kc + 1], start=(kc == 0),
                        stop=(kc == DC - 1)), reads=[WMR[k], SC], writes=[PB[pbk]])
                if col == 23:
                    P.op("dve", lambda: nc.vector.tensor_tensor(out=modT[:, 16:24], in0=psum[3][:, 16:24],
                                                                in1=pT[:, PC_BMOD + 16:PC_BMOD + 24], op=ALU.add),
                         reads=[PB[3], PT], writes=[MOD])
                    mod_ab(0, False, True)
                yield
            P.op("dve", lambda: nc.vector.tensor_tensor(out=modT[:, 24:72], in0=psum[2][:, 24:72],
                                                        in1=pT[:, PC_BMOD + 24:PC_BMOD + 72], op=ALU.add),
                 reads=[PB[2], PT], writes=[MOD])
            mod_ab(1, True, True)
            mod_ab(2, True, True)
            yield

        bg = []

        def bg_step(n):
            for _ in range(n):
                if bg:
                    try:
                        next(bg[0])
                    except StopIteration:
                        bg.pop(0)

        rtmp = sb("rtmp", [128, 512], F32)
        RT = Buf("rtmp")
        epsT = sb("epsT", [128, 1], F32)
        EPST = Buf("epsT", frozen=True)
        P.op("dve", lambda: nc.vector.memset(epsT[:], EPS), writes=[EPST])

        def rstd_from(pbank):
            P.op("act", lambda: nc.scalar.activation(out=rtmp[:], in_=psum[pbank][:], func=AF.Ln, bias=epsT[:, 0:1],
                                                     scale=1.0), reads=[PB[pbank], EPST], writes=[RT])
            P.op("act", lambda: nc.scalar.activation(out=rstd[:], in_=rtmp[:], func=AF.Exp, scale=-0.5),
                 reads=[RT], writes=[RS])

        pn_extra = []

        def prenorm(i, hT, HB, groups):
            for li, g in enumerate(groups):
                gs = slice(g * 512, (g + 1) * 512)
                ls = slice(li * 512, (li + 1) * 512)
                for c in range(DC):
                    sq, SQ = nsq()
                    if c % 2 == 0:
                        P.op("act", lambda c=c, gs=gs, sq=sq: nc.scalar.activation(out=sq[:], in_=xT[:, c, gs],
                                                                                  func=AF.Square),
                             reads=[XB[c][g]], writes=[SQ])
                    else:
                        P.op("dve", lambda c=c, gs=gs, sq=sq: nc.vector.tensor_tensor(
                            out=sq[:], in0=xT[:, c, gs], in1=xT[:, c, gs], op=ALU.mult),
                            reads=[XB[c][g]], writes=[SQ])
                    P.op("pe", lambda c=c, sq=sq: nc.tensor.matmul(psum[0][:], MEANB, sq[:], start=(c == 0),
                                                                   stop=(c == DC - 1)),
                         reads=[SQ, CST], writes=[PB[0]])
                rstd_from(0)
                tl = list(zip(tmpf, TF)) + pn_extra
                for c in range(DC):
                    tb, TB_ = tl[c % len(tl)]
                    P.op("dve", lambda c=c, gs=gs, tb=tb: nc.vector.scalar_tensor_tensor(
                        out=tb[:], in0=xT[:, c, gs], scalar=ab[:, i * 24 + c:i * 24 + c + 1], in1=rstd[:],
                        op0=ALU.mult, op1=ALU.mult), reads=[XB[c][g], RS, AB], writes=[TB_])
                    P.op("act", lambda c=c, ls=ls, tb=tb: nc.scalar.activation(
                        out=hT[:, c, ls], in_=tb[:], func=AF.Identity,
                        bias=ab[:, i * 24 + 8 + c:i * 24 + 9 + c], scale=1.0), reads=[TB_, AB], writes=[HB[li]])

        def postnorm_residual(i, ybuf, YB, groups, pstat):
            for li, g in enumerate(groups):
                gs = slice(g * 512, (g + 1) * 512)
                ls = slice(li * 512, (li + 1) * 512)
                rstd_from(pstat[li])
                for c in range(DC):
                    k = c % 2
                    P.op("dve", lambda c=c, ls=ls, k=k: nc.vector.scalar_tensor_tensor(
                        out=tmpf[k][:], in0=ybuf[:, c, ls],
                        scalar=ab[:, i * 24 + 16 + c:i * 24 + 17 + c], in1=rstd[:], op0=ALU.mult, op1=ALU.mult),
                        reads=[YB[c][li], RS, AB], writes=[TF[k]])
                    P.op("dve", lambda c=c, gs=gs, k=k: nc.vector.tensor_tensor(
                        out=xT[:, c, gs], in0=tmpf[k][:], in1=xT[:, c, gs], op=ALU.add),
                        reads=[TF[k], XB[c][g]], writes=[XB[c][g]])

        def ffn(i, wi_d, wo_d):
            with ExitStack() as es2:
                sb2 = lambda name, shape, dty: es2.enter_context(nc.sbuf_tensor("%s_f%d" % (name, i), shape, dty))
                hT = sb2("hT", [128, DC, 1024], BF16)
                HB = [Buf("h%d" % g) for g in range(2)]
                hid = sb2("hid", [128, FC, 1024], BF16)
                HID = [[Buf("hid%d_%d" % (m, g)) for g in range(2)] for m in range(FC)]
                ybuf = sb2("ybuf", [128, DC, 1024], F32)
                YB = [[Buf("y%d_%d" % (m, g)) for g in range(2)] for m in range(DC)]
                wsl = [sb2("wsl%d" % k, [128, 2, DC, 128], BF16) for k in range(3)]
                WS = [Buf("wsl%d" % k) for k in range(3)]
                w2sl = [sb2("w2sl%d" % k, [128, FC, 128], BF16) for k in range(2)]
                W2S = [Buf("w2sl%d" % k) for k in range(2)]
                pn_extra[:] = [(sb2("pnx%d" % k, [128, 512], F32), Buf("pnx%d" % k)) for k in range(2)]
                rsy = [sb2("rsy%d" % k, [128, 512], F32) for k in range(2)]
                RSY = [Buf("rsy%d_f%d" % (k, i)) for k in range(2)]
                deferred_post = []
                wcnt = 0
                w2cnt = 0
                for half in range(2):
                    groups = [2 * half, 2 * half + 1]
                    prenorm(i, hT, HB, groups)
                    order = [(m, 0) for m in range(3)] + [(m, 1) for m in range(3)] + \
                            [(m, li) for m in range(3, FC) for li in range(2)]
                    w_issued = set()
                    for it_, (m, li) in enumerate(order):
                        if deferred_post and it_ >= 3:
                            deferred_post.pop(0)()
                        s = (half * FC + m) % 3
                        if m not in w_issued:
                            w_issued.add(m)
                            P.dma("pool", lambda m=m, s=s: nc.gpsimd.dma_start(
                                out=wsl[s][:, 0],
                                in_=wi_d[:, m * 128:(m + 1) * 128].rearrange("(k p) f -> p k f", p=128)),
                                writes=[WS[s]])
                            P.dma("pool", lambda m=m, s=s: nc.gpsimd.dma_start(
                                out=wsl[s][:, 1],
                                in_=wi_d[:, DFF + m * 128:DFF + (m + 1) * 128].rearrange("(k p) f -> p k f", p=128)),
                                writes=[WS[s]])
                        ls = slice(li * 512, (li + 1) * 512)
                        pg = 4 + 2 * (it_ % 2)
                        pu = pg + 1
                        for kc in range(DC):
                            P.op("pe", lambda s=s, kc=kc, ls=ls, pg=pg: nc.tensor.matmul(
                                psum[pg][:], wsl[s][:, 0, kc, :], hT[:, kc, ls], start=(kc == 0),
                                stop=(kc == DC - 1)), reads=[WS[s], HB[li]], writes=[PB[pg]])
                        for kc in range(DC):
                            P.op("pe", lambda s=s, kc=kc, ls=ls, pu=pu: nc.tensor.matmul(
                                psum[pu][:], wsl[s][:, 1, kc, :], hT[:, kc, ls], start=(kc == 0),
                                stop=(kc == DC - 1)), reads=[WS[s], HB[li]], writes=[PB[pu]])
                        sq, SQ = nsq()
                        P.op("act", lambda pg=pg, sq=sq: nc.scalar.activation(out=sq[:], in_=psum[pg][:],
                                                                             func=AF.Silu),
                             reads=[PB[pg]], writes=[SQ])
                        P.op("dve", lambda m=m, ls=ls, pu=pu, sq=sq: nc.vector.tensor_tensor(
                            out=hid[:, m, ls], in0=psum[pu][:], in1=sq[:], op=ALU.mult),
                            reads=[PB[pu], SQ], writes=[HID[m][li]])
                    pend = []
                    for m in range(DC):
                        bg_step(2 if half == 0 else 3)
                        s = w2cnt % 2
                        w2cnt += 1
                        P.dma("pool", lambda m=m, s=s: nc.gpsimd.dma_start(
                            out=w2sl[s][:], in_=wo_d[:, m * 128:(m + 1) * 128].rearrange("(k p) f -> p k f", p=128)),
                            writes=[W2S[s]])
                        for li in range(2):
                            ls = slice(li * 512, (li + 1) * 512)
                            py = 4 + (m * 2 + li) % 2
                            for kc in range(FC):
                                P.op("pe", lambda s=s, kc=kc, ls=ls, py=py: nc.tensor.matmul(
                                    psum[py][:], w2sl[s][:, kc, :], hid[:, kc, ls], start=(kc == 0),
                                    stop=(kc == FC - 1)), reads=[W2S[s], HID[kc][li]], writes=[PB[py]])
                            P.op("act", lambda m=m, ls=ls, py=py: nc.scalar.copy(
                                out=ybuf[:, m, ls], in_=psum[py][:]), reads=[PB[py]], writes=[YB[m][li]])
                            sq, SQ = nsq()
                            P.op("act", lambda py=py, sq=sq: nc.scalar.activation(out=sq[:], in_=psum[py][:],
                                                                                 func=AF.Square),
                                 reads=[PB[py]], writes=[SQ])
                            def stat(m=m, li=li, sq=sq, SQ=SQ):
                                P.op("pe", lambda: nc.tensor.matmul(
                                    psum[6 + li][:], MEANB, sq[:], start=(m == 0), stop=(m == DC - 1)),
                                    reads=[SQ, CST], writes=[PB[6 + li]])
                            pend.append(stat)
                            if len(pend) > 1:
                                pend.pop(0)()
                    while pend:
                        pend.pop(0)()
                    if half == 1:
                        postnorm_residual(i, ybuf, YB, groups, [6, 7])
                    else:
                        for li in range(2):
                            P.op("act", lambda li=li: nc.scalar.activation(out=rtmp[:], in_=psum[6 + li][:], func=AF.Ln,
                                                                           bias=epsT[:, 0:1], scale=1.0),
                                 reads=[PB[6 + li], EPST], writes=[RT])
                            P.op("act", lambda li=li: nc.scalar.activation(out=rsy[li][:], in_=rtmp[:], func=AF.Exp,
                                                                           scale=-0.5), reads=[RT], writes=[RSY[li]])
                        for li in range(2):
                            g = groups[li]
                            for c in range(DC):
                                def chunk(li=li, g=g, c=c):
                                    gs = slice(g * 512, (g + 1) * 512)
                                    ls = slice(li * 512, (li + 1) * 512)
                                    k = c % 2
                                    P.op("dve", lambda: nc.vector.scalar_tensor_tensor(
                                        out=tmpf[k][:], in0=ybuf[:, c, ls],
                                        scalar=ab[:, i * 24 + 16 + c:i * 24 + 17 + c], in1=rsy[li][:], op0=ALU.mult,
                                        op1=ALU.mult), reads=[YB[c][li], RSY[li], AB], writes=[TF[k]])
                                    P.op("dve", lambda: nc.vector.tensor_tensor(
                                        out=xT[:, c, gs], in0=tmpf[k][:], in1=xT[:, c, gs], op=ALU.add),
                                        reads=[TF[k], XB[c][g]], writes=[XB[c][g]])
                                deferred_post.append(chunk)
                while deferred_post:
                    deferred_post.pop(0)()
                P.barrier()
                pn_extra[:] = []

        def mixer():
            i = 1
            with ExitStack() as esm:
                sbm = lambda name, shape, dty: esm.enter_context(nc.sbuf_tensor("mx_" + name, shape, dty))
                oT = sbm("oT", [128, NT, 4, 128], BF16)
                OT = [Buf("oT%d" % t) for t in range(NT)]
                gB = sbm("gB", [128, NT, 8], F32)
                bB = sbm("bB", [128, NT, 8], F32)
                EX = sbm("EX", [128, NT, 24], F32)
                lnB = sbm("lnB", [128, NT, 8], F32)
                GSRC, GATE = Buf("gsrc"), Buf("gate")
                with ExitStack() as esab:
                    sbab = lambda name, shape, dty: esab.enter_context(nc.sbuf_tensor("ab_" + name, shape, dty))
                    qkT = sbab("qkT", [128, 8, T], BF16)
                    vT = sbab("vT", [128, 4, T], BF16)
                    QKB, VB = Buf("qk"), Buf("v")
                    with ExitStack() as esa:
                        sba = lambda name, shape, dty: esa.enter_context(nc.sbuf_tensor("a_" + name, shape, dty))
                        hT_a2 = [sba("hT_a%d" % h_, [128, DC, 1024], BF16) for h_ in range(2)]
                        HB_a2 = [[Buf("mh%d_%d" % (h_, g)) for g in range(2)] for h_ in range(2)]
                        wsl_a = [sba("w%d" % k, [128, DC, 128], BF16) for k in range(3)]
                        WS_a = [Buf("mw%d" % k) for k in range(3)]
                        wg_a = sba("wg_a", [128, DC, 16], BF16)
                        WG_a = Buf("wg_a")
                        acc_a = [sba("acc_a%d" % k, [128, 512], F32) for k in range(2)]
                        ACC_a = [Buf("acc_a%d" % k) for k in range(2)]
                        sv_a = [sba("sv_a%d" % k, [128, 512], F32) for k in range(3)]
                        SV_a = [Buf("sv_a%d" % k) for k in range(3)]
                        rstd2_a = [rstd, sba("rstd2", [128, 512], F32)]
                        RS2_a = [RS, Buf("rstd2")]
                        qk_cnt = [0]
                        t7_a = [sba("t7_a%d" % k, [128, 8], F32) for k in range(2)]
                        T7_a = [Buf("t7_a%d" % k) for k in range(2)]
                        P.dma("pool", lambda: nc.gpsimd.dma_start(
                            out=wg_a[:], in_=win_d[:, 1536:1552].rearrange("(k p) f -> p k f", p=128)), writes=[WG_a])
                        chunks = [(qkT, cc, cc * 128, "q" if cc < 4 else "k") for cc in range(8)]
                        chunks += [(vT, cc, 1024 + cc * 128, "v") for cc in range(4)]
                        wcnt_a = 0
                        it_a = 0
                        q1_a, q2_a = [], []
                        pn_extra[:] = [(sba("pnx%d" % k, [128, 512], F32), Buf("pnxa%d" % k)) for k in range(2)]
                        for half in range(2):
                            groups_a = [2 * half, 2 * half + 1]
                            hT_a, HB_a = hT_a2[half], HB_a2[half]
                            if half == 0:
                                prenorm(i, hT_a, HB_a, groups_a)
                            for tl in range(8):
                                t = half * 8 + tl
                                li = tl // 4
                                tls = slice(tl * 128, (tl + 1) * 128)
                                for kc in range(DC):
                                    P.op("pe", lambda t=t, tls=tls, kc=kc, hT_a=hT_a: nc.tensor.matmul(
                                        psum[3][:, t * 16:(t + 1) * 16], hT_a[:, kc, tls], wg_a[:, kc, :],
                                        start=(kc == 0), stop=(kc == DC - 1)), reads=[HB_a[li], WG_a], writes=[PB[3]])
                            for ci, (dst, dc, col0, kind) in enumerate(chunks):
                                if half == 0 and ci == 8:
                                    while q1_a:
                                        f1, f2 = q1_a.pop(0)
                                        f1()
                                        if f2 is not None:
                                            q2_a.append(f2)
                                    while q2_a:
                                        q2_a.pop(0)()
                                    prenorm(i, hT_a2[1], HB_a2[1], [2, 3])
                                s_ = wcnt_a % 3
                                wcnt_a += 1
                                P.dma("pool", lambda s_=s_, col0=col0: nc.gpsimd.dma_start(
                                    out=wsl_a[s_][:], in_=win_d[:, col0:col0 + 128].rearrange("(k p) f -> p k f", p=128)),
                                    writes=[WS_a[s_]])
                                DB = {"q": QKB, "k": QKB, "v": VB}[kind]
                                for li in range(2):
                                    g = groups_a[li]
                                    gs = slice(g * 512, (g + 1) * 512)
                                    ls = slice(li * 512, (li + 1) * 512)
                                    pb = 4 + it_a % 4
                                    k2 = it_a % 2
                                    it_a += 1
                                    for kc in range(DC):
                                        P.op("pe", lambda s_=s_, kc=kc, ls=ls, pb=pb, hT_a=hT_a: nc.tensor.matmul(
                                            psum[pb][:], wsl_a[s_][:, kc, :], hT_a[:, kc, ls], start=(kc == 0),
                                            stop=(kc == DC - 1)), reads=[WS_a[s_], HB_a[li]], writes=[PB[pb]])
                                    cch = ci
                                    w0 = pT[:, PC_DNC + 0 * 12 + cch:PC_DNC + 0 * 12 + cch + 1]
                                    w1 = pT[:, PC_DNC + 1 * 12 + cch:PC_DNC + 1 * 12 + cch + 1]
                                    w2 = pT[:, PC_DNC + 2 * 12 + cch:PC_DNC + 2 * 12 + cch + 1]
                                    Pp = psum[pb]
                                    A_ = acc_a[k2]
                                    P.op("dve", lambda A_=A_, Pp=Pp, w1=w1: nc.vector.tensor_scalar(
                                        out=A_[:], in0=Pp[:], scalar1=w1, scalar2=None, op0=ALU.mult),
                                        reads=[PB[pb], PT], writes=[ACC_a[k2]])
                                    P.op("dve", lambda A_=A_, Pp=Pp, w0=w0: nc.vector.scalar_tensor_tensor(
                                        out=A_[:, 1:512], in0=Pp[:, 0:511], scalar=w0, in1=A_[:, 1:512], op0=ALU.mult,
                                        op1=ALU.add), reads=[PB[pb], PT, ACC_a[k2]], writes=[ACC_a[k2]])
                                    P.op("dve", lambda A_=A_, Pp=Pp, w2=w2: nc.vector.scalar_tensor_tensor(
                                        out=A_[:, 0:511], in0=Pp[:, 1:512], scalar=w2, in1=A_[:, 0:511], op0=ALU.mult,
                                        op1=ALU.add), reads=[PB[pb], PT, ACC_a[k2]], writes=[ACC_a[k2]])
                                    Pv = Pp[:].rearrange("p (r w) -> p r w", w=64)
                                    Av = A_[:].rearrange("p (r w) -> p r w", w=64)
                                    P.op("dve", lambda Pv=Pv, w0=w0, k2=k2: nc.vector.scalar_tensor_tensor(
                                        out=t7_a[k2][:, 0:7], in0=Pv[:, 0:7, 63], scalar=w0, in1=nlink[:, 1:8], op0=ALU.mult,
                                        op1=ALU.mult), reads=[PB[pb], PT, SML], writes=[T7_a[k2]])
                                    P.op("dve", lambda Av=Av, k2=k2: nc.vector.tensor_tensor(
                                        out=Av[:, 1:8, 0], in0=Av[:, 1:8, 0], in1=t7_a[k2][:, 0:7], op=ALU.add),
                                        reads=[T7_a[k2], ACC_a[k2]], writes=[ACC_a[k2]])
                                    P.op("dve", lambda Pv=Pv, w2=w2, k2=k2: nc.vector.scalar_tensor_tensor(
                                        out=t7_a[k2][:, 0:7], in0=Pv[:, 1:8, 0], scalar=w2, in1=nlink[:, 1:8], op0=ALU.mult,
                                        op1=ALU.mult), reads=[PB[pb], PT, SML], writes=[T7_a[k2]])
                                    P.op("dve", lambda Av=Av, k2=k2: nc.vector.tensor_tensor(
                                        out=Av[:, 0:7, 63], in0=Av[:, 0:7, 63], in1=t7_a[k2][:, 0:7], op=ALU.add),
                                        reads=[T7_a[k2], ACC_a[k2]], writes=[ACC_a[k2]])
                                    cs = (128.0 ** -0.5) if kind == "q" else 1.0
                                    j3 = qk_cnt[0] % 3
                                    j2 = qk_cnt[0] % 2
                                    if kind != "v":
                                        qk_cnt[0] += 1
                                    pn = j3

                                    def st1(kind=kind, A_=A_, dc=dc, gs=gs, k2=k2, DB=DB, pn=pn, j3=j3):
                                        if kind == "v":
                                            P.op("act", lambda: nc.scalar.activation(out=vT[:, dc, gs], in_=A_[:],
                                                                                     func=AF.Silu),
                                                 reads=[ACC_a[k2]], writes=[DB])
                                            return
                                        S_ = sv_a[j3]
                                        P.op("act", lambda: nc.scalar.activation(out=S_[:], in_=A_[:], func=AF.Silu),
                                             reads=[ACC_a[k2]], writes=[SV_a[j3]])
                                        sq, SQ = nsq()
                                        P.op("act", lambda: nc.scalar.activation(out=sq[:], in_=S_[:], func=AF.Square),
                                             reads=[SV_a[j3]], writes=[SQ])
                                        P.op("pe", lambda: nc.tensor.matmul(psum[pn][:], ONEB, sq[:], start=True, stop=True),
                                             reads=[SQ, CST], writes=[PB[pn]])

                                    def st2(pn=pn, dc=dc, gs=gs, cs=cs, j3=j3, j2=j2, DB=DB):
                                        rs_t, RS_B = rstd2_a[j2], RS2_a[j2]
                                        P.op("act", lambda: nc.scalar.activation(out=rtmp[:], in_=psum[pn][:], func=AF.Ln,
                                                                                 bias=epsT[:, 0:1], scale=1.0),
                                             reads=[PB[pn], EPST], writes=[RT])
                                        P.op("act", lambda: nc.scalar.activation(out=rs_t[:], in_=rtmp[:], func=AF.Exp,
                                                                                 scale=-0.5), reads=[RT], writes=[RS_B])
                                        P.op("dve", lambda: nc.vector.scalar_tensor_tensor(
                                            out=qkT[:, dc, gs], in0=sv_a[j3][:], scalar=cs, in1=rs_t[:], op0=ALU.mult,
                                            op1=ALU.mult), reads=[SV_a[j3], RS_B], writes=[DB])

                                    q1_a.append((st1, None if kind == "v" else st2))
                                    if len(q1_a) > 1:
                                        f1, f2 = q1_a.pop(0)
                                        f1()
                                        if f2 is not None:
                                            q2_a.append(f2)
                                    if len(q2_a) >= 3:
                                        q2_a.pop(0)()
                                        q2_a.pop(0)()
                            while q1_a:
                                f1, f2 = q1_a.pop(0)
                                f1()
                                if f2 is not None:
                                    q2_a.append(f2)
                            while q2_a:
                                q2_a.pop(0)()
                        gp = psum[3][:, 0:NT * 16].rearrange("p (t c) -> p t c", c=16)
                        gtmp = sba("gtmp", [128, NT, 8], F32)
                        GT = Buf("gtmp")
                        nal = sba("nal", [128, 8], F32)
                        NAL = Buf("nal")
                        P.op("dve", lambda: nc.vector.tensor_tensor(
                            out=gtmp[:], in0=gp[:, :, 0:8], in1=gconst[:, 8:16].unsqueeze(1).broadcast_to([128, NT, 8]),
                            op=ALU.add), reads=[PB[3], SML], writes=[GT])
                        P.op("act", lambda: nc.scalar.activation(out=gtmp[:], in_=gtmp[:], func=AF.Exp), reads=[GT],
                             writes=[GT])
                        P.op("dve", lambda: nc.vector.tensor_scalar(out=gtmp[:], in0=gtmp[:], scalar1=1.0, scalar2=None,
                                                                    op0=ALU.add), reads=[GT], writes=[GT])
                        P.op("act", lambda: nc.scalar.activation(out=gtmp[:], in_=gtmp[:], func=AF.Ln), reads=[GT],
                             writes=[GT])
                        P.op("act", lambda: nc.scalar.activation(out=nal[:], in_=gconst[:, 0:8], func=AF.Exp),
                             reads=[SML], writes=[NAL])
                        P.op("dve", lambda: nc.vector.scalar_tensor_tensor(
                            out=gB[:], in0=gtmp[:], scalar=-1.0, in1=nal[:].unsqueeze(1).broadcast_to([128, NT, 8]),
                            op0=ALU.mult, op1=ALU.mult), reads=[GT, NAL], writes=[GSRC])
                        P.op("act", lambda: nc.scalar.activation(out=bB[:], in_=gp[:, :, 8:16], func=AF.Sigmoid),
                             reads=[PB[3]], writes=[GATE])
                        P.op("act", lambda: nc.scalar.activation(out=lnB[:], in_=bB[:], func=AF.Ln), reads=[GATE],
                             writes=[GATE])
                        gate_cums(P, nc, NT, gB, EX, GATE, GSRC, cstf, CSTF, psum, PB, 0)
                        P.barrier()
                        pn_extra[:] = []
                    for b_ in (GSRC, GATE, QKB, VB):
                        b_.frozen = True
                    with ExitStack() as esb:
                        sbb = lambda name, shape, dty: esb.enter_context(nc.sbuf_tensor("b_" + name, shape, dty))
                        onb = sbb("onb", [128, 4, 128], BF16)
                        ONB = Buf("onb")
                        ms4 = sbb("ms4", [128, 8], F32)
                        MS4 = Buf("ms4")
                        tmps = [tmpf[0], tmpf[1], rtmp, rstd]
                        TMPS = [TF[0], TF[1], RT, RS]
                        v3 = lambda ap: ap.rearrange("p (h d) -> p h d", h=4)
                        ps3 = lambda b: psum[b][:].rearrange("p (h d) -> p h d", h=4)
                        bcl = lambda ap2: ap2.unsqueeze(2).broadcast_to([128, 4, 128])

                        def finish_tile(t, d, pq, po, eg_ap):
                            ts = slice(t * 128, (t + 1) * 128)
                            first = (d == 0) == (t < NT // 2)
                            tA, TA = tmps[2], TMPS[2]
                            tB, TB = tmps[3], TMPS[3]
                            part = oT[:, t, :, :]
                            P.op("dve", lambda: nc.vector.tensor_tensor(out=v3(tA[:]), in0=ps3(pq), in1=bcl(eg_ap),
                                                                        op=ALU.mult), reads=[PB[pq], GATE], writes=[TA])
                            if first:
                                P.op("dve", lambda: nc.vector.tensor_tensor(out=part, in0=ps3(po), in1=v3(tA[:]), op=ALU.add),
                                     reads=[PB[po], TA], writes=[OT[t]])
                                return
                            P.op("dve", lambda: nc.vector.tensor_tensor(out=v3(tA[:]), in0=ps3(po), in1=v3(tA[:]), op=ALU.add),
                                 reads=[PB[po], TA], writes=[TA])
                            P.op("pool", lambda: nc.gpsimd.tensor_tensor(out=v3(tA[:]), in0=v3(tA[:]), in1=part, op=ALU.add),
                                 reads=[TA, OT[t]], writes=[TA])
                            P.op("pool", lambda: nc.gpsimd.tensor_tensor(out=tB[:], in0=tA[:], in1=tA[:], op=ALU.mult),
                                 reads=[TA], writes=[TB])
                            P.op("dve", lambda: nc.vector.tensor_reduce(out=ms4[:, 0:4], in_=v3(tB[:]), axis=AX.X, op=ALU.add),
                                 reads=[TB], writes=[MS4])
                            P.op("act", lambda: nc.scalar.activation(out=ms4[:, 0:4], in_=ms4[:, 0:4], func=AF.Ln,
                                                                     bias=epsT[:, 0:1], scale=1.0 / 128.0),
                                 reads=[MS4, EPST], writes=[MS4])
                            P.op("act", lambda: nc.scalar.activation(out=ms4[:, 4:8], in_=ms4[:, 0:4], func=AF.Exp,
                                                                     scale=-0.5), reads=[MS4], writes=[MS4])
                            P.op("pool", lambda: nc.gpsimd.tensor_tensor(out=onb[:], in0=v3(tA[:]), in1=bcl(ms4[:, 4:8]),
                                                                         op=ALU.mult), reads=[TA, MS4], writes=[ONB])
                            for h in range(4):
                                P.op("pe", lambda h=h: nc.tensor.matmul(psum[pq][:, h * 128:(h + 1) * 128], onb[:, h, :], IDB,
                                                                        start=True, stop=True),
                                     reads=[ONB, CST], writes=[PB[pq]])
                            P.op("act", lambda: nc.scalar.activation(out=part, in_=ps3(pq), func=AF.Copy,
                                                                     scale=pT[:, PC_DNG:PC_DNG + 1]),
                                 reads=[PB[pq], PT], writes=[OT[t]])

                        deltanet(P, nc, esb, NT, qkT, QKB, vT, VB, gB, bB, lnB, EX, GATE, cstf, cstb, CSTF, CST, psum, PB,
                                 s0_d, carry, SML, st_d, tmps, TMPS, finish_tile, sq3=sqs)
                        P.barrier()
                cvT = sbm("cvT", [128, 4, T], BF16)
                CV = [[Buf("cv%d_%d" % (c, g)) for g in range(NG)] for c in range(4)]
                with ExitStack() as esc:
                    sbc_ = lambda name, shape, dty: esc.enter_context(nc.sbuf_tensor("c_" + name, shape, dty))
                    pad_c = sbc_("pad_c", [128, 4, 32, 94], BF16)
                    PAD_c = [Buf("pad_c%d" % c) for c in range(4)]
                    sg_c = [sbc_("sg_c%d" % k, [128, 512], F32) for k in range(2)]
                    SG_c = [Buf("sg_c%d" % k) for k in range(2)]
                    dg_c = [sbc_("dg_c%d" % k, [128, 31, 128], BF16) for k in range(2)]
                    DG_c = [Buf("dg_c%d" % k) for k in range(2)]
                    wtab_c = pT[:, PC_CVW:PC_CVW + 124].rearrange("p (t c) -> p t c", c=4)

                    def build_dg(c):
                        k2 = c % 2
                        P.op("pool", lambda: nc.gpsimd.tensor_tensor(
                            out=dg_c[k2][:], in0=IDB.unsqueeze(1).broadcast_to([128, 31, 128]),
                            in1=wtab_c[:, :, c].unsqueeze(2).broadcast_to([128, 31, 128]), op=ALU.mult),
                            reads=[CST, PT], writes=[DG_c[k2]])

                    build_dg(0)
                    build_dg(1)
                    esc1 = ExitStack()
                    hT_c2 = [esc1.enter_context(nc.sbuf_tensor("c_hT_c%d" % h_, [128, DC, 1024], BF16)) for h_ in range(2)]
                    HB_c2 = [[Buf("ch%d_%d" % (h_, g)) for g in range(2)] for h_ in range(2)]
                    wsl_c = [esc1.enter_context(nc.sbuf_tensor("c_w%d" % k, [128, 2, DC, 128], BF16)) for k in range(2)]
                    WS_c = [Buf("cw%d" % k) for k in range(2)]
                    pn_extra[:] = [(esc1.enter_context(nc.sbuf_tensor("c_pnx%d" % k, [128, 512], F32)), Buf("pnxc%d" % k))
                                   for k in range(2)]
                    for c in range(4):
                        P.op("pool", lambda c=c: nc.gpsimd.memset(pad_c[:, c], 0.0), writes=[PAD_c[c]])
                    wcnt_c = 0
                    it_c = 0
                    for half in range(2):
                        groups_c = [2 * half, 2 * half + 1]
                        hT_c, HB_c = hT_c2[half], HB_c2[half]
                        if half == 0:
                            prenorm(i, hT_c, HB_c, groups_c)
                        for c in range(4):
                            if half == 0 and c == 2:
                                prenorm(i, hT_c2[1], HB_c2[1], [2, 3])
                            s_ = wcnt_c % 2
                            wcnt_c += 1
                            for j, col0 in enumerate((2064 + c * 128, 2576 + c * 128)):
                                P.dma("pool", lambda s_=s_, j=j, col0=col0: nc.gpsimd.dma_start(
                                    out=wsl_c[s_][:, j], in_=win_d[:, col0:col0 + 128].rearrange("(k p) f -> p k f", p=128)),
                                    writes=[WS_c[s_]])
                            for li in range(2):
                                g = groups_c[li]
                                ls = slice(li * 512, (li + 1) * 512)
                                pv = 4 + 2 * (it_c % 2)
                                pg = pv + 1
                                k2 = it_c % 2
                                it_c += 1
                                for j, pb in ((0, pv), (1, pg)):
                                    for kc in range(DC):
                                        P.op("pe", lambda s_=s_, j=j, kc=kc, ls=ls, pb=pb, hT_c=hT_c: nc.tensor.matmul(
                                            psum[pb][:], wsl_c[s_][:, j, kc, :], hT_c[:, kc, ls], start=(kc == 0),
                                            stop=(kc == DC - 1)), reads=[WS_c[s_], HB_c[li]], writes=[PB[pb]])
                                P.op("act", lambda pg=pg, k2=k2: nc.scalar.activation(out=sg_c[k2][:], in_=psum[pg][:],
                                                                                     func=AF.Sigmoid),
                                     reads=[PB[pg]], writes=[SG_c[k2]])
                                P.op("dve", lambda c=c, g=g, pv=pv, k2=k2: nc.vector.tensor_tensor(
                                    out=pad_c[:, c, 8 * g:8 * g + 8, 15:79],
                                    in0=psum[pv][:].rearrange("p (r w) -> p r w", w=64),
                                    in1=sg_c[k2][:].rearrange("p (r w) -> p r w", w=64), op=ALU.mult),
                                    reads=[PB[pv], SG_c[k2]], writes=[PAD_c[c]])
                        for c in range(4):
                            s_ = wcnt_c % 2
                            wcnt_c += 1
                            col0 = 1552 + c * 128
                            P.dma("pool", lambda s_=s_, col0=col0: nc.gpsimd.dma_start(
                                out=wsl_c[s_][:, 0], in_=win_d[:, col0:col0 + 128].rearrange("(k p) f -> p k f", p=128)),
                                writes=[WS_c[s_]])
                            for li in range(2):
                                g = groups_c[li]
                                ls = slice(li * 512, (li + 1) * 512)
                                pv = 4 + 2 * (it_c % 2)
                                it_c += 1
                                for kc in range(DC):
                                    P.op("pe", lambda s_=s_, kc=kc, ls=ls, pv=pv, hT_c=hT_c: nc.tensor.matmul(
                                        psum[pv][:], wsl_c[s_][:, 0, kc, :], hT_c[:, kc, ls], start=(kc == 0),
                                        stop=(kc == DC - 1)), reads=[WS_c[s_], HB_c[li]], writes=[PB[pv]])
                                sq, SQ = nsq()
                                P.op("act", lambda pv=pv, sq=sq: nc.scalar.activation(out=sq[:], in_=psum[pv][:],
                                                                                     func=AF.Silu),
                                     reads=[PB[pv]], writes=[SQ])
                                otv = oT[:, 4 * g:4 * g + 4, c, :]
                                P.op("pool", lambda otv=otv, sq=sq: nc.gpsimd.tensor_tensor(
                                    out=otv, in0=otv, in1=sq[:].rearrange("p (t k) -> p t k", k=128), op=ALU.mult),
                                    reads=[SQ] + [OT[4 * g + q] for q in range(4)],
                                    writes=[OT[4 * g + q] for q in range(4)])
                    P.barrier()
                    pn_extra[:] = []
                    esc1.close()
                    cvf_c = sbc_("cvf_c", [128, 4, T], F32)
                    CVF_c = [[Buf("cvf_c%d_%d" % (c, g)) for g in range(NG)] for c in range(4)]
                    lkb_c = link32[:, 1:32].unsqueeze(2).broadcast_to([128, 31, 15])
                    for c in range(4):
                        P.op("dve", lambda c=c: nc.vector.tensor_tensor(
                            out=pad_c[:, c, 1:32, 0:15], in0=pad_c[:, c, 0:31, 64:79], in1=lkb_c, op=ALU.mult),
                            reads=[PAD_c[c], SML], writes=[PAD_c[c]])
                        P.op("dve", lambda c=c: nc.vector.tensor_tensor(
                            out=pad_c[:, c, 0:31, 79:94], in0=pad_c[:, c, 1:32, 15:30], in1=lkb_c, op=ALU.mult),
                            reads=[PAD_c[c], SML], writes=[PAD_c[c]])
                    M512_c = cstf[:, CF_M512, :]
                    it_c = 0
                    for c in range(4):
                        k2 = c % 2
                        if c >= 2:
                            build_dg(c)
                        for g in range(NG):
                            gs = slice(g * 512, (g + 1) * 512)
                            pb = 4 + it_c % 2
                            it_c += 1
                            for tau in range(31):
                                P.op("pe", lambda c=c, g=g, tau=tau, k2=k2, pb=pb: nc.tensor.matmul(
                                    psum[pb][:], dg_c[k2][:, tau, :], pad_c[:, c, 8 * g:8 * g + 8, tau:tau + 64],
                                    start=(tau == 0), stop=(tau == 30)), reads=[DG_c[k2], PAD_c[c]], writes=[PB[pb]])
                            P.op("act", lambda c=c, gs=gs, pb=pb: nc.scalar.activation(
                                out=cvf_c[:, c, gs], in_=psum[pb][:], func=AF.Identity,
                                bias=pT[:, PC_CVB + c:PC_CVB + c + 1], scale=1.0), reads=[PB[pb], PT], writes=[CVF_c[c][g]])
                    pend_c = []
                    for g in range(NG):
                        gs = slice(g * 512, (g + 1) * 512)
                        for c in range(4):
                            k2 = c % 2
                            P.op("act", lambda c=c, gs=gs, k2=k2: nc.scalar.activation(out=sg_c[k2][:], in_=cvf_c[:, c, gs],
                                                                                       func=AF.Square),
                                 reads=[CVF_c[c][g]], writes=[SG_c[k2]])

                            def stats(c=c, k2=k2, g=g, gs=gs):
                                P.op("pe", lambda: nc.tensor.matmul(psum[6][:], M512_c, cvf_c[:, c, gs], start=(c == 0),
                                                                    stop=(c == 3)), reads=[CSTF, CVF_c[c][g]], writes=[PB[6]])
                                P.op("pe", lambda: nc.tensor.matmul(psum[7][:], M512_c, sg_c[k2][:], start=(c == 0),
                                                                    stop=(c == 3)), reads=[CSTF, SG_c[k2]], writes=[PB[7]])
                            pend_c.append(stats)
                            if len(pend_c) > 1:
                                pend_c.pop(0)()
                        while pend_c:
                            pend_c.pop(0)()
                        P.op("act", lambda: nc.scalar.activation(out=tmpf[0][:], in_=psum[6][:], func=AF.Square),
                             reads=[PB[6]], writes=[TF[0]])
                        P.op("dve", lambda: nc.vector.tensor_tensor(out=rtmp[:], in0=psum[7][:], in1=tmpf[0][:],
                                                                    op=ALU.subtract), reads=[PB[7], TF[0]], writes=[RT])
                        P.op("act", lambda: nc.scalar.activation(out=rtmp[:], in_=rtmp[:], func=AF.Ln, bias=epsT[:, 0:1],
                                                                 scale=1.0), reads=[RT, EPST], writes=[RT])
                        P.op("act", lambda: nc.scalar.activation(out=rstd[:], in_=rtmp[:], func=AF.Exp, scale=-0.5),
                             reads=[RT], writes=[RS])
                        P.op("act", lambda: nc.scalar.copy(out=tmpf[1][:], in_=psum[6][:]), reads=[PB[6]], writes=[TF[1]])
                        for c in range(4):
                            k2 = c % 2
                            P.op("dve", lambda c=c, gs=gs, k2=k2: nc.vector.tensor_tensor(
                                out=sg_c[k2][:], in0=cvf_c[:, c, gs], in1=tmpf[1][:], op=ALU.subtract),
                                reads=[CVF_c[c][g], TF[1]], writes=[SG_c[k2]])
                            P.op("dve", lambda k2=k2: nc.vector.tensor_tensor(out=sg_c[k2][:], in0=sg_c[k2][:], in1=rstd[:],
                                                                              op=ALU.mult),
                                 reads=[SG_c[k2], RS], writes=[SG_c[k2]])
                            P.op("act", lambda c=c, gs=gs, k2=k2: nc.scalar.activation(
                                out=cvT[:, c, gs], in_=sg_c[k2][:], func=AF.Silu, bias=pT[:, PC_LNB + c:PC_LNB + c + 1],
                                scale=pT[:, PC_LNG + c:PC_LNG + c + 1]), reads=[SG_c[k2], PT], writes=[CV[c][g]])
                    P.barrier()
                with ExitStack() as esd:
                    sbd = lambda name, shape, dty: esd.enter_context(nc.sbuf_tensor("d_" + name, shape, dty))
                    ybuf_d2 = [sbd("ybuf_d%d" % h_, [128, DC, 1024], F32) for h_ in range(2)]
                    YB_d2 = [[[Buf("my%d_%d_%d" % (h_, m, g)) for g in range(2)] for m in range(DC)] for h_ in range(2)]
                    rsy_d = [sbd("rsy_d%d" % k, [128, 512], F32) for k in range(2)]
                    RSY_d = [Buf("rsy_d%d" % k) for k in range(2)]
                    defer_d = []
                    wsl_d = [sbd("w%d" % k, [128, DC, 128], BF16) for k in range(3)]
                    WS_d = [Buf("dw%d" % k) for k in range(3)]
                    wcnt_d = 0
                    pend_d = []
                    for half in range(2):
                        groups_d = [2 * half, 2 * half + 1]
                        ybuf_d, YB_d = ybuf_d2[half], YB_d2[half]
                        for m in range(DC):
                            s_ = wcnt_d % 3
                            wcnt_d += 1
                            P.dma("pool", lambda s_=s_, m=m: nc.gpsimd.dma_start(
                                out=wsl_d[s_][:], in_=wout_d[:, m * 128:(m + 1) * 128].rearrange("(k p) f -> p k f", p=128)),
                                writes=[WS_d[s_]])
                            for li in range(2):
                                g = groups_d[li]
                                gs = slice(g * 512, (g + 1) * 512)
                                ls = slice(li * 512, (li + 1) * 512)
                                py = 4 + (m * 2 + li) % 2
                                if defer_d:
                                    defer_d.pop(0)()
                                for kc in range(DC):
                                    if kc < 4:
                                        rhs = oT[:, 4 * g:4 * g + 4, kc, :]
                                        rd = [OT[4 * g + q] for q in range(4)]
                                    else:
                                        rhs = cvT[:, kc - 4, gs]
                                        rd = [CV[kc - 4][g]]
                                    P.op("pe", lambda s_=s_, kc=kc, rhs=rhs, py=py: nc.tensor.matmul(
                                        psum[py][:], wsl_d[s_][:, kc, :], rhs, start=(kc == 0), stop=(kc == DC - 1)),
                                        reads=[WS_d[s_]] + rd, writes=[PB[py]])
                                P.op("act", lambda m=m, ls=ls, py=py, ybuf_d=ybuf_d: nc.scalar.copy(
                                    out=ybuf_d[:, m, ls], in_=psum[py][:]), reads=[PB[py]], writes=[YB_d[m][li]])
                                sq, SQ = nsq()
                                P.op("act", lambda py=py, sq=sq: nc.scalar.activation(out=sq[:], in_=psum[py][:],
                                                                                     func=AF.Square),
                                     reads=[PB[py]], writes=[SQ])
                                def stat(m=m, li=li, sq=sq, SQ=SQ):
                                    P.op("pe", lambda: nc.tensor.matmul(
                                        psum[6 + li][:], MEANB, sq[:], start=(m == 0), stop=(m == DC - 1)),
                                        reads=[SQ, CST], writes=[PB[6 + li]])
                                pend_d.append(stat)
                                if len(pend_d) > 1:
                                    pend_d.pop(0)()
                        while pend_d:
                            pend_d.pop(0)()
                        if half == 1:
                            while defer_d:
                                defer_d.pop(0)()
                            postnorm_residual(i, ybuf_d, YB_d, groups_d, [6, 7])
                        else:
                            for li in range(2):
                                P.op("act", lambda li=li: nc.scalar.activation(out=rtmp[:], in_=psum[6 + li][:], func=AF.Ln,
                                                                               bias=epsT[:, 0:1], scale=1.0),
                                     reads=[PB[6 + li], EPST], writes=[RT])
                                P.op("act", lambda li=li: nc.scalar.activation(out=rsy_d[li][:], in_=rtmp[:], func=AF.Exp,
                                                                               scale=-0.5), reads=[RT], writes=[RSY_d[li]])
                            for li in range(2):
                                g = groups_d[li]
                                for c in range(DC):
                                    def chunk_d(li=li, g=g, c=c, ybuf_d=ybuf_d, YB_d=YB_d):
                                        gs = slice(g * 512, (g + 1) * 512)
                                        ls = slice(li * 512, (li + 1) * 512)
                                        k = c % 2
                                        P.op("dve", lambda: nc.vector.scalar_tensor_tensor(
                                            out=tmpf[k][:], in0=ybuf_d[:, c, ls],
                                            scalar=ab[:, i * 24 + 16 + c:i * 24 + 17 + c], in1=rsy_d[li][:], op0=ALU.mult,
                                            op1=ALU.mult), reads=[YB_d[c][li], RSY_d[li], AB], writes=[TF[k]])
                                        P.op("dve", lambda: nc.vector.tensor_tensor(
                                            out=xT[:, c, gs], in0=tmpf[k][:], in1=xT[:, c, gs], op=ALU.add),
                                            reads=[TF[k], XB[c][g]], writes=[XB[c][g]])
                                    defer_d.append(chunk_d)
                    P.barrier()

        if "ffn1" in stages:
            ffn(0, f1i_d, f1o_d)
        if "mixer" in stages:
            mixer()
        if "ffn2" in stages:
            ffn(2, f2i_d, f2o_d)

        if debug:
            P.dma("sp", lambda: nc.sync.dma_start(out=dbg_d[:, :, :], in_=xT[:]),
                  reads=[XB[c][g] for c in range(DC) for g in range(NG)])
        ost = [sb("ost%d" % i, [128, D], F32) for i in range(2)]
        OST = [Buf("ost%d" % i) for i in range(2)]
        for t in range(NT):
            s = t % 2
            for half in range(2):
                pb = 1 + (2 * t + half) % 2
                for cc in range(4):
                    c = half * 4 + cc
                    P.op("pe", lambda t=t, c=c, cc=cc, pb=pb: nc.tensor.transpose(
                        out=psum[pb][:, cc * 128:(cc + 1) * 128], in_=xT[:, c, t * 128:(t + 1) * 128],
                        identity=IDF), reads=[XB[c][t // 4], IDFB], writes=[PB[pb]])
                if half == 0:
                    P.op("act", lambda s=s, pb=pb: nc.scalar.copy(out=ost[s][:, 0:512], in_=psum[pb][:]),
                         reads=[PB[pb]], writes=[OST[s]])
                else:
                    P.op("act", lambda s=s, pb=pb: nc.scalar.copy(out=ost[s][:, 512:1024], in_=psum[pb][:]),
                         reads=[PB[pb]], writes=[OST[s]])
            P.dma("sp", lambda t=t, s=s: nc.sync.dma_start(out=y_d[t * 128:(t + 1) * 128, :], in_=ost[s][:]),
                  reads=[OST[s]])
        P.emit()
        build.stats = P.stats
    return nc


def make_pvec(cond, b_mod, norm_g, dn_conv_w, cv_dw_w, cv_dw_b, cv_ln_g, cv_ln_b, dn_norm_g):
    pv = np.zeros((384, 128), np.float32)
    pv[0:8] = cond.reshape(8, 128)
    pv[8:80] = b_mod.reshape(72, 128)
    pv[80:128] = norm_g.reshape(48, 128)
    pv[128:164] = dn_conv_w.reshape(36, 128)
    pv[164:288] = cv_dw_w.reshape(124, 128)
    pv[288:292] = cv_dw_b.reshape(4, 128)
    pv[292:296] = cv_ln_g.reshape(4, 128)
    pv[296:300] = cv_ln_b.reshape(4, 128)
    pv[300] = dn_norm_g.reshape(128)
    return pv


_NC_CACHE = {}


def kernel(x_prompt, x_sample, state_delta, c, c_ctx, w_mod, b_mod, norm_g, ffn1_w_in,
           ffn1_w_out, w_in, dn_conv_w, dn_a_log, dn_dt_bias, dn_norm_g, cv_dw_w, cv_dw_b,
           cv_ln_g, cv_ln_b, w_out, ffn2_w_in, ffn2_w_out, _debug=None,
           _stages=("ffn1", "mixer", "ffn2")):
    f = lambda a: np.ascontiguousarray(np.asarray(a, dtype=np.float32))
    x_prompt, x_sample = f(x_prompt), f(x_sample)
    key = (_debug, _stages)
    if key not in _NC_CACHE:
        _NC_CACHE[key] = build(debug=_debug, stages=_stages)
    nc = _NC_CACHE[key]
    cf, cb = make_consts()
    rep = lambda v: np.ascontiguousarray(np.broadcast_to(np.asarray(v, np.float32).reshape(1, -1), (128, np.size(v))))
    gconst = rep(np.concatenate([f(dn_a_log)[0].reshape(8), f(dn_dt_bias)[0].reshape(8)]))
    r8 = np.arange(8)
    r32 = np.arange(32)
    link8_p = (r8 % 4 != 0).astype(np.float32)
    link32_p = (r32 % 4 != 0).astype(np.float32)
    sd = f(state_delta)
    in_maps = []
    for core in range(8):
        if core < 4 or core >= 6:
            cp = core if core < 4 else 0
            xc = x_prompt[8 * cp:8 * cp + 8].reshape(T, D)
            cond = f(c_ctx)
            link8, link32v, carry = link8_p, link32_p, 0.0
            s0 = np.zeros((2, 128, 512), np.float32)
        else:
            b = core - 4
            xc = x_sample[b]
            cond = f(c)[b]
            link8, link32v, carry = np.zeros(8, np.float32), np.zeros(32, np.float32), 1.0
            s0 = np.ascontiguousarray(sd[b, 0].transpose(0, 2, 1, 3)).reshape(2, 128, 512)
        in_maps.append({
            "x": np.ascontiguousarray(xc),
            "pvec": make_pvec(cond, f(b_mod)[0], f(norm_g)[0], f(dn_conv_w)[0], f(cv_dw_w)[0], f(cv_dw_b)[0],
                              f(cv_ln_g)[0], f(cv_ln_b)[0], f(dn_norm_g)[0]),
            "w_mod": f(w_mod)[0], "ffn1_w_in": f(ffn1_w_in)[0], "ffn1_w_out": f(ffn1_w_out)[0],
            "ffn2_w_in": f(ffn2_w_in)[0], "ffn2_w_out": f(ffn2_w_out)[0],
            "w_in": f(w_in)[0], "w_out": f(w_out)[0],
            "cstf": cf, "cstb": cb, "gconst": gconst, "nlink": rep(link8 - 1.0), "link32": rep(link32v),
            "carry": np.full((128, 1), carry, np.float32), "s0": s0,
        })
    res = run_bass_kernel_spmd(nc, in_maps, core_ids=list(range(8)))
    r = res.results
    y_p = np.concatenate([r[i]["y"].reshape(8, 256, D) for i in range(4)], axis=0)
    y_s = np.stack([r[4]["y"], r[5]["y"]], axis=0)
    ns = np.concatenate([r[i]["st"].reshape(8, 2, 128, 4, 128).transpose(0, 1, 3, 2, 4) for i in range(4)], axis=0)
    ns = np.ascontiguousarray(ns.reshape(32, 1, 2, 4, 128, 128))
    if _debug:
        return (y_p, y_s, ns), [r[i]["dbg"] for i in range(8)]
    return (y_p, y_s, ns)
```

```python
import numpy as np
import concourse.bass as bass
import concourse.mybir as mybir
from concourse.bass_utils import run_bass_kernel_spmd
from contextlib import ExitStack

F32 = mybir.dt.float32
BF16 = mybir.dt.bfloat16
AF = mybir.ActivationFunctionType
ALU = mybir.AluOpType
AX = mybir.AxisListType

D = 1024
DC = 8
T = 2048
NT = T // 128
NG = T // 512
DFF = 2816
FC = DFF // 128
EPS = 1e-6
IN_COLS = 3088
NMASK = 16


class Buf:
    __slots__ = ("name", "w", "rs", "rdma", "frozen")

    def __init__(self, name, frozen=False):
        self.name = name
        self.w = None
        self.rs = {}
        self.rdma = []
        self.frozen = frozen


class Op:
    __slots__ = ("eng", "fn", "deps", "is_dma", "sig", "idx", "sem", "semval", "n")


class Prog:
    def __init__(self, nc, es, n_dma_sems=20):
        self.nc = nc
        self.ops = []
        self.engs = {"pe": nc.tensor, "act": nc.scalar, "dve": nc.vector,
                     "pool": nc.gpsimd, "sp": nc.sync}
        self.esem = {e: es.enter_context(nc.semaphore("es_" + e)) for e in self.engs}
        self.dsem = [es.enter_context(nc.semaphore("ds%d" % i)) for i in range(n_dma_sems)]
        self.dcnt = [0] * n_dma_sems
        self.dlast = [None] * n_dma_sems
        self.dnext = 0
        self.last = {e: None for e in self.engs}

    def _mk(self, eng, fn, reads, writes, is_dma):
        o = Op()
        o.eng, o.fn, o.is_dma, o.sig, o.idx = eng, fn, is_dma, False, 0
        o.sem = None
        o.semval = 0
        o.n = len(self.ops)
        deps = {}

        def add(d, raw):
            if d is None or d is o:
                return
            if (not d.is_dma) and (not is_dma) and d.eng == eng and not raw:
                return
            deps[d.n] = d

        for r in reads:
            add(r.w, True)
        for w in writes:
            add(w.w, False)
            for d in w.rs.values():
                add(d, False)
            for d in w.rdma:
                add(d, False)
        if is_dma:
            k = self.dnext
            self.dnext = (self.dnext + 1) % len(self.dsem)
            if self.dlast[k] is not None:
                deps[self.dlast[k].n] = self.dlast[k]
            self.dcnt[k] += 1
            o.sem = self.dsem[k]
            o.semval = 16 * self.dcnt[k]
            self.dlast[k] = o
        o.deps = list(deps.values())
        for d in o.deps:
            d.sig = True
        for w in writes:
            w.w = o
            w.rs = {}
            w.rdma = []
        for r in reads:
            if r.frozen:
                continue
            if is_dma:
                r.rdma.append(o)
            else:
                r.rs[eng] = o
        self.ops.append(o)
        self.last[eng] = o
        return o

    def op(self, eng, fn, reads=(), writes=()):
        return self._mk(eng, fn, reads, writes, False)

    def dma(self, q, fn, reads=(), writes=()):
        return self._mk(q, fn, reads, writes, True)

    def barrier(self):
        lasts = [o for o in self.last.values() if o is not None]
        dl = [o for o in self.dlast if o is not None]
        for e in self.engs:
            o = Op()
            o.eng, o.fn, o.is_dma, o.sig, o.idx = e, None, False, False, 0
            o.sem, o.semval, o.n = None, 0, len(self.ops)
            o.deps = [d for d in lasts + dl]
            for d in o.deps:
                d.sig = True
            self.ops.append(o)

    def emit(self):
        cnt = {e: 0 for e in self.engs}
        for o in self.ops:
            if (not o.is_dma) and o.sig and o.fn is not None:
                cnt[o.eng] += 1
                o.idx = cnt[o.eng]
        waited = {e: {} for e in self.engs}
        nwait = 0
        for o in self.ops:
            E = self.engs[o.eng]
            wt = waited[o.eng]
            for d in o.deps:
                if d.is_dma:
                    sem, val, key = d.sem, d.semval, id(d.sem)
                else:
                    if d.fn is None:
                        continue
                    sem, val, key = self.esem[d.eng], d.idx, d.eng
                if wt.get(key, 0) < val:
                    E.wait_ge(sem, val)
                    wt[key] = val
                    nwait += 1
            if o.fn is None:
                continue
            ins = o.fn()
            if o.is_dma:
                ins.then_inc(o.sem, 16)
            elif o.sig:
                ins.then_inc(self.esem[o.eng], 1)
        sp = self.engs["sp"]
        for k, d in enumerate(self.dlast):
            if d is not None:
                sp.wait_ge(d.sem, d.semval)
        self.stats = (len(self.ops), nwait, dict(cnt))


CF_ID, CF_A1, CF_A2, CF_A3, CF_A4, CF_ONE, CF_M512 = range(7)
NCF = 7
CB_ID, CB_MEAN, CB_ONE, CB_NBD16, CB_NOFF16, CB_NOFF32, CB_NOFF64 = range(7)
NCB = 7
NEGBIG = -30000.0
DN_WARM = 0


def make_consts():
    k = np.arange(128)[:, None]
    x = np.arange(128)[None, :]
    cf = np.zeros((128, NCF, 128), np.float32)
    cf[:, CF_ID] = (k == x)
    cf[:, CF_A1] = (k <= x)
    cf[:, CF_A2] = (k > x)
    cf[:, CF_A3] = (k >= x)
    cf[:, CF_A4] = (k < x)
    cf[:, CF_ONE] = 1.0
    cf[:, CF_M512] = 1.0 / 512.0
    cb = np.zeros((128, NCB, 128), np.float32)
    cb[:, CB_ID] = (k == x)
    cb[:, CB_MEAN] = 1.0 / 1024.0
    cb[:, CB_ONE] = 1.0
    cb[:, CB_NBD16] = -1.0 * (k // 16 == x // 16)
    for idx, b in ((CB_NOFF16, 16), (CB_NOFF32, 32), (CB_NOFF64, 64)):
        cb[:, idx] = -1.0 * ((k // (2 * b) == x // (2 * b)) & (k // b != x // b))
    return cf, cb


def gate_cums(P, nc, ntl, gB, EX, GATE, GSRC, cstf, CSTF, psum, PB, pbank):
    plan = [(CF_A1, 0), (CF_A2, 0), (CF_ONE, 0), (CF_A3, 4), (CF_A4, 4), (CF_ONE, 4)]
    for t in range(ntl):
        for i, (m, c0) in enumerate(plan):
            col = t * 24 + i * 4
            P.op("pe", lambda t=t, m=m, c0=c0, col=col: nc.tensor.matmul(
                psum[pbank][:, col:col + 4], cstf[:, m, :], gB[:, t, c0:c0 + 4], start=True, stop=True),
                reads=[CSTF, GSRC], writes=[PB[pbank]])
    P.op("act", lambda: nc.scalar.activation(out=EX[:].rearrange("p t c -> p (t c)"), in_=psum[pbank][:, 0:ntl * 24],
                                             func=AF.Exp), reads=[PB[pbank]], writes=[GATE])


def deltanet(P, nc, es, ntl, qkT, QKB, vT, VB, gB, bB, lnB, EX, GATE, cstf, cstb, CSTF, CST,
             psum, PB, S0_d, carry, CARRY, st_d, tmps, TMPS, finish_tile, sq3=None):
    sb = lambda name, shape, dty: es.enter_context(nc.sbuf_tensor("dn_" + name, shape, dty))
    HP = 2
    IDB = cstb[:, CB_ID, :]
    IDF = cstf[:, CF_ID, :]

    class Set:
        pass

    sets = []
    for d in range(2):
        S = Set()
        S.pairs = []
        for hp_ in range(2):
            Q = Set()
            pt = lambda name: sb("%s%d_%d" % (name, d, hp_), [128, 2, HP, 128], BF16)
            Q.NCt, Q.NCn, Q.P2, Q.P4, Q.RA, Q.RB = pt("NCt"), pt("NCn"), pt("P2"), pt("P4"), pt("RA"), pt("RB")
            Q.rhsE = Q.P4[:].rearrange("p v h d -> p (v h d)").bitcast(F32).rearrange("p (h d) -> p h d", h=HP)
            Q.EMi = Q.RA[:, 0]
            Q.Ers = Q.RA[:, 1]
            Q.ErC = sb("ErC%d_%d" % (d, hp_), [128, HP, 128], BF16)
            Q.B = {n: Buf("dn%d_%d_%s" % (d, hp_, n)) for n in ["NCt", "NCn", "P2", "P4", "RA", "RB", "ErC"]}
            Q.B["rhsE"] = Q.B["P4"]
            Q.B["EMi"] = Q.B["RA"]
            Q.B["Ers"] = Q.B["RA"]
            Q.bank = 2 * d + hp_
            S.pairs.append(Q)
        if d == 0 and sq3 is not None:
            S.ktok, S.vtok, S.kg = [q[:].rearrange("p (h d) -> p h d", h=4) for q in sq3]
        else:
            S.ktok = sb("ktok%d" % d, [128, 4, 128], BF16)[:]
            S.vtok = sb("vtok%d" % d, [128, 4, 128], BF16)[:]
            S.kg = sb("kg%d" % d, [128, 4, 128], BF16)[:]
        S.Yt = sb("Yt%d" % d, [128, 4, 128], BF16)
        S.kdec = [sb("kdec%d_%d" % (d, q), [128, 4, 128], BF16) for q in range(2)]
        S.QKt = [sb("QKt%d_%d" % (d, q), [128, 4, 128], BF16) for q in range(2)]
        S.Wt = [sb("Wt%d_%d" % (d, q), [128, 4, 128], BF16) for q in range(2)]
        S.bu = [sb("bu%d_%d" % (d, q), [128, 4, 128], BF16) for q in range(2)]
        S.vnew = sb("vnew%d" % d, [128, 4, 128], BF16)
        S.Sm = sb("Sm%d" % d, [128, 4, 128], F32)
        S.Sb = sb("Sb%d" % d, [128, 4, 128], BF16)
        S.B = {n: Buf("dn%d_%s" % (d, n)) for n in ["ktok", "vtok", "kg", "Yt", "vnew", "Sm", "Sb"]}
        for n in ("kdec", "QKt", "Wt", "bu"):
            for q in range(2):
                S.B[n + str(q)] = Buf("dn%d_%s%d" % (d, n, q))
        sets.append(S)

    def bc_h(ap2d, nh):
        return ap2d.unsqueeze(1).broadcast_to([128, nh, 128])

    def bc_c(ap2d):
        nh = ap2d.shape[1]
        return ap2d.unsqueeze(2).broadcast_to([128, nh, 128])

    def ps3(b, nh=4):
        return psum[b][:, 0:nh * 128].rearrange("p (h d) -> p h d", h=nh)

    def ps4(b):
        return psum[b][:].rearrange("p (v h d) -> p v h d", v=2, h=HP)

    MASKS = {0: (CF_A2, CF_A1, CF_A1, CF_A4),
             1: (CF_A4, CF_A3, CF_A3, CF_A2)}

    def mm(b, col, lhsT, rhs, start, stop, reads):
        P.op("pe", lambda: nc.tensor.matmul(psum[b][:, col * 128:(col + 1) * 128], lhsT, rhs, start=start, stop=stop),
             reads=reads, writes=[PB[b]])

    def inst_pre(t, d, q):
        S = sets[d]
        B = S.B
        ts = slice(t * 128, (t + 1) * 128)
        ex0 = d * 12
        b = S.pairs[0].bank
        for h in range(4):
            mm(b, h, qkT[:, 4 + h, ts], IDB, True, True, [QKB, CST])
        P.op("act", lambda: nc.scalar.copy(out=S.ktok, in_=ps3(b)), reads=[PB[b]], writes=[B["ktok"]])
        P.op("pool", lambda: nc.gpsimd.tensor_tensor(out=S.kg, in0=S.ktok, in1=bc_c(EX[:, t, ex0:ex0 + 4]),
                                                     op=ALU.mult), reads=[B["ktok"], GATE], writes=[B["kg"]])
        P.op("pool", lambda: nc.gpsimd.tensor_tensor(out=S.kdec[q][:], in0=S.ktok, in1=bc_c(EX[:, t, ex0 + 4:ex0 + 8]),
                                                     op=ALU.mult), reads=[B["ktok"], GATE], writes=[B["kdec%d" % q]])
        yield
        b2 = S.pairs[1].bank
        for h in range(4):
            mm(b2, h, vT[:, h, ts], IDB, True, True, [VB, CST])
        P.op("act", lambda: nc.scalar.copy(out=S.vtok, in_=ps3(b2)), reads=[PB[b2]], writes=[B["vtok"]])
        yield

    def inst_post(t, d, q):
        S = sets[d]
        B = S.B
        gc4 = slice(d * 4, d * 4 + 4)
        b = S.pairs[0].bank
        for h in range(4):
            mm(b, h, S.Yt[:, h, :], S.vtok[:, h, :], True, True, [B["Yt"], B["vtok"]])
        P.op("dve", lambda: nc.vector.tensor_tensor(out=S.bu[q][:], in0=ps3(b), in1=bc_c(bB[:, t, gc4]), op=ALU.mult),
             reads=[PB[b], GATE], writes=[B["bu%d" % q]])
        yield
        b2 = S.pairs[1].bank
        for h in range(4):
            mm(b2, h, S.kg[:, h, :], S.Yt[:, h, :], True, True, [B["kg"], B["Yt"]])
        P.op("act", lambda: nc.scalar.copy(out=S.Wt[q][:], in_=ps3(b2)), reads=[PB[b2]], writes=[B["Wt%d" % q]])
        yield

    def pair(t, d, hp, q):
        SI = sets[d]
        S = SI.pairs[hp]
        B = dict(S.B)
        B["QKt"] = SI.B["QKt%d" % q]
        B["Yt"] = SI.B["Yt"]
        ts = slice(t * 128, (t + 1) * 128)
        m_el, m_er, m_incl, m_strict = MASKS[d]
        hs = slice(hp * HP, (hp + 1) * HP)
        gcol = d * 4 + hp * HP

        def nb():
            return S.bank

        P.op("pool", lambda: nc.gpsimd.tensor_tensor(
            out=S.rhsE, in0=bc_h(cstf[:, m_er, :], HP), in1=bc_c(gB[:, t, gcol:gcol + HP]), op=ALU.mult),
            reads=[CSTF, GATE], writes=[B["rhsE"]])
        b = nb()
        rE = S.rhsE.rearrange("p h d -> p (h d)")
        P.op("pe", lambda: nc.tensor.matmul(psum[b][:, 0:256], cstf[:, m_el, :], rE, start=True, stop=True),
             reads=[CSTF, B["rhsE"]], writes=[PB[b]])
        P.op("act", lambda: nc.scalar.activation(out=S.EMi, in_=ps4(b)[:, 0], func=AF.Exp),
             reads=[PB[b]], writes=[B["EMi"]])
        for h in range(HP):
            P.op("act", lambda h=h: nc.scalar.activation(out=S.Ers[:, h, :], in_=ps4(b)[:, 0, h, :], func=AF.Exp,
                                                         bias=lnB[:, t, gcol + h:gcol + h + 1], scale=1.0),
                 reads=[PB[b], GATE], writes=[B["Ers"]])
        P.op("dve", lambda: nc.vector.tensor_tensor(out=S.EMi, in0=S.EMi, in1=bc_h(cstf[:, m_incl, :], HP),
                                                    op=ALU.mult), reads=[B["EMi"], CSTF], writes=[B["EMi"]])
        P.op("dve", lambda: nc.vector.tensor_tensor(out=S.Ers, in0=S.Ers, in1=bc_h(cstf[:, m_strict, :], HP),
                                                    op=ALU.mult), reads=[B["Ers"], CSTF], writes=[B["Ers"]])
        P.op("pool", lambda: nc.gpsimd.tensor_tensor(out=S.ErC[:], in0=S.Ers, in1=bc_h(cstb[:, CB_NBD16, :], HP),
                                                     op=ALU.mult), reads=[B["Ers"], CST], writes=[B["ErC"]])
        yield
        b1 = nb()
        for h in range(HP):
            mm(b1, h, qkT[:, 4 + hp * HP + h, ts], qkT[:, 4 + hp * HP + h, ts], True, True, [QKB])
            mm(b1, HP + h, qkT[:, 4 + hp * HP + h, ts], qkT[:, hp * HP + h, ts], True, True, [QKB])
        P.op("dve", lambda: nc.vector.tensor_tensor(out=S.NCt[:, 0], in0=ps4(b1)[:, 0], in1=S.Ers, op=ALU.mult),
             reads=[PB[b1], B["Ers"]], writes=[B["NCt"]])
        P.op("dve", lambda: nc.vector.tensor_tensor(out=S.NCt[:, 1], in0=ps4(b1)[:, 0], in1=S.ErC[:], op=ALU.mult),
             reads=[PB[b1], B["ErC"]], writes=[B["NCt"]])
        P.op("dve", lambda: nc.vector.tensor_tensor(out=SI.QKt[q][:, hs, :], in0=ps4(b1)[:, 1], in1=S.EMi, op=ALU.mult),
             reads=[PB[b1], B["EMi"]], writes=[B["QKt"]])
        P.op("pool", lambda: nc.gpsimd.tensor_tensor(out=S.RB[:, 0], in0=S.NCt[:, 1], in1=bc_h(IDB, HP), op=ALU.add),
             reads=[B["NCt"], CST], writes=[B["RB"]])
        yield
        b = nb()
        for v in range(2):
            for h in range(HP):
                mm(b, v * HP + h, S.NCt[:, v, h, :], IDB, True, True, [B["NCt"], CST])
        P.op("act", lambda b=b: nc.scalar.copy(out=S.NCn[:], in_=ps4(b)), reads=[PB[b]], writes=[B["NCn"]])
        P.op("pool", lambda: nc.gpsimd.tensor_tensor(out=S.RB[:, 1], in0=S.NCn[:, 1], in1=bc_h(IDB, HP), op=ALU.add),
             reads=[B["NCn"], CST], writes=[B["RB"]])
        yield
        Ct, Cn = S.NCt[:, 1], S.NCn[:, 1]
        Ntt, Nnn = S.NCt[:, 0], S.NCn[:, 0]

        def level(dst, dname, terms_t, terms_n, reads, mask=None, only_t=False, out_ap=None, add=None):
            b = nb()
            for _ in range(DN_WARM):
                P.op("pe", lambda: nc.tensor.matmul(psum[b][:], IDB, qkT[:, 0, 0:512], start=True, stop=True),
                     reads=[CST, QKB], writes=[PB[b]])
            for v, terms in ((0, terms_t), (1, terms_n)):
                if only_t and v == 1:
                    continue
                for h in range(HP):
                    n = len(terms)
                    for k, (l, r) in enumerate(terms):
                        lh = l if l is IDB else l[:, h, :]
                        rh = r if r is IDB else r[:, h, :]
                        P.op("pe", lambda lh=lh, rh=rh, k=k, n=n, v=v, h=h: nc.tensor.matmul(
                            psum[b][:, (v * HP + h) * 128:(v * HP + h + 1) * 128], lh, rh, start=(k == 0),
                            stop=(k == n - 1)), reads=reads + [CST], writes=[PB[b]])
            src = ps4(b)[:, 0] if only_t else ps4(b)
            o = out_ap if out_ap is not None else (dst[:, 0] if only_t else dst[:])
            if add is not None:
                P.op("dve", lambda: nc.vector.tensor_tensor(out=o, in0=src, in1=add[:], op=ALU.add),
                     reads=[PB[b]] + reads, writes=[B[dname]])
            elif mask is None:
                P.op("act", lambda: nc.scalar.copy(out=o, in_=src), reads=[PB[b]], writes=[B[dname]])
            else:
                mk = cstb[:, mask, :]
                mb = mk.unsqueeze(1).broadcast_to([128, HP, 128]) if only_t else \
                    mk.unsqueeze(1).unsqueeze(1).broadcast_to([128, 2, HP, 128])
                P.op("dve", lambda: nc.vector.tensor_tensor(out=o, in0=src, in1=mb, op=ALU.mult),
                     reads=[PB[b], CST], writes=[B[dname]])

        P2t, P2n = S.P2[:, 0], S.P2[:, 1]
        P4t, P4n = S.P4[:, 0], S.P4[:, 1]
        RAt, RAn = S.RA[:, 0], S.RA[:, 1]
        RBt, RBn = S.RB[:, 0], S.RB[:, 1]
        rd = [B["NCt"], B["NCn"], B["P2"], B["P4"], B["RA"], B["RB"]]
        level(S.P2, "P2", [(Cn, Ct)], [(Ct, Cn)], rd)
        yield
        level(S.RA, "RA", [(IDB, RBt), (RBn, P2t)], [(IDB, RBn), (P2t, RBn)], rd)
        yield
        level(S.P4, "P4", [(P2n, P2t)], [(P2t, P2n)], rd)
        yield
        level(S.RB, "RB", [(IDB, RAt), (RAn, P4t)], [(IDB, RAn), (P4t, RAn)], rd)
        yield
        level(S.P2, "P2", [(P4n, P4t)], [], rd, only_t=True)
        yield
        level(S.RA, "RA", [(IDB, RBt), (RBn, P2t)], [(IDB, RBn), (P2t, RBn)], rd)
        yield
        level(S.P4, "P4", [(Nnn, RAt)], [(Ntt, RAn)], rd, mask=CB_NOFF16)
        yield
        level(S.RB, "RB", [(RAn, P4t)], [(RAt, P4n)], rd, add=S.RA)
        yield
        level(S.P4, "P4", [(Nnn, RBt)], [(Ntt, RBn)], rd, mask=CB_NOFF32)
        yield
        level(S.RA, "RA", [(RBn, P4t)], [(RBt, P4n)], rd, add=S.RB)
        yield
        level(S.P4, "P4", [(Nnn, RAt)], [], rd, mask=CB_NOFF64, only_t=True)
        yield
        level(None, "Yt", [(RAn, P4t)], [], rd, only_t=True, out_ap=SI.Yt[:, hs, :], add=RAt)
        yield

    def scan_step(t, d, q, first, slot_start, slot_end):
        S = sets[d]
        B = S.B
        ts = slice(t * 128, (t + 1) * 128)
        gc4 = slice(d * 4, d * 4 + 4)
        ex0 = d * 12
        pa, pq, po, pS = 4, 5, 6, 7
        v3 = lambda ap: ap.rearrange("p (h d) -> p h d", h=4)
        Sm2 = S.Sm[:].rearrange("p h d -> p (h d)")
        if first:
            P.dma("sp", lambda: nc.sync.dma_start(out=Sm2, in_=S0_d[d]), writes=[B["Sm"]])
            P.op("act", lambda: nc.scalar.copy(out=S.Sb[:], in_=S.Sm[:]), reads=[B["Sm"]], writes=[B["Sb"]])
        elif slot_start:
            P.op("dve", lambda: nc.vector.tensor_scalar(out=S.Sm[:], in0=S.Sm[:], scalar1=carry[:, 0:1], scalar2=None,
                                                        op0=ALU.mult), reads=[B["Sm"], CARRY], writes=[B["Sm"]])
            P.op("act", lambda: nc.scalar.copy(out=S.Sb[:], in_=S.Sm[:]), reads=[B["Sm"]], writes=[B["Sb"]])
        for h in range(4):
            mm(pa, h, S.Wt[q][:, h, :], S.Sb[:, h, :], True, True, [B["Wt%d" % q], B["Sb"]])
        for h in range(4):
            mm(pq, h, qkT[:, h, ts], S.Sb[:, h, :], True, True, [QKB, B["Sb"]])
        yield
        tA, TA = tmps[0], TMPS[0]
        P.op("dve", lambda: nc.vector.tensor_tensor(out=v3(tA[:]), in0=ps3(pa), in1=bc_c(bB[:, t, gc4]), op=ALU.mult),
             reads=[PB[pa], GATE], writes=[TA])
        P.op("pool", lambda: nc.gpsimd.tensor_tensor(out=S.vnew[:], in0=S.bu[q][:], in1=v3(tA[:]), op=ALU.subtract),
             reads=[B["bu%d" % q], TA], writes=[B["vnew"]])
        yield
        for h in range(4):
            mm(po, h, S.QKt[q][:, h, :], S.vnew[:, h, :], True, True, [B["QKt%d" % q], B["vnew"]])
        for h in range(4):
            mm(pS, h, S.kdec[q][:, h, :], S.vnew[:, h, :], True, True, [B["kdec%d" % q], B["vnew"]])
        yield
        tB, TB = tmps[1], TMPS[1]
        P.op("pool", lambda: nc.gpsimd.tensor_tensor(out=v3(tB[:]), in0=S.Sm[:], in1=bc_c(EX[:, t, ex0 + 8:ex0 + 12]),
                                                     op=ALU.mult), reads=[B["Sm"], GATE], writes=[TB])
        P.op("dve", lambda: nc.vector.tensor_tensor(out=S.Sm[:], in0=ps3(pS), in1=v3(tB[:]), op=ALU.add),
             reads=[PB[pS], TB], writes=[B["Sm"]])
        P.op("act", lambda: nc.scalar.copy(out=S.Sb[:], in_=S.Sm[:]), reads=[B["Sm"]], writes=[B["Sb"]])
        if slot_end:
            slot = t // 2
            P.dma("sp", lambda: nc.sync.dma_start(out=st_d[slot, d], in_=Sm2), reads=[B["Sm"]])
        yield
        finish_tile(t, d, pq, po, EX[:, t, ex0:ex0 + 4])
        yield

    def run(gens):
        gens = list(gens)
        while gens:
            for g in list(gens):
                try:
                    next(g)
                except StopIteration:
                    gens.remove(g)

    scan_lock = [None]

    def locked_scan(key, g):
        while scan_lock[0] is not None and scan_lock[0] != key:
            yield
        scan_lock[0] = key
        yield from g
        scan_lock[0] = None

    def rr(gens):
        gens = list(gens)
        while gens:
            for g in list(gens):
                try:
                    next(g)
                except StopIteration:
                    gens.remove(g)
                yield

    def dir_driver(d):
        prev_scan = None
        for s in range(ntl):
            t = s if d == 0 else ntl - 1 - s
            q = s % 2
            work = [pair(t, d, 0, q), pair(t, d, 1, q), inst_pre(t, d, q)]
            if prev_scan is not None:
                work.append(prev_scan)
            yield from rr(work)
            yield from rr([inst_post(t, d, q)])
            if d == 0:
                st, en = (t % 2 == 0), (t % 2 == 1)
            else:
                st, en = (t % 2 == 1), (t % 2 == 0)
            prev_scan = locked_scan((d, s), scan_step(t, d, q, s == 0, st, en))
        yield from prev_scan

    g0, g1 = dir_driver(0), dir_driver(1)
    for _ in range(20):
        next(g0)
    run([g0, g1])


def build(debug=None, stages=("ffn1", "mixer", "ffn2")):
    nc = bass.Bass("TRN2", target_bir_lowering=False)
    dt = nc.dram_tensor
    x_d = dt("x", [T, D], F32, kind="ExternalInput").ap()
    pv_d = dt("pvec", [384, 128], F32, kind="ExternalInput").ap()
    wmod_d = dt("w_mod", [D, 9 * D], F32, kind="ExternalInput").ap()
    f1i_d = dt("ffn1_w_in", [D, 2 * DFF], F32, kind="ExternalInput").ap()
    f1o_d = dt("ffn1_w_out", [DFF, D], F32, kind="ExternalInput").ap()
    f2i_d = dt("ffn2_w_in", [D, 2 * DFF], F32, kind="ExternalInput").ap()
    f2o_d = dt("ffn2_w_out", [DFF, D], F32, kind="ExternalInput").ap()
    cf_d = dt("cstf", [128, NCF, 128], F32, kind="ExternalInput").ap()
    cb_d = dt("cstb", [128, NCB, 128], F32, kind="ExternalInput").ap()
    win_d = dt("w_in", [D, IN_COLS], F32, kind="ExternalInput").ap()
    wout_d = dt("w_out", [D, D], F32, kind="ExternalInput").ap()
    gc_d = dt("gconst", [128, 16], F32, kind="ExternalInput").ap()
    nl_d = dt("nlink", [128, 8], F32, kind="ExternalInput").ap()
    lk_d = dt("link32", [128, 32], F32, kind="ExternalInput").ap()
    ca_d = dt("carry", [128, 1], F32, kind="ExternalInput").ap()
    s0_d = dt("s0", [2, 128, 512], F32, kind="ExternalInput").ap()
    y_d = dt("y", [T, D], F32, kind="ExternalOutput").ap()
    st_d = dt("st", [8, 2, 128, 512], F32, kind="ExternalOutput").ap()
    dbg_d = None
    if debug:
        dbg_d = dt("dbg", [128, 8, T], F32, kind="ExternalOutput").ap()

    es = ExitStack()
    with es:
        P = Prog(nc, es)
        sb = lambda name, shape, dty: es.enter_context(nc.sbuf_tensor(name, shape, dty))
        ps = lambda name: es.enter_context(nc.psum_tensor(name, [128, 512], F32))

        xT = sb("xT", [128, DC, T], F32)
        XB = [[Buf("x%d_%d" % (c, g)) for g in range(NG)] for c in range(DC)]
        cstf = sb("cstf_s", [128, NCF, 128], F32)
        IDFB = Buf("cstf", frozen=True)
        CSTF = IDFB
        cstb = sb("cstb_s", [128, NCB, 128], BF16)
        CST = Buf("cst", frozen=True)
        smalls = sb("smalls", [128, 64], F32)
        SML = Buf("smalls", frozen=True)
        gconst, nlink, link32, carry = smalls[:, 0:16], smalls[:, 16:24], smalls[:, 24:56], smalls[:, 56:57]
        pT = sb("pT", [128, 384], F32)
        PT = Buf("pT", frozen=True)
        sc = sb("silu_c", [128, DC], BF16)
        SC = Buf("silu_c", frozen=True)
        WMR = [Buf("wmr%d" % k) for k in range(3)]
        bgbuf = {}
        modT = sb("modT", [128, 72], F32)
        MOD = Buf("mod", frozen=True)
        ab = sb("ab", [128, 3 * 3 * DC], F32)
        AB = Buf("ab", frozen=True)
        psum = [ps("ps%d" % i) for i in range(8)]
        PB = [Buf("ps%d" % i) for i in range(8)]
        rstd = sb("rstd", [128, 512], F32)
        RS = Buf("rstd")
        sqs = [sb("sq%d" % i, [128, 512], BF16) for i in range(3)]
        SQS = [Buf("sq%d" % i) for i in range(3)]
        tmpf = [sb("tmpf%d" % i, [128, 512], F32) for i in range(2)]
        TF = [Buf("tmpf%d" % i) for i in range(2)]
        sqi = [0]

        def nsq():
            sqi[0] = (sqi[0] + 1) % 3
            return sqs[sqi[0]], SQS[sqi[0]]

        IDF = cstf[:, CF_ID, :]
        IDB = cstb[:, CB_ID, :]
        MEANB = cstb[:, CB_MEAN, :]
        ONEB = cstb[:, CB_ONE, :]

        PC_COND, PC_BMOD, PC_NG = 0, 8, 80
        PC_DNC, PC_CVW, PC_CVB, PC_LNG, PC_LNB, PC_DNG = 128, 164, 288, 292, 296, 300

        P.dma("sp", lambda: nc.sync.dma_start(out=cstf[:], in_=cf_d[:, :, :]), writes=[IDFB])
        P.dma("pool", lambda: nc.gpsimd.dma_start(out=cstb[:], in_=cb_d[:, :, :]), writes=[CST])
        P.dma("sp", lambda: nc.sync.dma_start(out=smalls[:, 0:16], in_=gc_d[:, :]), writes=[SML])
        P.dma("sp", lambda: nc.sync.dma_start(out=smalls[:, 16:24], in_=nl_d[:, :]), writes=[SML])
        P.dma("sp", lambda: nc.sync.dma_start(out=smalls[:, 24:56], in_=lk_d[:, :]), writes=[SML])
        P.dma("sp", lambda: nc.sync.dma_start(out=smalls[:, 56:57], in_=ca_d[:, :]), writes=[SML])

        with ExitStack() as es0:
            sb0 = lambda name, shape, dty: es0.enter_context(nc.sbuf_tensor(name, shape, dty))
            pst = sb0("pstage", [128, 3, 128], F32)
            PST = Buf("pstage")
            P.dma("sp", lambda: nc.sync.dma_start(out=pst[:], in_=pv_d.rearrange("(k p) f -> p k f", p=128)),
                  writes=[PST])
            for k in range(3):
                P.op("pe", lambda k=k: nc.tensor.transpose(out=psum[0][:, k * 128:(k + 1) * 128], in_=pst[:, k, :],
                                                           identity=IDF), reads=[PST, IDFB], writes=[PB[0]])
            P.op("act", lambda: nc.scalar.copy(out=pT[:], in_=psum[0][:, 0:384]), reads=[PB[0]], writes=[PT])

            stage = [sb0("stage%d" % i, [128, D], F32) for i in range(2)]
            STG = [Buf("stage%d" % i) for i in range(2)]
            for t in range(NT):
                s = t % 2
                P.dma("sp", lambda t=t, s=s: nc.sync.dma_start(out=stage[s][:], in_=x_d[t * 128:(t + 1) * 128, :]),
                      writes=[STG[s]])
                for half in range(2):
                    pb = 1 + (2 * t + half) % 2
                    for cc in range(4):
                        c = half * 4 + cc
                        P.op("pe", lambda s=s, c=c, cc=cc, pb=pb: nc.tensor.transpose(
                            out=psum[pb][:, cc * 128:(cc + 1) * 128], in_=stage[s][:, c * 128:(c + 1) * 128],
                            identity=IDF), reads=[STG[s], IDFB], writes=[PB[pb]])
                    outap = xT[:, half * 4:half * 4 + 4, t * 128:(t + 1) * 128]
                    inap = psum[pb][:].rearrange("p (c t) -> p c t", c=4)
                    wr = [XB[half * 4 + cc][t // 4] for cc in range(4)]
                    if half == 0:
                        P.op("act", lambda outap=outap, inap=inap: nc.scalar.copy(out=outap, in_=inap),
                             reads=[PB[pb]], writes=wr)
                    else:
                        P.op("dve", lambda outap=outap, inap=inap: nc.vector.tensor_copy(out=outap, in_=inap),
                             reads=[PB[pb]], writes=wr)

            P.op("act", lambda: nc.scalar.activation(out=sc[:], in_=pT[:, PC_COND:PC_COND + 8], func=AF.Silu),
                 reads=[PT], writes=[SC])
            wm = [sb0("wm%d" % i, [128, DC, 512], BF16) for i in range(2)]
            WM = [Buf("wm%d" % i) for i in range(2)]
            for pc in range(18):
                s = pc % 2
                P.dma("pool", lambda pc=pc, s=s: nc.gpsimd.dma_start(
                    out=wm[s][:], in_=wmod_d[:, pc * 512:(pc + 1) * 512].rearrange("(k p) f -> p k f", p=128)),
                    writes=[WM[s]])
                for mm in range(4):
                    col = pc * 4 + mm
                    for kc in range(DC):
                        P.op("pe", lambda s=s, mm=mm, kc=kc, col=col: nc.tensor.matmul(
                            psum[3][:, col:col + 1], wm[s][:, kc, mm * 128:(mm + 1) * 128], sc[:, kc:kc + 1],
                            start=(kc == 0), stop=(kc == DC - 1)), reads=[WM[s], SC], writes=[PB[3]])
            P.op("dve", lambda: nc.vector.tensor_tensor(out=modT[:], in0=psum[3][:, 0:72],
                                                        in1=pT[:, PC_BMOD:PC_BMOD + 72], op=ALU.add),
                 reads=[PB[3], PT], writes=[MOD])

            def mod_ab(i, do_ab, do_g):
                rw = 0.5 if i != 1 else 1.0
                a_ap = ab[:, i * 24:i * 24 + 8]
                b_ap = ab[:, i * 24 + 8:i * 24 + 16]
                g_ap = ab[:, i * 24 + 16:i * 24 + 24]
                if do_ab:
                    P.op("dve", lambda: nc.vector.scalar_tensor_tensor(
                        out=a_ap, in0=modT[:, (3 * i + 1) * 8:(3 * i + 2) * 8], scalar=1.0,
                        in1=pT[:, PC_NG + 16 * i:PC_NG + 16 * i + 8], op0=ALU.add, op1=ALU.mult),
                        reads=[MOD, PT], writes=[AB])
                    P.op("dve", lambda: nc.vector.tensor_copy(out=b_ap, in_=modT[:, 3 * i * 8:3 * i * 8 + 8]),
                         reads=[MOD], writes=[AB])
                if do_g:
                    P.op("dve", lambda: nc.vector.scalar_tensor_tensor(
                        out=g_ap, in0=modT[:, (3 * i + 2) * 8:(3 * i + 3) * 8], scalar=rw,
                        in1=pT[:, PC_NG + 16 * i + 8:PC_NG + 16 * i + 16], op0=ALU.mult, op1=ALU.mult),
                        reads=[MOD, PT], writes=[AB])

            for i_ in range(3):
                mod_ab(i_, True, True)
            P.barrier()

        def mod_rest():
            wmr = bgbuf["wmr"]
            cols = list(range(16, 72))
            SKEW = 2

            def load(idx):
                col = cols[idx]
                k = idx % 3
                P.dma("pool", lambda: nc.gpsimd.dma_start(
                    out=wmr[k][:], in_=wmod_d[:, col * 128:(col + 1) * 128].rearrange("(k p) f -> p k f", p=128)),
                    writes=[WMR[k]])

            for idx in range(SKEW):
                load(idx)
            for idx, col in enumerate(cols):
                if idx + SKEW < len(cols):
                    load(idx + SKEW)
                k = idx % 3
                pbk = 3 if col < 24 else 2
                for kc in range(DC):
                    P.op("pe", lambda k=k, kc=kc, col=col, pbk=pbk: nc.tensor.matmul(
                        psum[pbk][:, col:col + 1], wmr[k][:, kc, :], sc[:, kc:kc + 1], start=(kc == 0),
                        stop=(kc == DC - 1)), reads=[WMR[k], SC], writes=[PB[pbk]])
                if col == 23:
                    P.op("dve", lambda: nc.vector.tensor_tensor(out=modT[:, 16:24], in0=psum[3][:, 16:24],
                                                                in1=pT[:, PC_BMOD + 16:PC_BMOD + 24], op=ALU.add),
                         reads=[PB[3], PT], writes=[MOD])
                    mod_ab(0, False, True)
                yield
            P.op("dve", lambda: nc.vector.tensor_tensor(out=modT[:, 24:72], in0=psum[2][:, 24:72],
                                                        in1=pT[:, PC_BMOD + 24:PC_BMOD + 72], op=ALU.add),
                 reads=[PB[2], PT], writes=[MOD])
            mod_ab(1, True, True)
            mod_ab(2, True, True)
            yield

        bg = []

        def bg_step(n):
            for _ in range(n):
                if bg:
                    try:
                        next(bg[0])
                    except StopIteration:
                        bg.pop(0)

        rtmp = sb("rtmp", [128, 512], F32)
        RT = Buf("rtmp")
        epsT = sb("epsT", [128, 1], F32)
        EPST = Buf("epsT", frozen=True)
        P.op("dve", lambda: nc.vector.memset(epsT[:], EPS), writes=[EPST])

        def rstd_from(pbank):
            P.op("act", lambda: nc.scalar.activation(out=rtmp[:], in_=psum[pbank][:], func=AF.Ln, bias=epsT[:, 0:1],
                                                     scale=1.0), reads=[PB[pbank], EPST], writes=[RT])
            P.op("act", lambda: nc.scalar.activation(out=rstd[:], in_=rtmp[:], func=AF.Exp, scale=-0.5),
                 reads=[RT], writes=[RS])

        pn_extra = []

        def prenorm(i, hT, HB, groups):
            for li, g in enumerate(groups):
                gs = slice(g * 512, (g + 1) * 512)
                ls = slice(li * 512, (li + 1) * 512)
                for c in range(DC):
                    sq, SQ = nsq()
                    if c % 2 == 0:
                        P.op("act", lambda c=c, gs=gs, sq=sq: nc.scalar.activation(out=sq[:], in_=xT[:, c, gs],
                                                                                  func=AF.Square),
                             reads=[XB[c][g]], writes=[SQ])
                    else:
                        P.op("dve", lambda c=c, gs=gs, sq=sq: nc.vector.tensor_tensor(
                            out=sq[:], in0=xT[:, c, gs], in1=xT[:, c, gs], op=ALU.mult),
                            reads=[XB[c][g]], writes=[SQ])
                    P.op("pe", lambda c=c, sq=sq: nc.tensor.matmul(psum[0][:], MEANB, sq[:], start=(c == 0),
                                                                   stop=(c == DC - 1)),
                         reads=[SQ, CST], writes=[PB[0]])
                rstd_from(0)
                tl = list(zip(tmpf, TF)) + pn_extra
                for c in range(DC):
                    tb, TB_ = tl[c % len(tl)]
                    P.op("dve", lambda c=c, gs=gs, tb=tb: nc.vector.scalar_tensor_tensor(
                        out=tb[:], in0=xT[:, c, gs], scalar=ab[:, i * 24 + c:i * 24 + c + 1], in1=rstd[:],
                        op0=ALU.mult, op1=ALU.mult), reads=[XB[c][g], RS, AB], writes=[TB_])
                    P.op("act", lambda c=c, ls=ls, tb=tb: nc.scalar.activation(
                        out=hT[:, c, ls], in_=tb[:], func=AF.Identity,
                        bias=ab[:, i * 24 + 8 + c:i * 24 + 9 + c], scale=1.0), reads=[TB_, AB], writes=[HB[li]])

        def postnorm_residual(i, ybuf, YB, groups, pstat):
            for li, g in enumerate(groups):
                gs = slice(g * 512, (g + 1) * 512)
                ls = slice(li * 512, (li + 1) * 512)
                rstd_from(pstat[li])
                for c in range(DC):
                    k = c % 2
                    P.op("dve", lambda c=c, ls=ls, k=k: nc.vector.scalar_tensor_tensor(
                        out=tmpf[k][:], in0=ybuf[:, c, ls],
                        scalar=ab[:, i * 24 + 16 + c:i * 24 + 17 + c], in1=rstd[:], op0=ALU.mult, op1=ALU.mult),
                        reads=[YB[c][li], RS, AB], writes=[TF[k]])
                    P.op("dve", lambda c=c, gs=gs, k=k: nc.vector.tensor_tensor(
                        out=xT[:, c, gs], in0=tmpf[k][:], in1=xT[:, c, gs], op=ALU.add),
                        reads=[TF[k], XB[c][g]], writes=[XB[c][g]])

        def store_tiles(t0, t1, ost, OST):
            for t in range(t0, t1):
                s = t % 2
                for half in range(2):
                    pb = 1 + (2 * t + half) % 2
                    for cc in range(4):
                        c = half * 4 + cc
                        P.op("pe", lambda t=t, c=c, cc=cc, pb=pb: nc.tensor.transpose(
                            out=psum[pb][:, cc * 128:(cc + 1) * 128], in_=xT[:, c, t * 128:(t + 1) * 128],
                            identity=IDF), reads=[XB[c][t // 4], IDFB], writes=[PB[pb]])
                    P.op("act", lambda s=s, pb=pb, half=half: nc.scalar.copy(
                        out=ost[s][:, half * 512:(half + 1) * 512], in_=psum[pb][:]), reads=[PB[pb]], writes=[OST[s]])
                P.dma("sp", lambda t=t, s=s: nc.sync.dma_start(out=y_d[t * 128:(t + 1) * 128, :], in_=ost[s]),
                      reads=[OST[s]])

        def ffn(i, wi_d, wo_d, store_in_scope=False):
            with ExitStack() as es2:
                sb2 = lambda name, shape, dty: es2.enter_context(nc.sbuf_tensor("%s_f%d" % (name, i), shape, dty))
                hT = sb2("hT", [128, DC, 1024], BF16)
                HB = [Buf("h%d" % g) for g in range(2)]
                hid = sb2("hid", [128, FC, 1024], BF16)
                HID = [[Buf("hid%d_%d" % (m, g)) for g in range(2)] for m in range(FC)]
                ybuf = sb2("ybuf", [128, DC, 1024], F32)
                YB = [[Buf("y%d_%d" % (m, g)) for g in range(2)] for m in range(DC)]
                wsl = [sb2("wsl%d" % k, [128, 2, DC, 128], BF16) for k in range(3)]
                WS = [Buf("wsl%d" % k) for k in range(3)]
                w2sl = [sb2("w2sl%d" % k, [128, FC, 128], BF16) for k in range(2)]
                W2S = [Buf("w2sl%d" % k) for k in range(2)]
                pn_extra[:] = [(sb2("pnx%d" % k, [128, 512], F32), Buf("pnx%d" % k)) for k in range(2)]
                rsy = [sb2("rsy%d" % k, [128, 512], F32) for k in range(2)]
                RSY = [Buf("rsy%d_f%d" % (k, i)) for k in range(2)]
                deferred_post = []
                wcnt = 0
                w2cnt = 0
                for half in range(2):
                    groups = [2 * half, 2 * half + 1]
                    prenorm(i, hT, HB, groups)
                    order = [(m, 0) for m in range(3)] + [(m, 1) for m in range(3)] + \
                            [(m, li) for m in range(3, FC) for li in range(2)]
                    w_issued = set()
                    for it_, (m, li) in enumerate(order):
                        if deferred_post and it_ >= 3:
                            deferred_post.pop(0)()
                        s = (half * FC + m) % 3
                        if m not in w_issued:
                            w_issued.add(m)
                            P.dma("pool", lambda m=m, s=s: nc.gpsimd.dma_start(
                                out=wsl[s][:, 0],
                                in_=wi_d[:, m * 128:(m + 1) * 128].rearrange("(k p) f -> p k f", p=128)),
                                writes=[WS[s]])
                            P.dma("pool", lambda m=m, s=s: nc.gpsimd.dma_start(
                                out=wsl[s][:, 1],
                                in_=wi_d[:, DFF + m * 128:DFF + (m + 1) * 128].rearrange("(k p) f -> p k f", p=128)),
                                writes=[WS[s]])
                        ls = slice(li * 512, (li + 1) * 512)
                        pg = 4 + 2 * (it_ % 2)
                        pu = pg + 1
                        for kc in range(DC):
                            P.op("pe", lambda s=s, kc=kc, ls=ls, pg=pg: nc.tensor.matmul(
                                psum[pg][:], wsl[s][:, 0, kc, :], hT[:, kc, ls], start=(kc == 0),
                                stop=(kc == DC - 1)), reads=[WS[s], HB[li]], writes=[PB[pg]])
                        for kc in range(DC):
                            P.op("pe", lambda s=s, kc=kc, ls=ls, pu=pu: nc.tensor.matmul(
                                psum[pu][:], wsl[s][:, 1, kc, :], hT[:, kc, ls], start=(kc == 0),
                                stop=(kc == DC - 1)), reads=[WS[s], HB[li]], writes=[PB[pu]])
                        sq, SQ = nsq()
                        P.op("act", lambda pg=pg, sq=sq: nc.scalar.activation(out=sq[:], in_=psum[pg][:],
                                                                             func=AF.Silu),
                             reads=[PB[pg]], writes=[SQ])
                        P.op("dve", lambda m=m, ls=ls, pu=pu, sq=sq: nc.vector.tensor_tensor(
                            out=hid[:, m, ls], in0=psum[pu][:], in1=sq[:], op=ALU.mult),
                            reads=[PB[pu], SQ], writes=[HID[m][li]])
                    if store_in_scope and half == 1:
                        while deferred_post:
                            deferred_post.pop(0)()
                        ostv = [hT[:, 2 * k:2 * k + 2, :].rearrange("p c t -> p (c t)").bitcast(F32) for k in range(2)]
                        OSTV = [Buf("ostv%d" % k) for k in range(2)]
                        store_tiles(0, NT // 2, ostv, OSTV)
                    pend = []
                    for m in range(DC):
                        bg_step(2 if half == 0 else 3)
                        s = w2cnt % 2
                        w2cnt += 1
                        P.dma("pool", lambda m=m, s=s: nc.gpsimd.dma_start(
                            out=w2sl[s][:], in_=wo_d[:, m * 128:(m + 1) * 128].rearrange("(k p) f -> p k f", p=128)),
                            writes=[W2S[s]])
                        for li in range(2):
                            ls = slice(li * 512, (li + 1) * 512)
                            py = 4 + (m * 2 + li) % 2
                            for kc in range(FC):
                                P.op("pe", lambda s=s, kc=kc, ls=ls, py=py: nc.tensor.matmul(
                                    psum[py][:], w2sl[s][:, kc, :], hid[:, kc, ls], start=(kc == 0),
                                    stop=(kc == FC - 1)), reads=[W2S[s], HID[kc][li]], writes=[PB[py]])
                            P.op("act", lambda m=m, ls=ls, py=py: nc.scalar.copy(
                                out=ybuf[:, m, ls], in_=psum[py][:]), reads=[PB[py]], writes=[YB[m][li]])
                            sq, SQ = nsq()
                            P.op("act", lambda py=py, sq=sq: nc.scalar.activation(out=sq[:], in_=psum[py][:],
                                                                                 func=AF.Square),
                                 reads=[PB[py]], writes=[SQ])
                            def stat(m=m, li=li, sq=sq, SQ=SQ):
                                P.op("pe", lambda: nc.tensor.matmul(
                                    psum[6 + li][:], MEANB, sq[:], start=(m == 0), stop=(m == DC - 1)),
                                    reads=[SQ, CST], writes=[PB[6 + li]])
                            pend.append(stat)
                            if len(pend) > 1:
                                pend.pop(0)()
                    while pend:
                        pend.pop(0)()
                    if half == 1:
                        postnorm_residual(i, ybuf, YB, groups, [6, 7])
                    else:
                        for li in range(2):
                            P.op("act", lambda li=li: nc.scalar.activation(out=rtmp[:], in_=psum[6 + li][:], func=AF.Ln,
                                                                           bias=epsT[:, 0:1], scale=1.0),
                                 reads=[PB[6 + li], EPST], writes=[RT])
                            P.op("act", lambda li=li: nc.scalar.activation(out=rsy[li][:], in_=rtmp[:], func=AF.Exp,
                                                                           scale=-0.5), reads=[RT], writes=[RSY[li]])
                        for li in range(2):
                            g = groups[li]
                            for c in range(DC):
                                def chunk(li=li, g=g, c=c):
                                    gs = slice(g * 512, (g + 1) * 512)
                                    ls = slice(li * 512, (li + 1) * 512)
                                    k = c % 2
                                    P.op("dve", lambda: nc.vector.scalar_tensor_tensor(
                                        out=tmpf[k][:], in0=ybuf[:, c, ls],
                                        scalar=ab[:, i * 24 + 16 + c:i * 24 + 17 + c], in1=rsy[li][:], op0=ALU.mult,
                                        op1=ALU.mult), reads=[YB[c][li], RSY[li], AB], writes=[TF[k]])
                                    P.op("dve", lambda: nc.vector.tensor_tensor(
                                        out=xT[:, c, gs], in0=tmpf[k][:], in1=xT[:, c, gs], op=ALU.add),
                                        reads=[TF[k], XB[c][g]], writes=[XB[c][g]])
                                deferred_post.append(chunk)
                while deferred_post:
                    deferred_post.pop(0)()
                if store_in_scope:
                    store_tiles(NT // 2, NT, ostv, OSTV)
                P.barrier()
                pn_extra[:] = []

        def mixer():
            i = 1
            with ExitStack() as esm:
                sbm = lambda name, shape, dty: esm.enter_context(nc.sbuf_tensor("mx_" + name, shape, dty))
                oT = sbm("oT", [128, NT, 4, 128], BF16)
                OT = [Buf("oT%d" % t) for t in range(NT)]
                gB = sbm("gB", [128, NT, 8], F32)
                bB = sbm("bB", [128, NT, 8], F32)
                EX = sbm("EX", [128, NT, 24], F32)
                lnB = sbm("lnB", [128, NT, 8], F32)
                GSRC, GATE = Buf("gsrc"), Buf("gate")
                with ExitStack() as esab:
                    sbab = lambda name, shape, dty: esab.enter_context(nc.sbuf_tensor("ab_" + name, shape, dty))
                    qkT = sbab("qkT", [128, 8, T], BF16)
                    vT = sbab("vT", [128, 4, T], BF16)
                    QKB, VB = Buf("qk"), Buf("v")
                    with ExitStack() as esa:
                        sba = lambda name, shape, dty: esa.enter_context(nc.sbuf_tensor("a_" + name, shape, dty))
                        hT_a = sba("hT_a", [128, DC, 1024], BF16)
                        HB_a = [Buf("mh%d" % g) for g in range(2)]
                        wsl_a = [sba("w%d" % k, [128, DC, 128], BF16) for k in range(3)]
                        WS_a = [Buf("mw%d" % k) for k in range(3)]
                        wg_a = sba("wg_a", [128, DC, 16], BF16)
                        WG_a = Buf("wg_a")
                        acc_a = [sba("acc_a%d" % k, [128, 512], F32) for k in range(2)]
                        ACC_a = [Buf("acc_a%d" % k) for k in range(2)]
                        sv_a = [sba("sv_a%d" % k, [128, 512], F32) for k in range(3)]
                        SV_a = [Buf("sv_a%d" % k) for k in range(3)]
                        rstd2_a = [rstd, sba("rstd2", [128, 512], F32)]
                        RS2_a = [RS, Buf("rstd2")]
                        qk_cnt = [0]
                        t7_a = [sba("t7_a%d" % k, [128, 8], F32) for k in range(2)]
                        T7_a = [Buf("t7_a%d" % k) for k in range(2)]
                        P.dma("pool", lambda: nc.gpsimd.dma_start(
                            out=wg_a[:], in_=win_d[:, 1536:1552].rearrange("(k p) f -> p k f", p=128)), writes=[WG_a])
                        chunks = [(qkT, cc, cc * 128, "q" if cc < 4 else "k") for cc in range(8)]
                        chunks += [(vT, cc, 1024 + cc * 128, "v") for cc in range(4)]
                        wcnt_a = 0
                        it_a = 0
                        q1_a, q2_a = [], []
                        pn_extra[:] = [(sba("pnx%d" % k, [128, 512], F32), Buf("pnxa%d" % k)) for k in range(2)]
                        for half in range(2):
                            groups_a = [2 * half, 2 * half + 1]
                            prenorm(i, hT_a, HB_a, groups_a)
                            for tl in range(8):
                                t = half * 8 + tl
                                li = tl // 4
                                tls = slice(tl * 128, (tl + 1) * 128)
                                for kc in range(DC):
                                    P.op("pe", lambda t=t, tls=tls, kc=kc: nc.tensor.matmul(
                                        psum[3][:, t * 16:(t + 1) * 16], hT_a[:, kc, tls], wg_a[:, kc, :],
                                        start=(kc == 0), stop=(kc == DC - 1)), reads=[HB_a[li], WG_a], writes=[PB[3]])
                            for ci, (dst, dc, col0, kind) in enumerate(chunks):
                                s_ = wcnt_a % 3
                                wcnt_a += 1
                                P.dma("pool", lambda s_=s_, col0=col0: nc.gpsimd.dma_start(
                                    out=wsl_a[s_][:], in_=win_d[:, col0:col0 + 128].rearrange("(k p) f -> p k f", p=128)),
                                    writes=[WS_a[s_]])
                                DB = {"q": QKB, "k": QKB, "v": VB}[kind]
                                for li in range(2):
                                    g = groups_a[li]
                                    gs = slice(g * 512, (g + 1) * 512)
                                    ls = slice(li * 512, (li + 1) * 512)
                                    pb = 4 + it_a % 4
                                    k2 = it_a % 2
                                    it_a += 1
                                    for kc in range(DC):
                                        P.op("pe", lambda s_=s_, kc=kc, ls=ls, pb=pb: nc.tensor.matmul(
                                            psum[pb][:], wsl_a[s_][:, kc, :], hT_a[:, kc, ls], start=(kc == 0),
                                            stop=(kc == DC - 1)), reads=[WS_a[s_], HB_a[li]], writes=[PB[pb]])
                                    cch = ci
                                    w0 = pT[:, PC_DNC + 0 * 12 + cch:PC_DNC + 0 * 12 + cch + 1]
                                    w1 = pT[:, PC_DNC + 1 * 12 + cch:PC_DNC + 1 * 12 + cch + 1]
                                    w2 = pT[:, PC_DNC + 2 * 12 + cch:PC_DNC + 2 * 12 + cch + 1]
                                    Pp = psum[pb]
                                    A_ = acc_a[k2]
                                    P.op("dve", lambda A_=A_, Pp=Pp, w1=w1: nc.vector.tensor_scalar(
                                        out=A_[:], in0=Pp[:], scalar1=w1, scalar2=None, op0=ALU.mult),
                                        reads=[PB[pb], PT], writes=[ACC_a[k2]])
                                    P.op("dve", lambda A_=A_, Pp=Pp, w0=w0: nc.vector.scalar_tensor_tensor(
                                        out=A_[:, 1:512], in0=Pp[:, 0:511], scalar=w0, in1=A_[:, 1:512], op0=ALU.mult,
                                        op1=ALU.add), reads=[PB[pb], PT, ACC_a[k2]], writes=[ACC_a[k2]])
                                    P.op("dve", lambda A_=A_, Pp=Pp, w2=w2: nc.vector.scalar_tensor_tensor(
                                        out=A_[:, 0:511], in0=Pp[:, 1:512], scalar=w2, in1=A_[:, 0:511], op0=ALU.mult,
                                        op1=ALU.add), reads=[PB[pb], PT, ACC_a[k2]], writes=[ACC_a[k2]])
                                    Pv = Pp[:].rearrange("p (r w) -> p r w", w=64)
                                    Av = A_[:].rearrange("p (r w) -> p r w", w=64)
                                    P.op("dve", lambda Pv=Pv, w0=w0, k2=k2: nc.vector.scalar_tensor_tensor(
                                        out=t7_a[k2][:, 0:7], in0=Pv[:, 0:7, 63], scalar=w0, in1=nlink[:, 1:8], op0=ALU.mult,
                                        op1=ALU.mult), reads=[PB[pb], PT, SML], writes=[T7_a[k2]])
                                    P.op("dve", lambda Av=Av, k2=k2: nc.vector.tensor_tensor(
                                        out=Av[:, 1:8, 0], in0=Av[:, 1:8, 0], in1=t7_a[k2][:, 0:7], op=ALU.add),
                                        reads=[T7_a[k2], ACC_a[k2]], writes=[ACC_a[k2]])
                                    P.op("dve", lambda Pv=Pv, w2=w2, k2=k2: nc.vector.scalar_tensor_tensor(
                                        out=t7_a[k2][:, 0:7], in0=Pv[:, 1:8, 0], scalar=w2, in1=nlink[:, 1:8], op0=ALU.mult,
                                        op1=ALU.mult), reads=[PB[pb], PT, SML], writes=[T7_a[k2]])
                                    P.op("dve", lambda Av=Av, k2=k2: nc.vector.tensor_tensor(
                                        out=Av[:, 0:7, 63], in0=Av[:, 0:7, 63], in1=t7_a[k2][:, 0:7], op=ALU.add),
                                        reads=[T7_a[k2], ACC_a[k2]], writes=[ACC_a[k2]])
                                    cs = (128.0 ** -0.5) if kind == "q" else 1.0
                                    j3 = qk_cnt[0] % 3
                                    j2 = qk_cnt[0] % 2
                                    if kind != "v":
                                        qk_cnt[0] += 1
                                    pn = j3

                                    def st1(kind=kind, A_=A_, dc=dc, gs=gs, k2=k2, DB=DB, pn=pn, j3=j3):
                                        if kind == "v":
                                            P.op("act", lambda: nc.scalar.activation(out=vT[:, dc, gs], in_=A_[:],
                                                                                     func=AF.Silu),
                                                 reads=[ACC_a[k2]], writes=[DB])
                                            return
                                        S_ = sv_a[j3]
                                        P.op("act", lambda: nc.scalar.activation(out=S_[:], in_=A_[:], func=AF.Silu),
                                             reads=[ACC_a[k2]], writes=[SV_a[j3]])
                                        sq, SQ = nsq()
                                        P.op("act", lambda: nc.scalar.activation(out=sq[:], in_=S_[:], func=AF.Square),
                                             reads=[SV_a[j3]], writes=[SQ])
                                        P.op("pe", lambda: nc.tensor.matmul(psum[pn][:], ONEB, sq[:], start=True, stop=True),
                                             reads=[SQ, CST], writes=[PB[pn]])

                                    def st2(pn=pn, dc=dc, gs=gs, cs=cs, j3=j3, j2=j2, DB=DB):
                                        rs_t, RS_B = rstd2_a[j2], RS2_a[j2]
                                        P.op("act", lambda: nc.scalar.activation(out=rtmp[:], in_=psum[pn][:], func=AF.Ln,
                                                                                 bias=epsT[:, 0:1], scale=1.0),
                                             reads=[PB[pn], EPST], writes=[RT])
                                        P.op("act", lambda: nc.scalar.activation(out=rs_t[:], in_=rtmp[:], func=AF.Exp,
                                                                                 scale=-0.5), reads=[RT], writes=[RS_B])
                                        P.op("dve", lambda: nc.vector.scalar_tensor_tensor(
                                            out=qkT[:, dc, gs], in0=sv_a[j3][:], scalar=cs, in1=rs_t[:], op0=ALU.mult,
                                            op1=ALU.mult), reads=[SV_a[j3], RS_B], writes=[DB])

                                    q1_a.append((st1, None if kind == "v" else st2))
                                    if len(q1_a) > 1:
                                        f1, f2 = q1_a.pop(0)
                                        f1()
                                        if f2 is not None:
                                            q2_a.append(f2)
                                    if len(q2_a) >= 3:
                                        q2_a.pop(0)()
                                        q2_a.pop(0)()
                            while q1_a:
                                f1, f2 = q1_a.pop(0)
                                f1()
                                if f2 is not None:
                                    q2_a.append(f2)
                            while q2_a:
                                q2_a.pop(0)()
                        gp = psum[3][:, 0:NT * 16].rearrange("p (t c) -> p t c", c=16)
                        gtmp = sba("gtmp", [128, NT, 8], F32)
                        GT = Buf("gtmp")
                        nal = sba("nal", [128, 8], F32)
                        NAL = Buf("nal")
                        P.op("dve", lambda: nc.vector.tensor_tensor(
                            out=gtmp[:], in0=gp[:, :, 0:8], in1=gconst[:, 8:16].unsqueeze(1).broadcast_to([128, NT, 8]),
                            op=ALU.add), reads=[PB[3], SML], writes=[GT])
                        P.op("act", lambda: nc.scalar.activation(out=gtmp[:], in_=gtmp[:], func=AF.Exp), reads=[GT],
                             writes=[GT])
                        P.op("dve", lambda: nc.vector.tensor_scalar(out=gtmp[:], in0=gtmp[:], scalar1=1.0, scalar2=None,
                                                                    op0=ALU.add), reads=[GT], writes=[GT])
                        P.op("act", lambda: nc.scalar.activation(out=gtmp[:], in_=gtmp[:], func=AF.Ln), reads=[GT],
                             writes=[GT])
                        P.op("act", lambda: nc.scalar.activation(out=nal[:], in_=gconst[:, 0:8], func=AF.Exp),
                             reads=[SML], writes=[NAL])
                        P.op("dve", lambda: nc.vector.scalar_tensor_tensor(
                            out=gB[:], in0=gtmp[:], scalar=-1.0, in1=nal[:].unsqueeze(1).broadcast_to([128, NT, 8]),
                            op0=ALU.mult, op1=ALU.mult), reads=[GT, NAL], writes=[GSRC])
                        P.op("act", lambda: nc.scalar.activation(out=bB[:], in_=gp[:, :, 8:16], func=AF.Sigmoid),
                             reads=[PB[3]], writes=[GATE])
                        P.op("act", lambda: nc.scalar.activation(out=lnB[:], in_=bB[:], func=AF.Ln), reads=[GATE],
                             writes=[GATE])
                        gate_cums(P, nc, NT, gB, EX, GATE, GSRC, cstf, CSTF, psum, PB, 0)
                        P.barrier()
                        pn_extra[:] = []
                    for b_ in (GSRC, GATE, QKB, VB):
                        b_.frozen = True
                    with ExitStack() as esb:
                        sbb = lambda name, shape, dty: esb.enter_context(nc.sbuf_tensor("b_" + name, shape, dty))
                        onb = sbb("onb", [128, 4, 128], BF16)
                        ONB = Buf("onb")
                        ms4 = sbb("ms4", [128, 8], F32)
                        MS4 = Buf("ms4")
                        tmps = [tmpf[0], tmpf[1], rtmp, rstd]
                        TMPS = [TF[0], TF[1], RT, RS]
                        v3 = lambda ap: ap.rearrange("p (h d) -> p h d", h=4)
                        ps3 = lambda b: psum[b][:].rearrange("p (h d) -> p h d", h=4)
                        bcl = lambda ap2: ap2.unsqueeze(2).broadcast_to([128, 4, 128])

                        def finish_tile(t, d, pq, po, eg_ap):
                            ts = slice(t * 128, (t + 1) * 128)
                            first = (d == 0) == (t < NT // 2)
                            tA, TA = tmps[2], TMPS[2]
                            tB, TB = tmps[3], TMPS[3]
                            part = oT[:, t, :, :]
                            P.op("dve", lambda: nc.vector.tensor_tensor(out=v3(tA[:]), in0=ps3(pq), in1=bcl(eg_ap),
                                                                        op=ALU.mult), reads=[PB[pq], GATE], writes=[TA])
                            if first:
                                P.op("dve", lambda: nc.vector.tensor_tensor(out=part, in0=ps3(po), in1=v3(tA[:]), op=ALU.add),
                                     reads=[PB[po], TA], writes=[OT[t]])
                                return
                            P.op("dve", lambda: nc.vector.tensor_tensor(out=v3(tA[:]), in0=ps3(po), in1=v3(tA[:]), op=ALU.add),
                                 reads=[PB[po], TA], writes=[TA])
                            P.op("pool", lambda: nc.gpsimd.tensor_tensor(out=v3(tA[:]), in0=v3(tA[:]), in1=part, op=ALU.add),
                                 reads=[TA, OT[t]], writes=[TA])
                            P.op("pool", lambda: nc.gpsimd.tensor_tensor(out=tB[:], in0=tA[:], in1=tA[:], op=ALU.mult),
                                 reads=[TA], writes=[TB])
                            P.op("dve", lambda: nc.vector.tensor_reduce(out=ms4[:, 0:4], in_=v3(tB[:]), axis=AX.X, op=ALU.add),
                                 reads=[TB], writes=[MS4])
                            P.op("act", lambda: nc.scalar.activation(out=ms4[:, 0:4], in_=ms4[:, 0:4], func=AF.Ln,
                                                                     bias=epsT[:, 0:1], scale=1.0 / 128.0),
                                 reads=[MS4, EPST], writes=[MS4])
                            P.op("act", lambda: nc.scalar.activation(out=ms4[:, 4:8], in_=ms4[:, 0:4], func=AF.Exp,
                                                                     scale=-0.5), reads=[MS4], writes=[MS4])
                            P.op("pool", lambda: nc.gpsimd.tensor_tensor(out=onb[:], in0=v3(tA[:]), in1=bcl(ms4[:, 4:8]),
                                                                         op=ALU.mult), reads=[TA, MS4], writes=[ONB])
                            for h in range(4):
                                P.op("pe", lambda h=h: nc.tensor.matmul(psum[pq][:, h * 128:(h + 1) * 128], onb[:, h, :], IDB,
                                                                        start=True, stop=True),
                                     reads=[ONB, CST], writes=[PB[pq]])
                            P.op("act", lambda: nc.scalar.activation(out=part, in_=ps3(pq), func=AF.Copy,
                                                                     scale=pT[:, PC_DNG:PC_DNG + 1]),
                                 reads=[PB[pq], PT], writes=[OT[t]])

                        deltanet(P, nc, esb, NT, qkT, QKB, vT, VB, gB, bB, lnB, EX, GATE, cstf, cstb, CSTF, CST, psum, PB,
                                 s0_d, carry, SML, st_d, tmps, TMPS, finish_tile, sq3=sqs)
                        P.barrier()
                cvT = sbm("cvT", [128, 4, T], BF16)
                CV = [[Buf("cv%d_%d" % (c, g)) for g in range(NG)] for c in range(4)]
                with ExitStack() as esc:
                    sbc_ = lambda name, shape, dty: esc.enter_context(nc.sbuf_tensor("c_" + name, shape, dty))
                    pad_c = sbc_("pad_c", [128, 4, 32, 94], BF16)
                    PAD_c = [Buf("pad_c%d" % c) for c in range(4)]
                    sg_c = [sbc_("sg_c%d" % k, [128, 512], F32) for k in range(2)]
                    SG_c = [Buf("sg_c%d" % k) for k in range(2)]
                    esc1 = ExitStack()
                    hT_c = esc1.enter_context(nc.sbuf_tensor("c_hT_c", [128, DC, 1024], BF16))
                    HB_c = [Buf("ch%d" % g) for g in range(2)]
                    wsl_c = [esc1.enter_context(nc.sbuf_tensor("c_w%d" % k, [128, 2, DC, 128], BF16)) for k in range(4)]
                    WS_c = [Buf("cw%d" % k) for k in range(4)]
                    pn_extra[:] = [(esc1.enter_context(nc.sbuf_tensor("c_pnx%d" % k, [128, 512], F32)), Buf("pnxc%d" % k))
                                   for k in range(2)]
                    for c in range(4):
                        P.op("pool", lambda c=c: nc.gpsimd.memset(pad_c[:, c], 0.0), writes=[PAD_c[c]])
                    wcnt_c = 0
                    it_c = 0
                    for half in range(2):
                        groups_c = [2 * half, 2 * half + 1]
                        prenorm(i, hT_c, HB_c, groups_c)
                        for c in range(4):
                            s_ = wcnt_c % 4
                            wcnt_c += 1
                            for j, col0 in enumerate((2064 + c * 128, 2576 + c * 128)):
                                P.dma("pool", lambda s_=s_, j=j, col0=col0: nc.gpsimd.dma_start(
                                    out=wsl_c[s_][:, j], in_=win_d[:, col0:col0 + 128].rearrange("(k p) f -> p k f", p=128)),
                                    writes=[WS_c[s_]])
                            for li in range(2):
                                g = groups_c[li]
                                ls = slice(li * 512, (li + 1) * 512)
                                pv = 4 + 2 * (it_c % 2)
                                pg = pv + 1
                                k2 = it_c % 2
                                it_c += 1
                                for j, pb in ((0, pv), (1, pg)):
                                    for kc in range(DC):
                                        P.op("pe", lambda s_=s_, j=j, kc=kc, ls=ls, pb=pb: nc.tensor.matmul(
                                            psum[pb][:], wsl_c[s_][:, j, kc, :], hT_c[:, kc, ls], start=(kc == 0),
                                            stop=(kc == DC - 1)), reads=[WS_c[s_], HB_c[li]], writes=[PB[pb]])
                                P.op("act", lambda pg=pg, k2=k2: nc.scalar.activation(out=sg_c[k2][:], in_=psum[pg][:],
                                                                                     func=AF.Sigmoid),
                                     reads=[PB[pg]], writes=[SG_c[k2]])
                                P.op("dve", lambda c=c, g=g, pv=pv, k2=k2: nc.vector.tensor_tensor(
                                    out=pad_c[:, c, 8 * g:8 * g + 8, 15:79],
                                    in0=psum[pv][:].rearrange("p (r w) -> p r w", w=64),
                                    in1=sg_c[k2][:].rearrange("p (r w) -> p r w", w=64), op=ALU.mult),
                                    reads=[PB[pv], SG_c[k2]], writes=[PAD_c[c]])
                        for c in range(4):
                            s_ = wcnt_c % 4
                            wcnt_c += 1
                            col0 = 1552 + c * 128
                            P.dma("pool", lambda s_=s_, col0=col0: nc.gpsimd.dma_start(
                                out=wsl_c[s_][:, 0], in_=win_d[:, col0:col0 + 128].rearrange("(k p) f -> p k f", p=128)),
                                writes=[WS_c[s_]])
                            for li in range(2):
                                g = groups_c[li]
                                ls = slice(li * 512, (li + 1) * 512)
                                pv = 4 + 2 * (it_c % 2)
                                it_c += 1
                                for kc in range(DC):
                                    P.op("pe", lambda s_=s_, kc=kc, ls=ls, pv=pv: nc.tensor.matmul(
                                        psum[pv][:], wsl_c[s_][:, 0, kc, :], hT_c[:, kc, ls], start=(kc == 0),
                                        stop=(kc == DC - 1)), reads=[WS_c[s_], HB_c[li]], writes=[PB[pv]])
                                sq, SQ = nsq()
                                P.op("act", lambda pv=pv, sq=sq: nc.scalar.activation(out=sq[:], in_=psum[pv][:],
                                                                                     func=AF.Silu),
                                     reads=[PB[pv]], writes=[SQ])
                                otv = oT[:, 4 * g:4 * g + 4, c, :]
                                P.op("pool", lambda otv=otv, sq=sq: nc.gpsimd.tensor_tensor(
                                    out=otv, in0=otv, in1=sq[:].rearrange("p (t k) -> p t k", k=128), op=ALU.mult),
                                    reads=[SQ] + [OT[4 * g + q] for q in range(4)],
                                    writes=[OT[4 * g + q] for q in range(4)])
                    P.barrier()
                    pn_extra[:] = []
                    esc1.close()
                    dg_c = [sbc_("dg_c%d" % k, [128, 31, 128], BF16) for k in range(2)]
                    DG_c = [Buf("dg_c%d" % k) for k in range(2)]
                    cvf_c = sbc_("cvf_c", [128, 4, T], F32)
                    CVF_c = [[Buf("cvf_c%d_%d" % (c, g)) for g in range(NG)] for c in range(4)]
                    lkb_c = link32[:, 1:32].unsqueeze(2).broadcast_to([128, 31, 15])
                    for c in range(4):
                        P.op("pool", lambda c=c: nc.gpsimd.tensor_tensor(
                            out=pad_c[:, c, 1:32, 0:15], in0=pad_c[:, c, 0:31, 64:79], in1=lkb_c, op=ALU.mult),
                            reads=[PAD_c[c], SML], writes=[PAD_c[c]])
                        P.op("pool", lambda c=c: nc.gpsimd.tensor_tensor(
                            out=pad_c[:, c, 0:31, 79:94], in0=pad_c[:, c, 1:32, 15:30], in1=lkb_c, op=ALU.mult),
                            reads=[PAD_c[c], SML], writes=[PAD_c[c]])
                    wtab_c = pT[:, PC_CVW:PC_CVW + 124].rearrange("p (t c) -> p t c", c=4)
                    M512_c = cstf[:, CF_M512, :]
                    it_c = 0
                    for c in range(4):
                        k2 = c % 2
                        P.op("pool", lambda c=c, k2=k2: nc.gpsimd.tensor_tensor(
                            out=dg_c[k2][:], in0=IDB.unsqueeze(1).broadcast_to([128, 31, 128]),
                            in1=wtab_c[:, :, c].unsqueeze(2).broadcast_to([128, 31, 128]), op=ALU.mult),
                            reads=[CST, PT], writes=[DG_c[k2]])
                        for g in range(NG):
                            gs = slice(g * 512, (g + 1) * 512)
                            pb = 4 + it_c % 2
                            it_c += 1
                            for tau in range(31):
                                P.op("pe", lambda c=c, g=g, tau=tau, k2=k2, pb=pb: nc.tensor.matmul(
                                    psum[pb][:], dg_c[k2][:, tau, :], pad_c[:, c, 8 * g:8 * g + 8, tau:tau + 64],
                                    start=(tau == 0), stop=(tau == 30)), reads=[DG_c[k2], PAD_c[c]], writes=[PB[pb]])
                            P.op("act", lambda c=c, gs=gs, pb=pb: nc.scalar.activation(
                                out=cvf_c[:, c, gs], in_=psum[pb][:], func=AF.Identity,
                                bias=pT[:, PC_CVB + c:PC_CVB + c + 1], scale=1.0), reads=[PB[pb], PT], writes=[CVF_c[c][g]])
                    pend_c = []
                    for g in range(NG):
                        gs = slice(g * 512, (g + 1) * 512)
                        for c in range(4):
                            k2 = c % 2
                            P.op("act", lambda c=c, gs=gs, k2=k2: nc.scalar.activation(out=sg_c[k2][:], in_=cvf_c[:, c, gs],
                                                                                       func=AF.Square),
                                 reads=[CVF_c[c][g]], writes=[SG_c[k2]])

                            def stats(c=c, k2=k2, g=g, gs=gs):
                                P.op("pe", lambda: nc.tensor.matmul(psum[6][:], M512_c, cvf_c[:, c, gs], start=(c == 0),
                                                                    stop=(c == 3)), reads=[CSTF, CVF_c[c][g]], writes=[PB[6]])
                                P.op("pe", lambda: nc.tensor.matmul(psum[7][:], M512_c, sg_c[k2][:], start=(c == 0),
                                                                    stop=(c == 3)), reads=[CSTF, SG_c[k2]], writes=[PB[7]])
                            pend_c.append(stats)
                            if len(pend_c) > 1:
                                pend_c.pop(0)()
                        while pend_c:
                            pend_c.pop(0)()
                        P.op("act", lambda: nc.scalar.activation(out=tmpf[0][:], in_=psum[6][:], func=AF.Square),
                             reads=[PB[6]], writes=[TF[0]])
                        P.op("dve", lambda: nc.vector.tensor_tensor(out=rtmp[:], in0=psum[7][:], in1=tmpf[0][:],
                                                                    op=ALU.subtract), reads=[PB[7], TF[0]], writes=[RT])
                        P.op("act", lambda: nc.scalar.activation(out=rtmp[:], in_=rtmp[:], func=AF.Ln, bias=epsT[:, 0:1],
                                                                 scale=1.0), reads=[RT, EPST], writes=[RT])
                        P.op("act", lambda: nc.scalar.activation(out=rstd[:], in_=rtmp[:], func=AF.Exp, scale=-0.5),
                             reads=[RT], writes=[RS])
                        P.op("act", lambda: nc.scalar.copy(out=tmpf[1][:], in_=psum[6][:]), reads=[PB[6]], writes=[TF[1]])
                        for c in range(4):
                            k2 = c % 2
                            P.op("dve", lambda c=c, gs=gs, k2=k2: nc.vector.tensor_tensor(
                                out=sg_c[k2][:], in0=cvf_c[:, c, gs], in1=tmpf[1][:], op=ALU.subtract),
                                reads=[CVF_c[c][g], TF[1]], writes=[SG_c[k2]])
                            P.op("dve", lambda k2=k2: nc.vector.tensor_tensor(out=sg_c[k2][:], in0=sg_c[k2][:], in1=rstd[:],
                                                                              op=ALU.mult),
                                 reads=[SG_c[k2], RS], writes=[SG_c[k2]])
                            P.op("act", lambda c=c, gs=gs, k2=k2: nc.scalar.activation(
                                out=cvT[:, c, gs], in_=sg_c[k2][:], func=AF.Silu, bias=pT[:, PC_LNB + c:PC_LNB + c + 1],
                                scale=pT[:, PC_LNG + c:PC_LNG + c + 1]), reads=[SG_c[k2], PT], writes=[CV[c][g]])
                    P.barrier()
                with ExitStack() as esd:
                    sbd = lambda name, shape, dty: esd.enter_context(nc.sbuf_tensor("d_" + name, shape, dty))
                    ybuf_d2 = [sbd("ybuf_d%d" % h_, [128, DC, 1024], F32) for h_ in range(2)]
                    YB_d2 = [[[Buf("my%d_%d_%d" % (h_, m, g)) for g in range(2)] for m in range(DC)] for h_ in range(2)]
                    rsy_d = [sbd("rsy_d%d" % k, [128, 512], F32) for k in range(2)]
                    RSY_d = [Buf("rsy_d%d" % k) for k in range(2)]
                    defer_d = []
                    wsl_d = [sbd("w%d" % k, [128, DC, 128], BF16) for k in range(3)]
                    WS_d = [Buf("dw%d" % k) for k in range(3)]
                    wcnt_d = 0
                    pend_d = []
                    for half in range(2):
                        groups_d = [2 * half, 2 * half + 1]
                        ybuf_d, YB_d = ybuf_d2[half], YB_d2[half]
                        for m in range(DC):
                            s_ = wcnt_d % 3
                            wcnt_d += 1
                            P.dma("pool", lambda s_=s_, m=m: nc.gpsimd.dma_start(
                                out=wsl_d[s_][:], in_=wout_d[:, m * 128:(m + 1) * 128].rearrange("(k p) f -> p k f", p=128)),
                                writes=[WS_d[s_]])
                            for li in range(2):
                                g = groups_d[li]
                                gs = slice(g * 512, (g + 1) * 512)
                                ls = slice(li * 512, (li + 1) * 512)
                                py = 4 + (m * 2 + li) % 2
                                if defer_d:
                                    defer_d.pop(0)()
                                for kc in range(DC):
                                    if kc < 4:
                                        rhs = oT[:, 4 * g:4 * g + 4, kc, :]
                                        rd = [OT[4 * g + q] for q in range(4)]
                                    else:
                                        rhs = cvT[:, kc - 4, gs]
                                        rd = [CV[kc - 4][g]]
                                    P.op("pe", lambda s_=s_, kc=kc, rhs=rhs, py=py: nc.tensor.matmul(
                                        psum[py][:], wsl_d[s_][:, kc, :], rhs, start=(kc == 0), stop=(kc == DC - 1)),
                                        reads=[WS_d[s_]] + rd, writes=[PB[py]])
                                P.op("act", lambda m=m, ls=ls, py=py, ybuf_d=ybuf_d: nc.scalar.copy(
                                    out=ybuf_d[:, m, ls], in_=psum[py][:]), reads=[PB[py]], writes=[YB_d[m][li]])
                                sq, SQ = nsq()
                                P.op("act", lambda py=py, sq=sq: nc.scalar.activation(out=sq[:], in_=psum[py][:],
                                                                                     func=AF.Square),
                                     reads=[PB[py]], writes=[SQ])
                                def stat(m=m, li=li, sq=sq, SQ=SQ):
                                    P.op("pe", lambda: nc.tensor.matmul(
                                        psum[6 + li][:], MEANB, sq[:], start=(m == 0), stop=(m == DC - 1)),
                                        reads=[SQ, CST], writes=[PB[6 + li]])
                                pend_d.append(stat)
                                if len(pend_d) > 1:
                                    pend_d.pop(0)()
                        while pend_d:
                            pend_d.pop(0)()
                        if half == 1:
                            while defer_d:
                                defer_d.pop(0)()
                            postnorm_residual(i, ybuf_d, YB_d, groups_d, [6, 7])
                        else:
                            for li in range(2):
                                P.op("act", lambda li=li: nc.scalar.activation(out=rtmp[:], in_=psum[6 + li][:], func=AF.Ln,
                                                                               bias=epsT[:, 0:1], scale=1.0),
                                     reads=[PB[6 + li], EPST], writes=[RT])
                                P.op("act", lambda li=li: nc.scalar.activation(out=rsy_d[li][:], in_=rtmp[:], func=AF.Exp,
                                                                               scale=-0.5), reads=[RT], writes=[RSY_d[li]])
                            for li in range(2):
                                g = groups_d[li]
                                for c in range(DC):
                                    def chunk_d(li=li, g=g, c=c, ybuf_d=ybuf_d, YB_d=YB_d):
                                        gs = slice(g * 512, (g + 1) * 512)
                                        ls = slice(li * 512, (li + 1) * 512)
                                        k = c % 2
                                        P.op("dve", lambda: nc.vector.scalar_tensor_tensor(
                                            out=tmpf[k][:], in0=ybuf_d[:, c, ls],
                                            scalar=ab[:, i * 24 + 16 + c:i * 24 + 17 + c], in1=rsy_d[li][:], op0=ALU.mult,
                                            op1=ALU.mult), reads=[YB_d[c][li], RSY_d[li], AB], writes=[TF[k]])
                                        P.op("dve", lambda: nc.vector.tensor_tensor(
                                            out=xT[:, c, gs], in0=tmpf[k][:], in1=xT[:, c, gs], op=ALU.add),
                                            reads=[TF[k], XB[c][g]], writes=[XB[c][g]])
                                    defer_d.append(chunk_d)
                    P.barrier()

        if "ffn1" in stages:
            ffn(0, f1i_d, f1o_d)
        if "mixer" in stages:
            mixer()
        if "ffn2" in stages:
            ffn(2, f2i_d, f2o_d, store_in_scope=True)
        else:
            ost_ = [sb("ost%d" % i, [128, D], F32) for i in range(2)]
            store_tiles(0, NT, [o_[:] for o_ in ost_], [Buf("ost%d" % i) for i in range(2)])
        if debug:
            P.dma("sp", lambda: nc.sync.dma_start(out=dbg_d[:, :, :], in_=xT[:]),
                  reads=[XB[c][g] for c in range(DC) for g in range(NG)])
        P.emit()
        build.stats = P.stats
    return nc


def make_pvec(cond, b_mod, norm_g, dn_conv_w, cv_dw_w, cv_dw_b, cv_ln_g, cv_ln_b, dn_norm_g):
    pv = np.zeros((384, 128), np.float32)
    pv[0:8] = cond.reshape(8, 128)
    pv[8:80] = b_mod.reshape(72, 128)
    pv[80:128] = norm_g.reshape(48, 128)
    pv[128:164] = dn_conv_w.reshape(36, 128)
    pv[164:288] = cv_dw_w.reshape(124, 128)
    pv[288:292] = cv_dw_b.reshape(4, 128)
    pv[292:296] = cv_ln_g.reshape(4, 128)
    pv[296:300] = cv_ln_b.reshape(4, 128)
    pv[300] = dn_norm_g.reshape(128)
    return pv


_NC_CACHE = {}


def kernel(x_prompt, x_sample, state_delta, c, c_ctx, w_mod, b_mod, norm_g, ffn1_w_in,
           ffn1_w_out, w_in, dn_conv_w, dn_a_log, dn_dt_bias, dn_norm_g, cv_dw_w, cv_dw_b,
           cv_ln_g, cv_ln_b, w_out, ffn2_w_in, ffn2_w_out, _debug=None,
           _stages=("ffn1", "mixer", "ffn2")):
    f = lambda a: np.ascontiguousarray(np.asarray(a, dtype=np.float32))
    x_prompt, x_sample = f(x_prompt), f(x_sample)
    key = (_debug, _stages)
    if key not in _NC_CACHE:
        _NC_CACHE[key] = build(debug=_debug, stages=_stages)
    nc = _NC_CACHE[key]
    cf, cb = make_consts()
    rep = lambda v: np.ascontiguousarray(np.broadcast_to(np.asarray(v, np.float32).reshape(1, -1), (128, np.size(v))))
    gconst = rep(np.concatenate([f(dn_a_log)[0].reshape(8), f(dn_dt_bias)[0].reshape(8)]))
    r8 = np.arange(8)
    r32 = np.arange(32)
    link8_p = (r8 % 4 != 0).astype(np.float32)
    link32_p = (r32 % 4 != 0).astype(np.float32)
    sd = f(state_delta)
    in_maps = []
    for core in range(8):
        if core < 4 or core >= 6:
            cp = core if core < 4 else 0
            xc = x_prompt[8 * cp:8 * cp + 8].reshape(T, D)
            cond = f(c_ctx)
            link8, link32v, carry = link8_p, link32_p, 0.0
            s0 = np.zeros((2, 128, 512), np.float32)
        else:
            b = core - 4
            xc = x_sample[b]
            cond = f(c)[b]
            link8, link32v, carry = np.zeros(8, np.float32), np.zeros(32, np.float32), 1.0
            s0 = np.ascontiguousarray(sd[b, 0].transpose(0, 2, 1, 3)).reshape(2, 128, 512)
        in_maps.append({
            "x": np.ascontiguousarray(xc),
            "pvec": make_pvec(cond, f(b_mod)[0], f(norm_g)[0], f(dn_conv_w)[0], f(cv_dw_w)[0], f(cv_dw_b)[0],
                              f(cv_ln_g)[0], f(cv_ln_b)[0], f(dn_norm_g)[0]),
            "w_mod": f(w_mod)[0], "ffn1_w_in": f(ffn1_w_in)[0], "ffn1_w_out": f(ffn1_w_out)[0],
            "ffn2_w_in": f(ffn2_w_in)[0], "ffn2_w_out": f(ffn2_w_out)[0],
            "w_in": f(w_in)[0], "w_out": f(w_out)[0],
            "cstf": cf, "cstb": cb, "gconst": gconst, "nlink": rep(link8 - 1.0), "link32": rep(link32v),
            "carry": np.full((128, 1), carry, np.float32), "s0": s0,
        })
    res = run_bass_kernel_spmd(nc, in_maps, core_ids=list(range(8)))
    r = res.results
    y_p = np.concatenate([r[i]["y"].reshape(8, 256, D) for i in range(4)], axis=0)
    y_s = np.stack([r[4]["y"], r[5]["y"]], axis=0)
    ns = np.concatenate([r[i]["st"].reshape(8, 2, 128, 4, 128).transpose(0, 1, 3, 2, 4) for i in range(4)], axis=0)
    ns = np.ascontiguousarray(ns.reshape(32, 1, 2, 4, 128, 128))
    if _debug:
        return (y_p, y_s, ns), [r[i]["dbg"] for i in range(8)]
    return (y_p, y_s, ns)
```

```python
import numpy as np
import concourse.bass as bass
import concourse.mybir as mybir
from concourse.bass_utils import run_bass_kernel_spmd
from contextlib import ExitStack

F32 = mybir.dt.float32
BF16 = mybir.dt.bfloat16
AF = mybir.ActivationFunctionType
ALU = mybir.AluOpType
AX = mybir.AxisListType

D = 1024
DC = 8
T = 2048
NT = T // 128
NG = T // 512
DFF = 2816
FC = DFF // 128
EPS = 1e-6
IN_COLS = 3088
NMASK = 16


class Buf:
    __slots__ = ("name", "w", "rs", "rdma", "frozen")

    def __init__(self, name, frozen=False):
        self.name = name
        self.w = None
        self.rs = {}
        self.rdma = []
        self.frozen = frozen


class Op:
    __slots__ = ("eng", "fn", "deps", "is_dma", "sig", "idx", "sem", "semval", "n")


class Prog:
    def __init__(self, nc, es, n_dma_sems=20):
        self.nc = nc
        self.ops = []
        self.engs = {"pe": nc.tensor, "act": nc.scalar, "dve": nc.vector,
                     "pool": nc.gpsimd, "sp": nc.sync}
        self.esem = {e: es.enter_context(nc.semaphore("es_" + e)) for e in self.engs}
        self.dsem = [es.enter_context(nc.semaphore("ds%d" % i)) for i in range(n_dma_sems)]
        self.dcnt = [0] * n_dma_sems
        self.dlast = [None] * n_dma_sems
        self.dnext = 0
        self.last = {e: None for e in self.engs}

    def _mk(self, eng, fn, reads, writes, is_dma):
        o = Op()
        o.eng, o.fn, o.is_dma, o.sig, o.idx = eng, fn, is_dma, False, 0
        o.sem = None
        o.semval = 0
        o.n = len(self.ops)
        deps = {}

        def add(d, raw):
            if d is None or d is o:
                return
            if (not d.is_dma) and (not is_dma) and d.eng == eng and not raw:
                return
            deps[d.n] = d

        for r in reads:
            add(r.w, True)
        for w in writes:
            add(w.w, False)
            for d in w.rs.values():
                add(d, False)
            for d in w.rdma:
                add(d, False)
        if is_dma:
            k = self.dnext
            self.dnext = (self.dnext + 1) % len(self.dsem)
            if self.dlast[k] is not None:
                deps[self.dlast[k].n] = self.dlast[k]
            self.dcnt[k] += 1
            o.sem = self.dsem[k]
            o.semval = 16 * self.dcnt[k]
            self.dlast[k] = o
        o.deps = list(deps.values())
        for d in o.deps:
            d.sig = True
        for w in writes:
            w.w = o
            w.rs = {}
            w.rdma = []
        for r in reads:
            if r.frozen:
                continue
            if is_dma:
                r.rdma.append(o)
            else:
                r.rs[eng] = o
        self.ops.append(o)
        self.last[eng] = o
        return o

    def op(self, eng, fn, reads=(), writes=()):
        return self._mk(eng, fn, reads, writes, False)

    def dma(self, q, fn, reads=(), writes=()):
        return self._mk(q, fn, reads, writes, True)

    def barrier(self):
        lasts = [o for o in self.last.values() if o is not None]
        dl = [o for o in self.dlast if o is not None]
        for e in self.engs:
            o = Op()
            o.eng, o.fn, o.is_dma, o.sig, o.idx = e, None, False, False, 0
            o.sem, o.semval, o.n = None, 0, len(self.ops)
            o.deps = [d for d in lasts + dl]
            for d in o.deps:
                d.sig = True
            self.ops.append(o)

    def emit(self):
        cnt = {e: 0 for e in self.engs}
        for o in self.ops:
            if (not o.is_dma) and o.sig and o.fn is not None:
                cnt[o.eng] += 1
                o.idx = cnt[o.eng]
        waited = {e: {} for e in self.engs}
        nwait = 0
        for o in self.ops:
            E = self.engs[o.eng]
            wt = waited[o.eng]
            for d in o.deps:
                if d.is_dma:
                    sem, val, key = d.sem, d.semval, id(d.sem)
                else:
                    if d.fn is None:
                        continue
                    sem, val, key = self.esem[d.eng], d.idx, d.eng
                if wt.get(key, 0) < val:
                    E.wait_ge(sem, val)
                    wt[key] = val
                    nwait += 1
            if o.fn is None:
                continue
            ins = o.fn()
            if o.is_dma:
                ins.then_inc(o.sem, 16)
            elif o.sig:
                ins.then_inc(self.esem[o.eng], 1)
        sp = self.engs["sp"]
        for k, d in enumerate(self.dlast):
            if d is not None:
                sp.wait_ge(d.sem, d.semval)
        self.stats = (len(self.ops), nwait, dict(cnt))


CF_ID, CF_A1, CF_A2, CF_A3, CF_A4, CF_ONE, CF_M512 = range(7)
NCF = 7
CB_ID, CB_MEAN, CB_ONE, CB_NBD16, CB_NOFF16, CB_NOFF32, CB_NOFF64 = range(7)
NCB = 7
NEGBIG = -30000.0
DN_WARM = 0


def make_consts():
    k = np.arange(128)[:, None]
    x = np.arange(128)[None, :]
    cf = np.zeros((128, NCF, 128), np.float32)
    cf[:, CF_ID] = (k == x)
    cf[:, CF_A1] = (k <= x)
    cf[:, CF_A2] = (k > x)
    cf[:, CF_A3] = (k >= x)
    cf[:, CF_A4] = (k < x)
    cf[:, CF_ONE] = 1.0
    cf[:, CF_M512] = 1.0 / 512.0
    cb = np.zeros((128, NCB, 128), np.float32)
    cb[:, CB_ID] = (k == x)
    cb[:, CB_MEAN] = 1.0 / 1024.0
    cb[:, CB_ONE] = 1.0
    cb[:, CB_NBD16] = -1.0 * (k // 16 == x // 16)
    for idx, b in ((CB_NOFF16, 16), (CB_NOFF32, 32), (CB_NOFF64, 64)):
        cb[:, idx] = -1.0 * ((k // (2 * b) == x // (2 * b)) & (k // b != x // b))
    return cf, cb


def gate_cums(P, nc, ntl, gB, EX, GATE, GSRC, cstf, CSTF, psum, PB, pbank):
    plan = [(CF_A1, 0), (CF_A2, 0), (CF_ONE, 0), (CF_A3, 4), (CF_A4, 4), (CF_ONE, 4)]
    for t in range(ntl):
        for i, (m, c0) in enumerate(plan):
            col = t * 24 + i * 4
            P.op("pe", lambda t=t, m=m, c0=c0, col=col: nc.tensor.matmul(
                psum[pbank][:, col:col + 4], cstf[:, m, :], gB[:, t, c0:c0 + 4], start=True, stop=True),
                reads=[CSTF, GSRC], writes=[PB[pbank]])
    P.op("act", lambda: nc.scalar.activation(out=EX[:].rearrange("p t c -> p (t c)"), in_=psum[pbank][:, 0:ntl * 24],
                                             func=AF.Exp), reads=[PB[pbank]], writes=[GATE])


def deltanet(P, nc, es, ntl, qkT, QKB, vT, VB, gB, bB, lnB, EX, GATE, cstf, cstb, CSTF, CST,
             psum, PB, S0_d, carry, CARRY, st_d, tmps, TMPS, finish_tile, sq3=None):
    sb = lambda name, shape, dty: es.enter_context(nc.sbuf_tensor("dn_" + name, shape, dty))
    HP = 2
    IDB = cstb[:, CB_ID, :]
    IDF = cstf[:, CF_ID, :]

    class Set:
        pass

    sets = []
    for d in range(2):
        S = Set()
        S.pairs = []
        for hp_ in range(2):
            Q = Set()
            pt = lambda name: sb("%s%d_%d" % (name, d, hp_), [128, 2, HP, 128], BF16)
            Q.NCt, Q.NCn, Q.P2, Q.P4, Q.RA, Q.RB = pt("NCt"), pt("NCn"), pt("P2"), pt("P4"), pt("RA"), pt("RB")
            Q.rhsE = Q.P4[:].rearrange("p v h d -> p (v h d)").bitcast(F32).rearrange("p (h d) -> p h d", h=HP)
            Q.EMi = Q.RA[:, 0]
            Q.Ers = Q.RA[:, 1]
            Q.ErC = sb("ErC%d_%d" % (d, hp_), [128, HP, 128], BF16)
            Q.B = {n: Buf("dn%d_%d_%s" % (d, hp_, n)) for n in ["NCt", "NCn", "P2", "P4", "RA", "RB", "ErC"]}
            Q.B["rhsE"] = Q.B["P4"]
            Q.B["EMi"] = Q.B["RA"]
            Q.B["Ers"] = Q.B["RA"]
            Q.bank = 2 * d + hp_
            S.pairs.append(Q)
        if d == 0 and sq3 is not None:
            S.ktok, S.vtok, S.kg = [q[:].rearrange("p (h d) -> p h d", h=4) for q in sq3]
        else:
            S.ktok = sb("ktok%d" % d, [128, 4, 128], BF16)[:]
            S.vtok = sb("vtok%d" % d, [128, 4, 128], BF16)[:]
            S.kg = sb("kg%d" % d, [128, 4, 128], BF16)[:]
        S.Yt = sb("Yt%d" % d, [128, 4, 128], BF16)
        S.kdec = [sb("kdec%d_%d" % (d, q), [128, 4, 128], BF16) for q in range(2)]
        S.QKt = [sb("QKt%d_%d" % (d, q), [128, 4, 128], BF16) for q in range(2)]
        S.Wt = [sb("Wt%d_%d" % (d, q), [128, 4, 128], BF16) for q in range(2)]
        S.bu = [sb("bu%d_%d" % (d, q), [128, 4, 128], BF16) for q in range(2)]
        S.vnew = sb("vnew%d" % d, [128, 4, 128], BF16)
        S.Sm = sb("Sm%d" % d, [128, 4, 128], F32)
        S.Sb = sb("Sb%d" % d, [128, 4, 128], BF16)
        S.B = {n: Buf("dn%d_%s" % (d, n)) for n in ["ktok", "vtok", "kg", "Yt", "vnew", "Sm", "Sb"]}
        for n in ("kdec", "QKt", "Wt", "bu"):
            for q in range(2):
                S.B[n + str(q)] = Buf("dn%d_%s%d" % (d, n, q))
        sets.append(S)

    def bc_h(ap2d, nh):
        return ap2d.unsqueeze(1).broadcast_to([128, nh, 128])

    def bc_c(ap2d):
        nh = ap2d.shape[1]
        return ap2d.unsqueeze(2).broadcast_to([128, nh, 128])

    def ps3(b, nh=4):
        return psum[b][:, 0:nh * 128].rearrange("p (h d) -> p h d", h=nh)

    def ps4(b):
        return psum[b][:].rearrange("p (v h d) -> p v h d", v=2, h=HP)

    MASKS = {0: (CF_A2, CF_A1, CF_A1, CF_A4),
             1: (CF_A4, CF_A3, CF_A3, CF_A2)}

    def mm(b, col, lhsT, rhs, start, stop, reads):
        P.op("pe", lambda: nc.tensor.matmul(psum[b][:, col * 128:(col + 1) * 128], lhsT, rhs, start=start, stop=stop),
             reads=reads, writes=[PB[b]])

    def inst_pre(t, d, q):
        S = sets[d]
        B = S.B
        ts = slice(t * 128, (t + 1) * 128)
        ex0 = d * 12
        b = S.pairs[0].bank
        for h in range(4):
            mm(b, h, qkT[:, 4 + h, ts], IDB, True, True, [QKB, CST])
        P.op("act", lambda: nc.scalar.copy(out=S.ktok, in_=ps3(b)), reads=[PB[b]], writes=[B["ktok"]])
        P.op("pool", lambda: nc.gpsimd.tensor_tensor(out=S.kg, in0=S.ktok, in1=bc_c(EX[:, t, ex0:ex0 + 4]),
                                                     op=ALU.mult), reads=[B["ktok"], GATE], writes=[B["kg"]])
        P.op("pool", lambda: nc.gpsimd.tensor_tensor(out=S.kdec[q][:], in0=S.ktok, in1=bc_c(EX[:, t, ex0 + 4:ex0 + 8]),
                                                     op=ALU.mult), reads=[B["ktok"], GATE], writes=[B["kdec%d" % q]])
        yield
        b2 = S.pairs[1].bank
        for h in range(4):
            mm(b2, h, vT[:, h, ts], IDB, True, True, [VB, CST])
        P.op("act", lambda: nc.scalar.copy(out=S.vtok, in_=ps3(b2)), reads=[PB[b2]], writes=[B["vtok"]])
        yield

    def inst_post(t, d, q):
        S = sets[d]
        B = S.B
        gc4 = slice(d * 4, d * 4 + 4)
        b = S.pairs[0].bank
        for h in range(4):
            mm(b, h, S.Yt[:, h, :], S.vtok[:, h, :], True, True, [B["Yt"], B["vtok"]])
        P.op("dve", lambda: nc.vector.tensor_tensor(out=S.bu[q][:], in0=ps3(b), in1=bc_c(bB[:, t, gc4]), op=ALU.mult),
             reads=[PB[b], GATE], writes=[B["bu%d" % q]])
        yield
        b2 = S.pairs[1].bank
        for h in range(4):
            mm(b2, h, S.kg[:, h, :], S.Yt[:, h, :], True, True, [B["kg"], B["Yt"]])
        P.op("act", lambda: nc.scalar.copy(out=S.Wt[q][:], in_=ps3(b2)), reads=[PB[b2]], writes=[B["Wt%d" % q]])
        yield

    def pair(t, d, hp, q):
        SI = sets[d]
        S = SI.pairs[hp]
        B = dict(S.B)
        B["QKt"] = SI.B["QKt%d" % q]
        B["Yt"] = SI.B["Yt"]
        ts = slice(t * 128, (t + 1) * 128)
        m_el, m_er, m_incl, m_strict = MASKS[d]
        hs = slice(hp * HP, (hp + 1) * HP)
        gcol = d * 4 + hp * HP

        def nb():
            return S.bank

        P.op("pool", lambda: nc.gpsimd.tensor_tensor(
            out=S.rhsE, in0=bc_h(cstf[:, m_er, :], HP), in1=bc_c(gB[:, t, gcol:gcol + HP]), op=ALU.mult),
            reads=[CSTF, GATE], writes=[B["rhsE"]])
        b = nb()
        rE = S.rhsE.rearrange("p h d -> p (h d)")
        P.op("pe", lambda: nc.tensor.matmul(psum[b][:, 0:256], cstf[:, m_el, :], rE, start=True, stop=True),
             reads=[CSTF, B["rhsE"]], writes=[PB[b]])
        P.op("act", lambda: nc.scalar.activation(out=S.EMi, in_=ps4(b)[:, 0], func=AF.Exp),
             reads=[PB[b]], writes=[B["EMi"]])
        for h in range(HP):
            P.op("act", lambda h=h: nc.scalar.activation(out=S.Ers[:, h, :], in_=ps4(b)[:, 0, h, :], func=AF.Exp,
                                                         bias=lnB[:, t, gcol + h:gcol + h + 1], scale=1.0),
                 reads=[PB[b], GATE], writes=[B["Ers"]])
        P.op("dve", lambda: nc.vector.tensor_tensor(out=S.EMi, in0=S.EMi, in1=bc_h(cstf[:, m_incl, :], HP),
                                                    op=ALU.mult), reads=[B["EMi"], CSTF], writes=[B["EMi"]])
        P.op("dve", lambda: nc.vector.tensor_tensor(out=S.Ers, in0=S.Ers, in1=bc_h(cstf[:, m_strict, :], HP),
                                                    op=ALU.mult), reads=[B["Ers"], CSTF], writes=[B["Ers"]])
        P.op("pool", lambda: nc.gpsimd.tensor_tensor(out=S.ErC[:], in0=S.Ers, in1=bc_h(cstb[:, CB_NBD16, :], HP),
                                                     op=ALU.mult), reads=[B["Ers"], CST], writes=[B["ErC"]])
        yield
        b1 = nb()
        for h in range(HP):
            mm(b1, h, qkT[:, 4 + hp * HP + h, ts], qkT[:, 4 + hp * HP + h, ts], True, True, [QKB])
            mm(b1, HP + h, qkT[:, 4 + hp * HP + h, ts], qkT[:, hp * HP + h, ts], True, True, [QKB])
        P.op("dve", lambda: nc.vector.tensor_tensor(out=S.NCt[:, 0], in0=ps4(b1)[:, 0], in1=S.Ers, op=ALU.mult),
             reads=[PB[b1], B["Ers"]], writes=[B["NCt"]])
        P.op("dve", lambda: nc.vector.tensor_tensor(out=S.NCt[:, 1], in0=ps4(b1)[:, 0], in1=S.ErC[:], op=ALU.mult),
             reads=[PB[b1], B["ErC"]], writes=[B["NCt"]])
        P.op("dve", lambda: nc.vector.tensor_tensor(out=SI.QKt[q][:, hs, :], in0=ps4(b1)[:, 1], in1=S.EMi, op=ALU.mult),
             reads=[PB[b1], B["EMi"]], writes=[B["QKt"]])
        P.op("pool", lambda: nc.gpsimd.tensor_tensor(out=S.RB[:, 0], in0=S.NCt[:, 1], in1=bc_h(IDB, HP), op=ALU.add),
             reads=[B["NCt"], CST], writes=[B["RB"]])
        yield
        b = nb()
        for v in range(2):
            for h in range(HP):
                mm(b, v * HP + h, S.NCt[:, v, h, :], IDB, True, True, [B["NCt"], CST])
        P.op("act", lambda b=b: nc.scalar.copy(out=S.NCn[:], in_=ps4(b)), reads=[PB[b]], writes=[B["NCn"]])
        P.op("pool", lambda: nc.gpsimd.tensor_tensor(out=S.RB[:, 1], in0=S.NCn[:, 1], in1=bc_h(IDB, HP), op=ALU.add),
             reads=[B["NCn"], CST], writes=[B["RB"]])
        yield
        Ct, Cn = S.NCt[:, 1], S.NCn[:, 1]
        Ntt, Nnn = S.NCt[:, 0], S.NCn[:, 0]

        def level(dst, dname, terms_t, terms_n, reads, mask=None, only_t=False, out_ap=None, add=None):
            b = nb()
            for _ in range(DN_WARM):
                P.op("pe", lambda: nc.tensor.matmul(psum[b][:], IDB, qkT[:, 0, 0:512], start=True, stop=True),
                     reads=[CST, QKB], writes=[PB[b]])
            for v, terms in ((0, terms_t), (1, terms_n)):
                if only_t and v == 1:
                    continue
                for h in range(HP):
                    n = len(terms)
                    for k, (l, r) in enumerate(terms):
                        lh = l if l is IDB else l[:, h, :]
                        rh = r if r is IDB else r[:, h, :]
                        P.op("pe", lambda lh=lh, rh=rh, k=k, n=n, v=v, h=h: nc.tensor.matmul(
                            psum[b][:, (v * HP + h) * 128:(v * HP + h + 1) * 128], lh, rh, start=(k == 0),
                            stop=(k == n - 1)), reads=reads + [CST], writes=[PB[b]])
            src = ps4(b)[:, 0] if only_t else ps4(b)
            o = out_ap if out_ap is not None else (dst[:, 0] if only_t else dst[:])
            if add is not None:
                P.op("dve", lambda: nc.vector.tensor_tensor(out=o, in0=src, in1=add[:], op=ALU.add),
                     reads=[PB[b]] + reads, writes=[B[dname]])
            elif mask is None:
                P.op("act", lambda: nc.scalar.copy(out=o, in_=src), reads=[PB[b]], writes=[B[dname]])
            else:
                mk = cstb[:, mask, :]
                mb = mk.unsqueeze(1).broadcast_to([128, HP, 128]) if only_t else \
                    mk.unsqueeze(1).unsqueeze(1).broadcast_to([128, 2, HP, 128])
                P.op("dve", lambda: nc.vector.tensor_tensor(out=o, in0=src, in1=mb, op=ALU.mult),
                     reads=[PB[b], CST], writes=[B[dname]])

        P2t, P2n = S.P2[:, 0], S.P2[:, 1]
        P4t, P4n = S.P4[:, 0], S.P4[:, 1]
        RAt, RAn = S.RA[:, 0], S.RA[:, 1]
        RBt, RBn = S.RB[:, 0], S.RB[:, 1]
        rd = [B["NCt"], B["NCn"], B["P2"], B["P4"], B["RA"], B["RB"]]
        level(S.P2, "P2", [(Cn, Ct)], [(Ct, Cn)], rd)
        yield
        level(S.RA, "RA", [(IDB, RBt), (RBn, P2t)], [(IDB, RBn), (P2t, RBn)], rd)
        yield
        level(S.P4, "P4", [(P2n, P2t)], [(P2t, P2n)], rd)
        yield
        level(S.RB, "RB", [(IDB, RAt), (RAn, P4t)], [(IDB, RAn), (P4t, RAn)], rd)
        yield
        level(S.P2, "P2", [(P4n, P4t)], [], rd, only_t=True)
        yield
        level(S.RA, "RA", [(IDB, RBt), (RBn, P2t)], [(IDB, RBn), (P2t, RBn)], rd)
        yield
        level(S.P4, "P4", [(Nnn, RAt)], [(Ntt, RAn)], rd, mask=CB_NOFF16)
        yield
        level(S.RB, "RB", [(RAn, P4t)], [(RAt, P4n)], rd, add=S.RA)
        yield
        level(S.P4, "P4", [(Nnn, RBt)], [(Ntt, RBn)], rd, mask=CB_NOFF32)
        yield
        level(S.RA, "RA", [(RBn, P4t)], [(RBt, P4n)], rd, add=S.RB)
        yield
        level(S.P4, "P4", [(Nnn, RAt)], [], rd, mask=CB_NOFF64, only_t=True)
        yield
        level(None, "Yt", [(RAn, P4t)], [], rd, only_t=True, out_ap=SI.Yt[:, hs, :], add=RAt)
        yield

    def scan_step(t, d, q, first, slot_start, slot_end):
        S = sets[d]
        B = S.B
        ts = slice(t * 128, (t + 1) * 128)
        gc4 = slice(d * 4, d * 4 + 4)
        ex0 = d * 12
        pa, pq, po, pS = 4, 5, 6, 7
        v3 = lambda ap: ap.rearrange("p (h d) -> p h d", h=4)
        Sm2 = S.Sm[:].rearrange("p h d -> p (h d)")
        if first:
            P.dma("sp", lambda: nc.sync.dma_start(out=Sm2, in_=S0_d[d]), writes=[B["Sm"]])
            P.op("act", lambda: nc.scalar.copy(out=S.Sb[:], in_=S.Sm[:]), reads=[B["Sm"]], writes=[B["Sb"]])
        elif slot_start:
            P.op("dve", lambda: nc.vector.tensor_scalar(out=S.Sm[:], in0=S.Sm[:], scalar1=carry[:, 0:1], scalar2=None,
                                                        op0=ALU.mult), reads=[B["Sm"], CARRY], writes=[B["Sm"]])
            P.op("act", lambda: nc.scalar.copy(out=S.Sb[:], in_=S.Sm[:]), reads=[B["Sm"]], writes=[B["Sb"]])
        for h in range(4):
            mm(pa, h, S.Wt[q][:, h, :], S.Sb[:, h, :], True, True, [B["Wt%d" % q], B["Sb"]])
        for h in range(4):
            mm(pq, h, qkT[:, h, ts], S.Sb[:, h, :], True, True, [QKB, B["Sb"]])
        yield
        tA, TA = tmps[0], TMPS[0]
        P.op("dve", lambda: nc.vector.tensor_tensor(out=v3(tA[:]), in0=ps3(pa), in1=bc_c(bB[:, t, gc4]), op=ALU.mult),
             reads=[PB[pa], GATE], writes=[TA])
        P.op("pool", lambda: nc.gpsimd.tensor_tensor(out=S.vnew[:], in0=S.bu[q][:], in1=v3(tA[:]), op=ALU.subtract),
             reads=[B["bu%d" % q], TA], writes=[B["vnew"]])
        yield
        for h in range(4):
            mm(po, h, S.QKt[q][:, h, :], S.vnew[:, h, :], True, True, [B["QKt%d" % q], B["vnew"]])
        for h in range(4):
            mm(pS, h, S.kdec[q][:, h, :], S.vnew[:, h, :], True, True, [B["kdec%d" % q], B["vnew"]])
        yield
        tB, TB = tmps[1], TMPS[1]
        P.op("pool", lambda: nc.gpsimd.tensor_tensor(out=v3(tB[:]), in0=S.Sm[:], in1=bc_c(EX[:, t, ex0 + 8:ex0 + 12]),
                                                     op=ALU.mult), reads=[B["Sm"], GATE], writes=[TB])
        P.op("dve", lambda: nc.vector.tensor_tensor(out=S.Sm[:], in0=ps3(pS), in1=v3(tB[:]), op=ALU.add),
             reads=[PB[pS], TB], writes=[B["Sm"]])
        P.op("act", lambda: nc.scalar.copy(out=S.Sb[:], in_=S.Sm[:]), reads=[B["Sm"]], writes=[B["Sb"]])
        if slot_end:
            slot = t // 2
            P.dma("sp", lambda: nc.sync.dma_start(out=st_d[slot, d], in_=Sm2), reads=[B["Sm"]])
        yield
        finish_tile(t, d, pq, po, EX[:, t, ex0:ex0 + 4])
        yield

    def run(gens):
        gens = list(gens)
        while gens:
            for g in list(gens):
                try:
                    next(g)
                except StopIteration:
                    gens.remove(g)

    scan_lock = [None]

    def locked_scan(key, g):
        while scan_lock[0] is not None and scan_lock[0] != key:
            yield
        scan_lock[0] = key
        yield from g
        scan_lock[0] = None

    def rr(gens):
        gens = list(gens)
        while gens:
            for g in list(gens):
                try:
                    next(g)
                except StopIteration:
                    gens.remove(g)
                yield

    def dir_driver(d):
        prev_scan = None
        for s in range(ntl):
            t = s if d == 0 else ntl - 1 - s
            q = s % 2
            work = [pair(t, d, 0, q), pair(t, d, 1, q), inst_pre(t, d, q)]
            if prev_scan is not None:
                work.append(prev_scan)
            yield from rr(work)
            yield from rr([inst_post(t, d, q)])
            if d == 0:
                st, en = (t % 2 == 0), (t % 2 == 1)
            else:
                st, en = (t % 2 == 1), (t % 2 == 0)
            prev_scan = locked_scan((d, s), scan_step(t, d, q, s == 0, st, en))
        yield from prev_scan

    g0, g1 = dir_driver(0), dir_driver(1)
    for _ in range(20):
        next(g0)
    run([g0, g1])


def build(debug=None, stages=("ffn1", "mixer", "ffn2")):
    nc = bass.Bass("TRN2", target_bir_lowering=False)
    dt = nc.dram_tensor
    x_d = dt("x", [T, D], F32, kind="ExternalInput").ap()
    pv_d = dt("pvec", [384, 128], F32, kind="ExternalInput").ap()
    wmod_d = dt("w_mod", [D, 9 * D], F32, kind="ExternalInput").ap()
    f1i_d = dt("ffn1_w_in", [D, 2 * DFF], F32, kind="ExternalInput").ap()
    f1o_d = dt("ffn1_w_out", [DFF, D], F32, kind="ExternalInput").ap()
    f2i_d = dt("ffn2_w_in", [D, 2 * DFF], F32, kind="ExternalInput").ap()
    f2o_d = dt("ffn2_w_out", [DFF, D], F32, kind="ExternalInput").ap()
    cf_d = dt("cstf", [128, NCF, 128], F32, kind="ExternalInput").ap()
    cb_d = dt("cstb", [128, NCB, 128], F32, kind="ExternalInput").ap()
    win_d = dt("w_in", [D, IN_COLS], F32, kind="ExternalInput").ap()
    wout_d = dt("w_out", [D, D], F32, kind="ExternalInput").ap()
    gc_d = dt("gconst", [128, 16], F32, kind="ExternalInput").ap()
    nl_d = dt("nlink", [128, 8], F32, kind="ExternalInput").ap()
    lk_d = dt("link32", [128, 32], F32, kind="ExternalInput").ap()
    ca_d = dt("carry", [128, 1], F32, kind="ExternalInput").ap()
    s0_d = dt("s0", [2, 128, 512], F32, kind="ExternalInput").ap()
    y_d = dt("y", [T, D], F32, kind="ExternalOutput").ap()
    st_d = dt("st", [8, 2, 128, 512], F32, kind="ExternalOutput").ap()
    dbg_d = None
    if debug:
        dbg_d = dt("dbg", [128, 8, T], F32, kind="ExternalOutput").ap()

    es = ExitStack()
    with es:
        P = Prog(nc, es)
        sb = lambda name, shape, dty: es.enter_context(nc.sbuf_tensor(name, shape, dty))
        ps = lambda name: es.enter_context(nc.psum_tensor(name, [128, 512], F32))

        xT = sb("xT", [128, DC, T], F32)
        XB = [[Buf("x%d_%d" % (c, g)) for g in range(NG)] for c in range(DC)]
        cstf = sb("cstf_s", [128, NCF, 128], F32)
        IDFB = Buf("cstf", frozen=True)
        CSTF = IDFB
        cstb = sb("cstb_s", [128, NCB, 128], BF16)
        CST = Buf("cst", frozen=True)
        smalls = sb("smalls", [128, 64], F32)
        SML = Buf("smalls", frozen=True)
        gconst, nlink, link32, carry = smalls[:, 0:16], smalls[:, 16:24], smalls[:, 24:56], smalls[:, 56:57]
        pT = sb("pT", [128, 384], F32)
        PT = Buf("pT", frozen=True)
        sc = sb("silu_c", [128, DC], BF16)
        SC = Buf("silu_c", frozen=True)
        WMR = [Buf("wmr%d" % k) for k in range(3)]
        bgbuf = {}
        modT = sb("modT", [128, 72], F32)
        MOD = Buf("mod", frozen=True)
        ab = sb("ab", [128, 3 * 3 * DC], F32)
        AB = Buf("ab", frozen=True)
        psum = [ps("ps%d" % i) for i in range(8)]
        PB = [Buf("ps%d" % i) for i in range(8)]
        rstd = sb("rstd", [128, 512], F32)
        RS = Buf("rstd")
        sqs = [sb("sq%d" % i, [128, 512], BF16) for i in range(3)]
        SQS = [Buf("sq%d" % i) for i in range(3)]
        tmpf = [sb("tmpf%d" % i, [128, 512], F32) for i in range(2)]
        TF = [Buf("tmpf%d" % i) for i in range(2)]
        sqi = [0]

        def nsq():
            sqi[0] = (sqi[0] + 1) % 3
            return sqs[sqi[0]], SQS[sqi[0]]

        IDF = cstf[:, CF_ID, :]
        IDB = cstb[:, CB_ID, :]
        MEANB = cstb[:, CB_MEAN, :]
        ONEB = cstb[:, CB_ONE, :]

        PC_COND, PC_BMOD, PC_NG = 0, 8, 80
        PC_DNC, PC_CVW, PC_CVB, PC_LNG, PC_LNB, PC_DNG = 128, 164, 288, 292, 296, 300

        P.dma("sp", lambda: nc.sync.dma_start(out=cstf[:], in_=cf_d[:, :, :]), writes=[IDFB])
        P.dma("pool", lambda: nc.gpsimd.dma_start(out=cstb[:], in_=cb_d[:, :, :]), writes=[CST])
        P.dma("sp", lambda: nc.sync.dma_start(out=smalls[:, 0:16], in_=gc_d[:, :]), writes=[SML])
        P.dma("sp", lambda: nc.sync.dma_start(out=smalls[:, 16:24], in_=nl_d[:, :]), writes=[SML])
        P.dma("sp", lambda: nc.sync.dma_start(out=smalls[:, 24:56], in_=lk_d[:, :]), writes=[SML])
        P.dma("sp", lambda: nc.sync.dma_start(out=smalls[:, 56:57], in_=ca_d[:, :]), writes=[SML])

        with ExitStack() as es0:
            sb0 = lambda name, shape, dty: es0.enter_context(nc.sbuf_tensor(name, shape, dty))
            pst = sb0("pstage", [128, 3, 128], F32)
            PST = Buf("pstage")
            P.dma("sp", lambda: nc.sync.dma_start(out=pst[:], in_=pv_d.rearrange("(k p) f -> p k f", p=128)),
                  writes=[PST])
            for k in range(3):
                P.op("pe", lambda k=k: nc.tensor.transpose(out=psum[0][:, k * 128:(k + 1) * 128], in_=pst[:, k, :],
                                                           identity=IDF), reads=[PST, IDFB], writes=[PB[0]])
            P.op("act", lambda: nc.scalar.copy(out=pT[:], in_=psum[0][:, 0:384]), reads=[PB[0]], writes=[PT])

            stage = [sb0("stage%d" % i, [128, D], F32) for i in range(2)]
            STG = [Buf("stage%d" % i) for i in range(2)]
            for t in range(NT):
                s = t % 2
                P.dma("sp", lambda t=t, s=s: nc.sync.dma_start(out=stage[s][:], in_=x_d[t * 128:(t + 1) * 128, :]),
                      writes=[STG[s]])
                for half in range(2):
                    pb = 1 + (2 * t + half) % 2
                    for cc in range(4):
                        c = half * 4 + cc
                        P.op("pe", lambda s=s, c=c, cc=cc, pb=pb: nc.tensor.transpose(
                            out=psum[pb][:, cc * 128:(cc + 1) * 128], in_=stage[s][:, c * 128:(c + 1) * 128],
                            identity=IDF), reads=[STG[s], IDFB], writes=[PB[pb]])
                    outap = xT[:, half * 4:half * 4 + 4, t * 128:(t + 1) * 128]
                    inap = psum[pb][:].rearrange("p (c t) -> p c t", c=4)
                    wr = [XB[half * 4 + cc][t // 4] for cc in range(4)]
                    if half == 0:
                        P.op("act", lambda outap=outap, inap=inap: nc.scalar.copy(out=outap, in_=inap),
                             reads=[PB[pb]], writes=wr)
                    else:
                        P.op("dve", lambda outap=outap, inap=inap: nc.vector.tensor_copy(out=outap, in_=inap),
                             reads=[PB[pb]], writes=wr)

            P.op("act", lambda: nc.scalar.activation(out=sc[:], in_=pT[:, PC_COND:PC_COND + 8], func=AF.Silu),
                 reads=[PT], writes=[SC])
            wm = [sb0("wm%d" % i, [128, DC, 512], BF16) for i in range(2)]
            WM = [Buf("wm%d" % i) for i in range(2)]
            for pc in range(18):
                s = pc % 2
                P.dma("pool", lambda pc=pc, s=s: nc.gpsimd.dma_start(
                    out=wm[s][:], in_=wmod_d[:, pc * 512:(pc + 1) * 512].rearrange("(k p) f -> p k f", p=128)),
                    writes=[WM[s]])
                for mm in range(4):
                    col = pc * 4 + mm
                    for kc in range(DC):
                        P.op("pe", lambda s=s, mm=mm, kc=kc, col=col: nc.tensor.matmul(
                            psum[3][:, col:col + 1], wm[s][:, kc, mm * 128:(mm + 1) * 128], sc[:, kc:kc + 1],
                            start=(kc == 0), stop=(kc == DC - 1)), reads=[WM[s], SC], writes=[PB[3]])
            P.op("dve", lambda: nc.vector.tensor_tensor(out=modT[:], in0=psum[3][:, 0:72],
                                                        in1=pT[:, PC_BMOD:PC_BMOD + 72], op=ALU.add),
                 reads=[PB[3], PT], writes=[MOD])

            def mod_ab(i, do_ab, do_g):
                rw = 0.5 if i != 1 else 1.0
                a_ap = ab[:, i * 24:i * 24 + 8]
                b_ap = ab[:, i * 24 + 8:i * 24 + 16]
                g_ap = ab[:, i * 24 + 16:i * 24 + 24]
                if do_ab:
                    P.op("dve", lambda: nc.vector.scalar_tensor_tensor(
                        out=a_ap, in0=modT[:, (3 * i + 1) * 8:(3 * i + 2) * 8], scalar=1.0,
                        in1=pT[:, PC_NG + 16 * i:PC_NG + 16 * i + 8], op0=ALU.add, op1=ALU.mult),
                        reads=[MOD, PT], writes=[AB])
                    P.op("dve", lambda: nc.vector.tensor_copy(out=b_ap, in_=modT[:, 3 * i * 8:3 * i * 8 + 8]),
                         reads=[MOD], writes=[AB])
                if do_g:
                    P.op("dve", lambda: nc.vector.scalar_tensor_tensor(
                        out=g_ap, in0=modT[:, (3 * i + 2) * 8:(3 * i + 3) * 8], scalar=rw,
                        in1=pT[:, PC_NG + 16 * i + 8:PC_NG + 16 * i + 16], op0=ALU.mult, op1=ALU.mult),
                        reads=[MOD, PT], writes=[AB])

            for i_ in range(3):
                mod_ab(i_, True, True)
            P.barrier()

        def mod_rest():
            wmr = bgbuf["wmr"]
            cols = list(range(16, 72))
            SKEW = 2

            def load(idx):
                col = cols[idx]
                k = idx % 3
                P.dma("pool", lambda: nc.gpsimd.dma_start(
                    out=wmr[k][:], in_=wmod_d[:, col * 128:(col + 1) * 128].rearrange("(k p) f -> p k f", p=128)),
                    writes=[WMR[k]])

            for idx in range(SKEW):
                load(idx)
            for idx, col in enumerate(cols):
                if idx + SKEW < len(cols):
                    load(idx + SKEW)
                k = idx % 3
                pbk = 3 if col < 24 else 2
                for kc in range(DC):
                    P.op("pe", lambda k=k, kc=kc, col=col, pbk=pbk: nc.tensor.matmul(
                        psum[pbk][:, col:col + 1], wmr[k][:, kc, :], sc[:, kc:kc + 1], start=(kc == 0),
                        stop=(kc == DC - 1)), reads=[WMR[k], SC], writes=[PB[pbk]])
                if col == 23:
                    P.op("dve", lambda: nc.vector.tensor_tensor(out=modT[:, 16:24], in0=psum[3][:, 16:24],
                                                                in1=pT[:, PC_BMOD + 16:PC_BMOD + 24], op=ALU.add),
                         reads=[PB[3], PT], writes=[MOD])
                    mod_ab(0, False, True)
                yield
            P.op("dve", lambda: nc.vector.tensor_tensor(out=modT[:, 24:72], in0=psum[2][:, 24:72],
                                                        in1=pT[:, PC_BMOD + 24:PC_BMOD + 72], op=ALU.add),
                 reads=[PB[2], PT], writes=[MOD])
            mod_ab(1, True, True)
            mod_ab(2, True, True)
            yield

        bg = []

        def bg_step(n):
            for _ in range(n):
                if bg:
                    try:
                        next(bg[0])
                    except StopIteration:
                        bg.pop(0)

        rtmp = sb("rtmp", [128, 512], F32)
        RT = Buf("rtmp")
        epsT = sb("epsT", [128, 1], F32)
        EPST = Buf("epsT", frozen=True)
        P.op("dve", lambda: nc.vector.memset(epsT[:], EPS), writes=[EPST])

        def rstd_from(pbank):
            P.op("act", lambda: nc.scalar.activation(out=rtmp[:], in_=psum[pbank][:], func=AF.Ln, bias=epsT[:, 0:1],
                                                     scale=1.0), reads=[PB[pbank], EPST], writes=[RT])
            P.op("act", lambda: nc.scalar.activation(out=rstd[:], in_=rtmp[:], func=AF.Exp, scale=-0.5),
                 reads=[RT], writes=[RS])

        pn_extra = []

        def prenorm(i, hT, HB, groups):
            for li, g in enumerate(groups):
                gs = slice(g * 512, (g + 1) * 512)
                ls = slice(li * 512, (li + 1) * 512)
                for c in range(DC):
                    sq, SQ = nsq()
                    if c % 2 == 0:
                        P.op("act", lambda c=c, gs=gs, sq=sq: nc.scalar.activation(out=sq[:], in_=xT[:, c, gs],
                                                                                  func=AF.Square),
                             reads=[XB[c][g]], writes=[SQ])
                    else:
                        P.op("dve", lambda c=c, gs=gs, sq=sq: nc.vector.tensor_tensor(
                            out=sq[:], in0=xT[:, c, gs], in1=xT[:, c, gs], op=ALU.mult),
                            reads=[XB[c][g]], writes=[SQ])
                    P.op("pe", lambda c=c, sq=sq: nc.tensor.matmul(psum[0][:], MEANB, sq[:], start=(c == 0),
                                                                   stop=(c == DC - 1)),
                         reads=[SQ, CST], writes=[PB[0]])
                rstd_from(0)
                tl = list(zip(tmpf, TF)) + pn_extra
                for c in range(DC):
                    tb, TB_ = tl[c % len(tl)]
                    P.op("dve", lambda c=c, gs=gs, tb=tb: nc.vector.scalar_tensor_tensor(
                        out=tb[:], in0=xT[:, c, gs], scalar=ab[:, i * 24 + c:i * 24 + c + 1], in1=rstd[:],
                        op0=ALU.mult, op1=ALU.mult), reads=[XB[c][g], RS, AB], writes=[TB_])
                    P.op("act", lambda c=c, ls=ls, tb=tb: nc.scalar.activation(
                        out=hT[:, c, ls], in_=tb[:], func=AF.Identity,
                        bias=ab[:, i * 24 + 8 + c:i * 24 + 9 + c], scale=1.0), reads=[TB_, AB], writes=[HB[li]])

        def postnorm_residual(i, ybuf, YB, groups, pstat):
            for li, g in enumerate(groups):
                gs = slice(g * 512, (g + 1) * 512)
                ls = slice(li * 512, (li + 1) * 512)
                rstd_from(pstat[li])
                for c in range(DC):
                    k = c % 2
                    P.op("dve", lambda c=c, ls=ls, k=k: nc.vector.scalar_tensor_tensor(
                        out=tmpf[k][:], in0=ybuf[:, c, ls],
                        scalar=ab[:, i * 24 + 16 + c:i * 24 + 17 + c], in1=rstd[:], op0=ALU.mult, op1=ALU.mult),
                        reads=[YB[c][li], RS, AB], writes=[TF[k]])
                    P.op("dve", lambda c=c, gs=gs, k=k: nc.vector.tensor_tensor(
                        out=xT[:, c, gs], in0=tmpf[k][:], in1=xT[:, c, gs], op=ALU.add),
                        reads=[TF[k], XB[c][g]], writes=[XB[c][g]])

        def ffn(i, wi_d, wo_d):
            with ExitStack() as es2:
                sb2 = lambda name, shape, dty: es2.enter_context(nc.sbuf_tensor("%s_f%d" % (name, i), shape, dty))
                hT = sb2("hT", [128, DC, 1024], BF16)
                HB = [Buf("h%d" % g) for g in range(2)]
                hid = sb2("hid", [128, FC, 1024], BF16)
                HID = [[Buf("hid%d_%d" % (m, g)) for g in range(2)] for m in range(FC)]
                ybuf = sb2("ybuf", [128, DC, 1024], F32)
                YB = [[Buf("y%d_%d" % (m, g)) for g in range(2)] for m in range(DC)]
                wsl = [sb2("wsl%d" % k, [128, 2, DC, 128], BF16) for k in range(3)]
                WS = [Buf("wsl%d" % k) for k in range(3)]
                w2sl = [sb2("w2sl%d" % k, [128, FC, 128], BF16) for k in range(2)]
                W2S = [Buf("w2sl%d" % k) for k in range(2)]
                pn_extra[:] = [(sb2("pnx%d" % k, [128, 512], F32), Buf("pnx%d" % k)) for k in range(2)]
                rsy = [sb2("rsy%d" % k, [128, 512], F32) for k in range(2)]
                RSY = [Buf("rsy%d_f%d" % (k, i)) for k in range(2)]
                deferred_post = []
                wcnt = 0
                w2cnt = 0
                for half in range(2):
                    groups = [2 * half, 2 * half + 1]
                    prenorm(i, hT, HB, groups)
                    order = [(m, 0) for m in range(3)] + [(m, 1) for m in range(3)] + \
                            [(m, li) for m in range(3, FC) for li in range(2)]
                    w_issued = set()
                    for it_, (m, li) in enumerate(order):
                        if deferred_post and it_ >= 3:
                            deferred_post.pop(0)()
                        s = (half * FC + m) % 3
                        if m not in w_issued:
                            w_issued.add(m)
                            P.dma("pool", lambda m=m, s=s: nc.gpsimd.dma_start(
                                out=wsl[s][:, 0],
                                in_=wi_d[:, m * 128:(m + 1) * 128].rearrange("(k p) f -> p k f", p=128)),
                                writes=[WS[s]])
                            P.dma("pool", lambda m=m, s=s: nc.gpsimd.dma_start(
                                out=wsl[s][:, 1],
                                in_=wi_d[:, DFF + m * 128:DFF + (m + 1) * 128].rearrange("(k p) f -> p k f", p=128)),
                                writes=[WS[s]])
                        ls = slice(li * 512, (li + 1) * 512)
                        pg = 4 + 2 * (it_ % 2)
                        pu = pg + 1
                        for kc in range(DC):
                            P.op("pe", lambda s=s, kc=kc, ls=ls, pg=pg: nc.tensor.matmul(
                                psum[pg][:], wsl[s][:, 0, kc, :], hT[:, kc, ls], start=(kc == 0),
                                stop=(kc == DC - 1)), reads=[WS[s], HB[li]], writes=[PB[pg]])
                        for kc in range(DC):
                            P.op("pe", lambda s=s, kc=kc, ls=ls, pu=pu: nc.tensor.matmul(
                                psum[pu][:], wsl[s][:, 1, kc, :], hT[:, kc, ls], start=(kc == 0),
                                stop=(kc == DC - 1)), reads=[WS[s], HB[li]], writes=[PB[pu]])
                        sq, SQ = nsq()
                        P.op("act", lambda pg=pg, sq=sq: nc.scalar.activation(out=sq[:], in_=psum[pg][:],
                                                                             func=AF.Silu),
                             reads=[PB[pg]], writes=[SQ])
                        P.op("dve", lambda m=m, ls=ls, pu=pu, sq=sq: nc.vector.tensor_tensor(
                            out=hid[:, m, ls], in0=psum[pu][:], in1=sq[:], op=ALU.mult),
                            reads=[PB[pu], SQ], writes=[HID[m][li]])
                    pend = []
                    for m in range(DC):
                        bg_step(2 if half == 0 else 3)
                        s = w2cnt % 2
                        w2cnt += 1
                        P.dma("pool", lambda m=m, s=s: nc.gpsimd.dma_start(
                            out=w2sl[s][:], in_=wo_d[:, m * 128:(m + 1) * 128].rearrange("(k p) f -> p k f", p=128)),
                            writes=[W2S[s]])
                        for li in range(2):
                            ls = slice(li * 512, (li + 1) * 512)
                            py = 4 + (m * 2 + li) % 2
                            for kc in range(FC):
                                P.op("pe", lambda s=s, kc=kc, ls=ls, py=py: nc.tensor.matmul(
                                    psum[py][:], w2sl[s][:, kc, :], hid[:, kc, ls], start=(kc == 0),
                                    stop=(kc == FC - 1)), reads=[W2S[s], HID[kc][li]], writes=[PB[py]])
                            P.op("act", lambda m=m, ls=ls, py=py: nc.scalar.copy(
                                out=ybuf[:, m, ls], in_=psum[py][:]), reads=[PB[py]], writes=[YB[m][li]])
                            sq, SQ = nsq()
                            P.op("act", lambda py=py, sq=sq: nc.scalar.activation(out=sq[:], in_=psum[py][:],
                                                                                 func=AF.Square),
                                 reads=[PB[py]], writes=[SQ])
                            def stat(m=m, li=li, sq=sq, SQ=SQ):
                                P.op("pe", lambda: nc.tensor.matmul(
                                    psum[6 + li][:], MEANB, sq[:], start=(m == 0), stop=(m == DC - 1)),
                                    reads=[SQ, CST], writes=[PB[6 + li]])
                            pend.append(stat)
                            if len(pend) > 1:
                                pend.pop(0)()
                    while pend:
                        pend.pop(0)()
                    if half == 1:
                        postnorm_residual(i, ybuf, YB, groups, [6, 7])
                    else:
                        for li in range(2):
                            P.op("act", lambda li=li: nc.scalar.activation(out=rtmp[:], in_=psum[6 + li][:], func=AF.Ln,
                                                                           bias=epsT[:, 0:1], scale=1.0),
                                 reads=[PB[6 + li], EPST], writes=[RT])
                            P.op("act", lambda li=li: nc.scalar.activation(out=rsy[li][:], in_=rtmp[:], func=AF.Exp,
                                                                           scale=-0.5), reads=[RT], writes=[RSY[li]])
                        for li in range(2):
                            g = groups[li]
                            for c in range(DC):
                                def chunk(li=li, g=g, c=c):
                                    gs = slice(g * 512, (g + 1) * 512)
                                    ls = slice(li * 512, (li + 1) * 512)
                                    k = c % 2
                                    P.op("dve", lambda: nc.vector.scalar_tensor_tensor(
                                        out=tmpf[k][:], in0=ybuf[:, c, ls],
                                        scalar=ab[:, i * 24 + 16 + c:i * 24 + 17 + c], in1=rsy[li][:], op0=ALU.mult,
                                        op1=ALU.mult), reads=[YB[c][li], RSY[li], AB], writes=[TF[k]])
                                    P.op("dve", lambda: nc.vector.tensor_tensor(
                                        out=xT[:, c, gs], in0=tmpf[k][:], in1=xT[:, c, gs], op=ALU.add),
                                        reads=[TF[k], XB[c][g]], writes=[XB[c][g]])
                                deferred_post.append(chunk)
                while deferred_post:
                    deferred_post.pop(0)()
                P.barrier()
                pn_extra[:] = []

        def mixer():
            i = 1
            with ExitStack() as esm:
                sbm = lambda name, shape, dty: esm.enter_context(nc.sbuf_tensor("mx_" + name, shape, dty))
                oT = sbm("oT", [128, NT, 4, 128], BF16)
                OT = [Buf("oT%d" % t) for t in range(NT)]
                gB = sbm("gB", [128, NT, 8], F32)
                bB = sbm("bB", [128, NT, 8], F32)
                EX = sbm("EX", [128, NT, 24], F32)
                lnB = sbm("lnB", [128, NT, 8], F32)
                GSRC, GATE = Buf("gsrc"), Buf("gate")
                with ExitStack() as esab:
                    sbab = lambda name, shape, dty: esab.enter_context(nc.sbuf_tensor("ab_" + name, shape, dty))
                    qkT = sbab("qkT", [128, 8, T], BF16)
                    vT = sbab("vT", [128, 4, T], BF16)
                    QKB, VB = Buf("qk"), Buf("v")
                    with ExitStack() as esa:
                        sba = lambda name, shape, dty: esa.enter_context(nc.sbuf_tensor("a_" + name, shape, dty))
                        hT_a = sba("hT_a", [128, DC, 1024], BF16)
                        HB_a = [Buf("mh%d" % g) for g in range(2)]
                        wsl_a = [sba("w%d" % k, [128, DC, 128], BF16) for k in range(3)]
                        WS_a = [Buf("mw%d" % k) for k in range(3)]
                        wg_a = sba("wg_a", [128, DC, 16], BF16)
                        WG_a = Buf("wg_a")
                        acc_a = [sba("acc_a%d" % k, [128, 512], F32) for k in range(2)]
                        ACC_a = [Buf("acc_a%d" % k) for k in range(2)]
                        sv_a = [sba("sv_a%d" % k, [128, 512], F32) for k in range(3)]
                        SV_a = [Buf("sv_a%d" % k) for k in range(3)]
                        rstd2_a = [rstd, sba("rstd2", [128, 512], F32)]
                        RS2_a = [RS, Buf("rstd2")]
                        qk_cnt = [0]
                        t7_a = [sba("t7_a%d" % k, [128, 8], F32) for k in range(2)]
                        T7_a = [Buf("t7_a%d" % k) for k in range(2)]
                        P.dma("pool", lambda: nc.gpsimd.dma_start(
                            out=wg_a[:], in_=win_d[:, 1536:1552].rearrange("(k p) f -> p k f", p=128)), writes=[WG_a])
                        chunks = [(qkT, cc, cc * 128, "q" if cc < 4 else "k") for cc in range(8)]
                        chunks += [(vT, cc, 1024 + cc * 128, "v") for cc in range(4)]
                        wcnt_a = 0
                        it_a = 0
                        q1_a, q2_a = [], []
                        pn_extra[:] = [(sba("pnx%d" % k, [128, 512], F32), Buf("pnxa%d" % k)) for k in range(2)]
                        for half in range(2):
                            groups_a = [2 * half, 2 * half + 1]
                            prenorm(i, hT_a, HB_a, groups_a)
                            def gate_mm(tl0, tl1, half=half):
                                for tl in range(tl0, tl1):
                                    t = half * 8 + tl
                                    li = tl // 4
                                    tls = slice(tl * 128, (tl + 1) * 128)
                                    for kc in range(DC):
                                        P.op("pe", lambda t=t, tls=tls, kc=kc: nc.tensor.matmul(
                                            psum[3][:, t * 16:(t + 1) * 16], hT_a[:, kc, tls], wg_a[:, kc, :],
                                            start=(kc == 0), stop=(kc == DC - 1)), reads=[HB_a[li], WG_a],
                                            writes=[PB[3]])

                            gate_mm(0, 4)
                            order_a = [(ci, 0) for ci in range(3)] + [(ci, 1) for ci in range(3)] + \
                                      [(ci, li) for ci in range(3, len(chunks)) for li in range(2)]
                            seen_a = {}
                            for oi, (ci, li_only) in enumerate(order_a):
                                if oi == 3:
                                    gate_mm(4, 8)
                                dst, dc, col0, kind = chunks[ci]
                                if ci not in seen_a:
                                    seen_a[ci] = (half * len(chunks) + ci) % 3
                                    s_ = seen_a[ci]
                                    P.dma("pool", lambda s_=s_, col0=col0: nc.gpsimd.dma_start(
                                        out=wsl_a[s_][:],
                                        in_=win_d[:, col0:col0 + 128].rearrange("(k p) f -> p k f", p=128)),
                                        writes=[WS_a[s_]])
                                s_ = seen_a[ci]
                                DB = {"q": QKB, "k": QKB, "v": VB}[kind]
                                for li in (li_only,):
                                    g = groups_a[li]
                                    gs = slice(g * 512, (g + 1) * 512)
                                    ls = slice(li * 512, (li + 1) * 512)
                                    pb = 4 + it_a % 4
                                    k2 = it_a % 2
                                    it_a += 1
                                    for kc in range(DC):
                                        P.op("pe", lambda s_=s_, kc=kc, ls=ls, pb=pb: nc.tensor.matmul(
                                            psum[pb][:], wsl_a[s_][:, kc, :], hT_a[:, kc, ls], start=(kc == 0),
                                            stop=(kc == DC - 1)), reads=[WS_a[s_], HB_a[li]], writes=[PB[pb]])
                                    cch = ci
                                    w0 = pT[:, PC_DNC + 0 * 12 + cch:PC_DNC + 0 * 12 + cch + 1]
                                    w1 = pT[:, PC_DNC + 1 * 12 + cch:PC_DNC + 1 * 12 + cch + 1]
                                    w2 = pT[:, PC_DNC + 2 * 12 + cch:PC_DNC + 2 * 12 + cch + 1]
                                    Pp = psum[pb]
                                    A_ = acc_a[k2]
                                    P.op("dve", lambda A_=A_, Pp=Pp, w1=w1: nc.vector.tensor_scalar(
                                        out=A_[:], in0=Pp[:], scalar1=w1, scalar2=None, op0=ALU.mult),
                                        reads=[PB[pb], PT], writes=[ACC_a[k2]])
                                    P.op("dve", lambda A_=A_, Pp=Pp, w0=w0: nc.vector.scalar_tensor_tensor(
                                        out=A_[:, 1:512], in0=Pp[:, 0:511], scalar=w0, in1=A_[:, 1:512], op0=ALU.mult,
                                        op1=ALU.add), reads=[PB[pb], PT, ACC_a[k2]], writes=[ACC_a[k2]])
                                    P.op("dve", lambda A_=A_, Pp=Pp, w2=w2: nc.vector.scalar_tensor_tensor(
                                        out=A_[:, 0:511], in0=Pp[:, 1:512], scalar=w2, in1=A_[:, 0:511], op0=ALU.mult,
                                        op1=ALU.add), reads=[PB[pb], PT, ACC_a[k2]], writes=[ACC_a[k2]])
                                    Pv = Pp[:].rearrange("p (r w) -> p r w", w=64)
                                    Av = A_[:].rearrange("p (r w) -> p r w", w=64)
                                    P.op("dve", lambda Pv=Pv, w0=w0, k2=k2: nc.vector.scalar_tensor_tensor(
                                        out=t7_a[k2][:, 0:7], in0=Pv[:, 0:7, 63], scalar=w0, in1=nlink[:, 1:8], op0=ALU.mult,
                                        op1=ALU.mult), reads=[PB[pb], PT, SML], writes=[T7_a[k2]])
                                    P.op("dve", lambda Av=Av, k2=k2: nc.vector.tensor_tensor(
                                        out=Av[:, 1:8, 0], in0=Av[:, 1:8, 0], in1=t7_a[k2][:, 0:7], op=ALU.add),
                                        reads=[T7_a[k2], ACC_a[k2]], writes=[ACC_a[k2]])
                                    P.op("dve", lambda Pv=Pv, w2=w2, k2=k2: nc.vector.scalar_tensor_tensor(
                                        out=t7_a[k2][:, 0:7], in0=Pv[:, 1:8, 0], scalar=w2, in1=nlink[:, 1:8], op0=ALU.mult,
                                        op1=ALU.mult), reads=[PB[pb], PT, SML], writes=[T7_a[k2]])
                                    P.op("dve", lambda Av=Av, k2=k2: nc.vector.tensor_tensor(
                                        out=Av[:, 0:7, 63], in0=Av[:, 0:7, 63], in1=t7_a[k2][:, 0:7], op=ALU.add),
                                        reads=[T7_a[k2], ACC_a[k2]], writes=[ACC_a[k2]])
                                    cs = (128.0 ** -0.5) if kind == "q" else 1.0
                                    j3 = qk_cnt[0] % 3
                                    j2 = qk_cnt[0] % 2
                                    if kind != "v":
                                        qk_cnt[0] += 1
                                    pn = j3

                                    def st1(kind=kind, A_=A_, dc=dc, gs=gs, k2=k2, DB=DB, pn=pn, j3=j3):
                                        if kind == "v":
                                            P.op("act", lambda: nc.scalar.activation(out=vT[:, dc, gs], in_=A_[:],
                                                                                     func=AF.Silu),
                                                 reads=[ACC_a[k2]], writes=[DB])
                                            return
                                        S_ = sv_a[j3]
                                        P.op("act", lambda: nc.scalar.activation(out=S_[:], in_=A_[:], func=AF.Silu),
                                             reads=[ACC_a[k2]], writes=[SV_a[j3]])
                                        sq, SQ = nsq()
                                        P.op("act", lambda: nc.scalar.activation(out=sq[:], in_=S_[:], func=AF.Square),
                                             reads=[SV_a[j3]], writes=[SQ])
                                        P.op("pe", lambda: nc.tensor.matmul(psum[pn][:], ONEB, sq[:], start=True, stop=True),
                                             reads=[SQ, CST], writes=[PB[pn]])

                                    def st2(pn=pn, dc=dc, gs=gs, cs=cs, j3=j3, j2=j2, DB=DB):
                                        rs_t, RS_B = rstd2_a[j2], RS2_a[j2]
                                        P.op("act", lambda: nc.scalar.activation(out=rtmp[:], in_=psum[pn][:], func=AF.Ln,
                                                                                 bias=epsT[:, 0:1], scale=1.0),
                                             reads=[PB[pn], EPST], writes=[RT])
                                        P.op("act", lambda: nc.scalar.activation(out=rs_t[:], in_=rtmp[:], func=AF.Exp,
                                                                                 scale=-0.5), reads=[RT], writes=[RS_B])
                                        P.op("dve", lambda: nc.vector.scalar_tensor_tensor(
                                            out=qkT[:, dc, gs], in0=sv_a[j3][:], scalar=cs, in1=rs_t[:], op0=ALU.mult,
                                            op1=ALU.mult), reads=[SV_a[j3], RS_B], writes=[DB])

                                    q1_a.append((st1, None if kind == "v" else st2))
                                    if len(q1_a) > 1:
                                        f1, f2 = q1_a.pop(0)
                                        f1()
                                        if f2 is not None:
                                            q2_a.append(f2)
                                    if len(q2_a) >= 3:
                                        q2_a.pop(0)()
                                        q2_a.pop(0)()
                            while q1_a:
                                f1, f2 = q1_a.pop(0)
                                f1()
                                if f2 is not None:
                                    q2_a.append(f2)
                            while q2_a:
                                q2_a.pop(0)()
                        gp = psum[3][:, 0:NT * 16].rearrange("p (t c) -> p t c", c=16)
                        gtmp = sba("gtmp", [128, NT, 8], F32)
                        GT = Buf("gtmp")
                        nal = sba("nal", [128, 8], F32)
                        NAL = Buf("nal")
                        P.op("dve", lambda: nc.vector.tensor_tensor(
                            out=gtmp[:], in0=gp[:, :, 0:8], in1=gconst[:, 8:16].unsqueeze(1).broadcast_to([128, NT, 8]),
                            op=ALU.add), reads=[PB[3], SML], writes=[GT])
                        P.op("act", lambda: nc.scalar.activation(out=gtmp[:], in_=gtmp[:], func=AF.Exp), reads=[GT],
                             writes=[GT])
                        P.op("dve", lambda: nc.vector.tensor_scalar(out=gtmp[:], in0=gtmp[:], scalar1=1.0, scalar2=None,
                                                                    op0=ALU.add), reads=[GT], writes=[GT])
                        P.op("act", lambda: nc.scalar.activation(out=gtmp[:], in_=gtmp[:], func=AF.Ln), reads=[GT],
                             writes=[GT])
                        P.op("act", lambda: nc.scalar.activation(out=nal[:], in_=gconst[:, 0:8], func=AF.Exp),
                             reads=[SML], writes=[NAL])
                        P.op("dve", lambda: nc.vector.scalar_tensor_tensor(
                            out=gB[:], in0=gtmp[:], scalar=-1.0, in1=nal[:].unsqueeze(1).broadcast_to([128, NT, 8]),
                            op0=ALU.mult, op1=ALU.mult), reads=[GT, NAL], writes=[GSRC])
                        P.op("act", lambda: nc.scalar.activation(out=bB[:], in_=gp[:, :, 8:16], func=AF.Sigmoid),
                             reads=[PB[3]], writes=[GATE])
                        P.op("act", lambda: nc.scalar.activation(out=lnB[:], in_=bB[:], func=AF.Ln), reads=[GATE],
                             writes=[GATE])
                        gate_cums(P, nc, NT, gB, EX, GATE, GSRC, cstf, CSTF, psum, PB, 0)
                        P.barrier()
                        pn_extra[:] = []
                    for b_ in (GSRC, GATE, QKB, VB):
                        b_.frozen = True
                    with ExitStack() as esb:
                        sbb = lambda name, shape, dty: esb.enter_context(nc.sbuf_tensor("b_" + name, shape, dty))
                        onb = sbb("onb", [128, 4, 128], BF16)
                        ONB = Buf("onb")
                        ms4 = sbb("ms4", [128, 8], F32)
                        MS4 = Buf("ms4")
                        tmps = [tmpf[0], tmpf[1], rtmp, rstd]
                        TMPS = [TF[0], TF[1], RT, RS]
                        v3 = lambda ap: ap.rearrange("p (h d) -> p h d", h=4)
                        ps3 = lambda b: psum[b][:].rearrange("p (h d) -> p h d", h=4)
                        bcl = lambda ap2: ap2.unsqueeze(2).broadcast_to([128, 4, 128])

                        def finish_tile(t, d, pq, po, eg_ap):
                            ts = slice(t * 128, (t + 1) * 128)
                            first = (d == 0) == (t < NT // 2)
                            tA, TA = tmps[2], TMPS[2]
                            tB, TB = tmps[3], TMPS[3]
                            part = oT[:, t, :, :]
                            P.op("dve", lambda: nc.vector.tensor_tensor(out=v3(tA[:]), in0=ps3(pq), in1=bcl(eg_ap),
                                                                        op=ALU.mult), reads=[PB[pq], GATE], writes=[TA])
                            if first:
                                P.op("dve", lambda: nc.vector.tensor_tensor(out=part, in0=ps3(po), in1=v3(tA[:]), op=ALU.add),
                                     reads=[PB[po], TA], writes=[OT[t]])
                                return
                            P.op("dve", lambda: nc.vector.tensor_tensor(out=v3(tA[:]), in0=ps3(po), in1=v3(tA[:]), op=ALU.add),
                                 reads=[PB[po], TA], writes=[TA])
                            P.op("pool", lambda: nc.gpsimd.tensor_tensor(out=v3(tA[:]), in0=v3(tA[:]), in1=part, op=ALU.add),
                                 reads=[TA, OT[t]], writes=[TA])
                            P.op("pool", lambda: nc.gpsimd.tensor_tensor(out=tB[:], in0=tA[:], in1=tA[:], op=ALU.mult),
                                 reads=[TA], writes=[TB])
                            P.op("dve", lambda: nc.vector.tensor_reduce(out=ms4[:, 0:4], in_=v3(tB[:]), axis=AX.X, op=ALU.add),
                                 reads=[TB], writes=[MS4])
                            P.op("act", lambda: nc.scalar.activation(out=ms4[:, 0:4], in_=ms4[:, 0:4], func=AF.Ln,
                                                                     bias=epsT[:, 0:1], scale=1.0 / 128.0),
                                 reads=[MS4, EPST], writes=[MS4])
                            P.op("act", lambda: nc.scalar.activation(out=ms4[:, 4:8], in_=ms4[:, 0:4], func=AF.Exp,
                                                                     scale=-0.5), reads=[MS4], writes=[MS4])
                            P.op("pool", lambda: nc.gpsimd.tensor_tensor(out=onb[:], in0=v3(tA[:]), in1=bcl(ms4[:, 4:8]),
                                                                         op=ALU.mult), reads=[TA, MS4], writes=[ONB])
                            for h in range(4):
                                P.op("pe", lambda h=h: nc.tensor.matmul(psum[pq][:, h * 128:(h + 1) * 128], onb[:, h, :], IDB,
                                                                        start=True, stop=True),
                                     reads=[ONB, CST], writes=[PB[pq]])
                            P.op("act", lambda: nc.scalar.activation(out=part, in_=ps3(pq), func=AF.Copy,
                                                                     scale=pT[:, PC_DNG:PC_DNG + 1]),
                                 reads=[PB[pq], PT], writes=[OT[t]])

                        deltanet(P, nc, esb, NT, qkT, QKB, vT, VB, gB, bB, lnB, EX, GATE, cstf, cstb, CSTF, CST, psum, PB,
                                 s0_d, carry, SML, st_d, tmps, TMPS, finish_tile, sq3=sqs)
                        P.barrier()
                cvT = sbm("cvT", [128, 4, T], BF16)
                CV = [[Buf("cv%d_%d" % (c, g)) for g in range(NG)] for c in range(4)]
                with ExitStack() as esc:
                    sbc_ = lambda name, shape, dty: esc.enter_context(nc.sbuf_tensor("c_" + name, shape, dty))
                    pad_c = sbc_("pad_c", [128, 4, 32, 94], BF16)
                    PAD_c = [Buf("pad_c%d" % c) for c in range(4)]
                    sg_c = [sbc_("sg_c%d" % k, [128, 512], F32) for k in range(2)]
                    SG_c = [Buf("sg_c%d" % k) for k in range(2)]
                    dg_c = [sbc_("dg_c%d" % k, [128, 31, 128], BF16) for k in range(2)]
                    DG_c = [Buf("dg_c%d" % k) for k in range(2)]
                    wtab_c = pT[:, PC_CVW:PC_CVW + 124].rearrange("p (t c) -> p t c", c=4)

                    def build_dg(c):
                        k2 = c % 2
                        P.op("pool", lambda: nc.gpsimd.tensor_tensor(
                            out=dg_c[k2][:], in0=IDB.unsqueeze(1).broadcast_to([128, 31, 128]),
                            in1=wtab_c[:, :, c].unsqueeze(2).broadcast_to([128, 31, 128]), op=ALU.mult),
                            reads=[CST, PT], writes=[DG_c[k2]])

                    build_dg(0)
                    build_dg(1)
                    esc1 = ExitStack()
                    hT_c = esc1.enter_context(nc.sbuf_tensor("c_hT_c", [128, DC, 1024], BF16))
                    HB_c = [Buf("ch%d" % g) for g in range(2)]
                    wsl_c = [esc1.enter_context(nc.sbuf_tensor("c_w%d" % k, [128, 2, DC, 128], BF16)) for k in range(4)]
                    WS_c = [Buf("cw%d" % k) for k in range(4)]
                    pn_extra[:] = [(esc1.enter_context(nc.sbuf_tensor("c_pnx%d" % k, [128, 512], F32)), Buf("pnxc%d" % k))
                                   for k in range(2)]
                    for c in range(4):
                        P.op("pool", lambda c=c: nc.gpsimd.memset(pad_c[:, c], 0.0), writes=[PAD_c[c]])
                    wcnt_c = 0
                    it_c = 0
                    for half in range(2):
                        groups_c = [2 * half, 2 * half + 1]
                        prenorm(i, hT_c, HB_c, groups_c)
                        for c in range(4):
                            s_ = wcnt_c % 4
                            wcnt_c += 1
                            for j, col0 in enumerate((2064 + c * 128, 2576 + c * 128)):
                                P.dma("pool", lambda s_=s_, j=j, col0=col0: nc.gpsimd.dma_start(
                                    out=wsl_c[s_][:, j], in_=win_d[:, col0:col0 + 128].rearrange("(k p) f -> p k f", p=128)),
                                    writes=[WS_c[s_]])
                            for li in range(2):
                                g = groups_c[li]
                                ls = slice(li * 512, (li + 1) * 512)
                                pv = 4 + 2 * (it_c % 2)
                                pg = pv + 1
                                k2 = it_c % 2
                                it_c += 1
                                for j, pb in ((0, pv), (1, pg)):
                                    for kc in range(DC):
                                        P.op("pe", lambda s_=s_, j=j, kc=kc, ls=ls, pb=pb: nc.tensor.matmul(
                                            psum[pb][:], wsl_c[s_][:, j, kc, :], hT_c[:, kc, ls], start=(kc == 0),
                                            stop=(kc == DC - 1)), reads=[WS_c[s_], HB_c[li]], writes=[PB[pb]])
                                P.op("act", lambda pg=pg, k2=k2: nc.scalar.activation(out=sg_c[k2][:], in_=psum[pg][:],
                                                                                     func=AF.Sigmoid),
                                     reads=[PB[pg]], writes=[SG_c[k2]])
                                P.op("dve", lambda c=c, g=g, pv=pv, k2=k2: nc.vector.tensor_tensor(
                                    out=pad_c[:, c, 8 * g:8 * g + 8, 15:79],
                                    in0=psum[pv][:].rearrange("p (r w) -> p r w", w=64),
                                    in1=sg_c[k2][:].rearrange("p (r w) -> p r w", w=64), op=ALU.mult),
                                    reads=[PB[pv], SG_c[k2]], writes=[PAD_c[c]])
                        for c in range(4):
                            s_ = wcnt_c % 4
                            wcnt_c += 1
                            col0 = 1552 + c * 128
                            P.dma("pool", lambda s_=s_, col0=col0: nc.gpsimd.dma_start(
                                out=wsl_c[s_][:, 0], in_=win_d[:, col0:col0 + 128].rearrange("(k p) f -> p k f", p=128)),
                                writes=[WS_c[s_]])
                            for li in range(2):
                                g = groups_c[li]
                                ls = slice(li * 512, (li + 1) * 512)
                                pv = 4 + 2 * (it_c % 2)
                                it_c += 1
                                for kc in range(DC):
                                    P.op("pe", lambda s_=s_, kc=kc, ls=ls, pv=pv: nc.tensor.matmul(
                                        psum[pv][:], wsl_c[s_][:, 0, kc, :], hT_c[:, kc, ls], start=(kc == 0),
                                        stop=(kc == DC - 1)), reads=[WS_c[s_], HB_c[li]], writes=[PB[pv]])
                                sq, SQ = nsq()
                                P.op("act", lambda pv=pv, sq=sq: nc.scalar.activation(out=sq[:], in_=psum[pv][:],
                                                                                     func=AF.Silu),
                                     reads=[PB[pv]], writes=[SQ])
                                otv = oT[:, 4 * g:4 * g + 4, c, :]
                                P.op("pool", lambda otv=otv, sq=sq: nc.gpsimd.tensor_tensor(
                                    out=otv, in0=otv, in1=sq[:].rearrange("p (t k) -> p t k", k=128), op=ALU.mult),
                                    reads=[SQ] + [OT[4 * g + q] for q in range(4)],
                                    writes=[OT[4 * g + q] for q in range(4)])
                    P.barrier()
                    pn_extra[:] = []
                    esc1.close()
                    cvf_c = sbc_("cvf_c", [128, 4, T], F32)
                    CVF_c = [[Buf("cvf_c%d_%d" % (c, g)) for g in range(NG)] for c in range(4)]
                    lkb_c = link32[:, 1:32].unsqueeze(2).broadcast_to([128, 31, 15])
                    for c in range(4):
                        P.op("dve", lambda c=c: nc.vector.tensor_tensor(
                            out=pad_c[:, c, 1:32, 0:15], in0=pad_c[:, c, 0:31, 64:79], in1=lkb_c, op=ALU.mult),
                            reads=[PAD_c[c], SML], writes=[PAD_c[c]])
                        P.op("dve", lambda c=c: nc.vector.tensor_tensor(
                            out=pad_c[:, c, 0:31, 79:94], in0=pad_c[:, c, 1:32, 15:30], in1=lkb_c, op=ALU.mult),
                            reads=[PAD_c[c], SML], writes=[PAD_c[c]])
                    M512_c = cstf[:, CF_M512, :]
                    it_c = 0
                    for c in range(4):
                        k2 = c % 2
                        if c >= 2:
                            build_dg(c)
                        for g in range(NG):
                            gs = slice(g * 512, (g + 1) * 512)
                            pb = 4 + it_c % 2
                            it_c += 1
                            for tau in range(31):
                                P.op("pe", lambda c=c, g=g, tau=tau, k2=k2, pb=pb: nc.tensor.matmul(
                                    psum[pb][:], dg_c[k2][:, tau, :], pad_c[:, c, 8 * g:8 * g + 8, tau:tau + 64],
                                    start=(tau == 0), stop=(tau == 30)), reads=[DG_c[k2], PAD_c[c]], writes=[PB[pb]])
                            P.op("act", lambda c=c, gs=gs, pb=pb: nc.scalar.activation(
                                out=cvf_c[:, c, gs], in_=psum[pb][:], func=AF.Identity,
                                bias=pT[:, PC_CVB + c:PC_CVB + c + 1], scale=1.0), reads=[PB[pb], PT], writes=[CVF_c[c][g]])
                    pend_c = []
                    for g in range(NG):
                        gs = slice(g * 512, (g + 1) * 512)
                        for c in range(4):
                            k2 = c % 2
                            P.op("act", lambda c=c, gs=gs, k2=k2: nc.scalar.activation(out=sg_c[k2][:], in_=cvf_c[:, c, gs],
                                                                                       func=AF.Square),
                                 reads=[CVF_c[c][g]], writes=[SG_c[k2]])

                            def stats(c=c, k2=k2, g=g, gs=gs):
                                P.op("pe", lambda: nc.tensor.matmul(psum[6][:], M512_c, cvf_c[:, c, gs], start=(c == 0),
                                                                    stop=(c == 3)), reads=[CSTF, CVF_c[c][g]], writes=[PB[6]])
                                P.op("pe", lambda: nc.tensor.matmul(psum[7][:], M512_c, sg_c[k2][:], start=(c == 0),
                                                                    stop=(c == 3)), reads=[CSTF, SG_c[k2]], writes=[PB[7]])
                            pend_c.append(stats)
                            if len(pend_c) > 1:
                                pend_c.pop(0)()
                        while pend_c:
                            pend_c.pop(0)()
                        P.op("act", lambda: nc.scalar.activation(out=tmpf[0][:], in_=psum[6][:], func=AF.Square),
                             reads=[PB[6]], writes=[TF[0]])
                        P.op("dve", lambda: nc.vector.tensor_tensor(out=rtmp[:], in0=psum[7][:], in1=tmpf[0][:],
                                                                    op=ALU.subtract), reads=[PB[7], TF[0]], writes=[RT])
                        P.op("act", lambda: nc.scalar.activation(out=rtmp[:], in_=rtmp[:], func=AF.Ln, bias=epsT[:, 0:1],
                                                                 scale=1.0), reads=[RT, EPST], writes=[RT])
                        P.op("act", lambda: nc.scalar.activation(out=rstd[:], in_=rtmp[:], func=AF.Exp, scale=-0.5),
                             reads=[RT], writes=[RS])
                        P.op("act", lambda: nc.scalar.copy(out=tmpf[1][:], in_=psum[6][:]), reads=[PB[6]], writes=[TF[1]])
                        for c in range(4):
                            k2 = c % 2
                            P.op("dve", lambda c=c, gs=gs, k2=k2: nc.vector.tensor_tensor(
                                out=sg_c[k2][:], in0=cvf_c[:, c, gs], in1=tmpf[1][:], op=ALU.subtract),
                                reads=[CVF_c[c][g], TF[1]], writes=[SG_c[k2]])
                            P.op("dve", lambda k2=k2: nc.vector.tensor_tensor(out=sg_c[k2][:], in0=sg_c[k2][:], in1=rstd[:],
                                                                              op=ALU.mult),
                                 reads=[SG_c[k2], RS], writes=[SG_c[k2]])
                            P.op("act", lambda c=c, gs=gs, k2=k2: nc.scalar.activation(
                                out=cvT[:, c, gs], in_=sg_c[k2][:], func=AF.Silu, bias=pT[:, PC_LNB + c:PC_LNB + c + 1],
                                scale=pT[:, PC_LNG + c:PC_LNG + c + 1]), reads=[SG_c[k2], PT], writes=[CV[c][g]])
                    P.barrier()
                with ExitStack() as esd:
                    sbd = lambda name, shape, dty: esd.enter_context(nc.sbuf_tensor("d_" + name, shape, dty))
                    ybuf_d2 = [sbd("ybuf_d%d" % h_, [128, DC, 1024], F32) for h_ in range(2)]
                    YB_d2 = [[[Buf("my%d_%d_%d" % (h_, m, g)) for g in range(2)] for m in range(DC)] for h_ in range(2)]
                    rsy_d = [sbd("rsy_d%d" % k, [128, 512], F32) for k in range(2)]
                    RSY_d = [Buf("rsy_d%d" % k) for k in range(2)]
                    defer_d = []
                    wsl_d = [sbd("w%d" % k, [128, DC, 128], BF16) for k in range(3)]
                    WS_d = [Buf("dw%d" % k) for k in range(3)]
                    wcnt_d = 0
                    pend_d = []
                    for half in range(2):
                        groups_d = [2 * half, 2 * half + 1]
                        ybuf_d, YB_d = ybuf_d2[half], YB_d2[half]
                        for m in range(DC):
                            s_ = wcnt_d % 3
                            wcnt_d += 1
                            P.dma("pool", lambda s_=s_, m=m: nc.gpsimd.dma_start(
                                out=wsl_d[s_][:], in_=wout_d[:, m * 128:(m + 1) * 128].rearrange("(k p) f -> p k f", p=128)),
                                writes=[WS_d[s_]])
                            for li in range(2):
                                g = groups_d[li]
                                gs = slice(g * 512, (g + 1) * 512)
                                ls = slice(li * 512, (li + 1) * 512)
                                py = 4 + (m * 2 + li) % 2
                                if defer_d:
                                    defer_d.pop(0)()
                                for kc in range(DC):
                                    if kc < 4:
                                        rhs = oT[:, 4 * g:4 * g + 4, kc, :]
                                        rd = [OT[4 * g + q] for q in range(4)]
                                    else:
                                        rhs = cvT[:, kc - 4, gs]
                                        rd = [CV[kc - 4][g]]
                                    P.op("pe", lambda s_=s_, kc=kc, rhs=rhs, py=py: nc.tensor.matmul(
                                        psum[py][:], wsl_d[s_][:, kc, :], rhs, start=(kc == 0), stop=(kc == DC - 1)),
                                        reads=[WS_d[s_]] + rd, writes=[PB[py]])
                                P.op("act", lambda m=m, ls=ls, py=py, ybuf_d=ybuf_d: nc.scalar.copy(
                                    out=ybuf_d[:, m, ls], in_=psum[py][:]), reads=[PB[py]], writes=[YB_d[m][li]])
                                sq, SQ = nsq()
                                P.op("act", lambda py=py, sq=sq: nc.scalar.activation(out=sq[:], in_=psum[py][:],
                                                                                     func=AF.Square),
                                     reads=[PB[py]], writes=[SQ])
                                def stat(m=m, li=li, sq=sq, SQ=SQ):
                                    P.op("pe", lambda: nc.tensor.matmul(
                                        psum[6 + li][:], MEANB, sq[:], start=(m == 0), stop=(m == DC - 1)),
                                        reads=[SQ, CST], writes=[PB[6 + li]])
                                pend_d.append(stat)
                                if len(pend_d) > 1:
                                    pend_d.pop(0)()
                        while pend_d:
                            pend_d.pop(0)()
                        if half == 1:
                            while defer_d:
                                defer_d.pop(0)()
                            postnorm_residual(i, ybuf_d, YB_d, groups_d, [6, 7])
                        else:
                            for li in range(2):
                                P.op("act", lambda li=li: nc.scalar.activation(out=rtmp[:], in_=psum[6 + li][:], func=AF.Ln,
                                                                               bias=epsT[:, 0:1], scale=1.0),
                                     reads=[PB[6 + li], EPST], writes=[RT])
                                P.op("act", lambda li=li: nc.scalar.activation(out=rsy_d[li][:], in_=rtmp[:], func=AF.Exp,
                                                                               scale=-0.5), reads=[RT], writes=[RSY_d[li]])
                            for li in range(2):
                                g = groups_d[li]
                                for c in range(DC):
                                    def chunk_d(li=li, g=g, c=c, ybuf_d=ybuf_d, YB_d=YB_d):
                                        gs = slice(g * 512, (g + 1) * 512)
                                        ls = slice(li * 512, (li + 1) * 512)
                                        k = c % 2
                                        P.op("dve", lambda: nc.vector.scalar_tensor_tensor(
                                            out=tmpf[k][:], in0=ybuf_d[:, c, ls],
                                            scalar=ab[:, i * 24 + 16 + c:i * 24 + 17 + c], in1=rsy_d[li][:], op0=ALU.mult,
                                            op1=ALU.mult), reads=[YB_d[c][li], RSY_d[li], AB], writes=[TF[k]])
                                        P.op("dve", lambda: nc.vector.tensor_tensor(
                                            out=xT[:, c, gs], in0=tmpf[k][:], in1=xT[:, c, gs], op=ALU.add),
                                            reads=[TF[k], XB[c][g]], writes=[XB[c][g]])
                                    defer_d.append(chunk_d)
                    P.barrier()

        if "ffn1" in stages:
            ffn(0, f1i_d, f1o_d)
        if "mixer" in stages:
            mixer()
        if "ffn2" in stages:
            ffn(2, f2i_d, f2o_d)

        if debug:
            P.dma("sp", lambda: nc.sync.dma_start(out=dbg_d[:, :, :], in_=xT[:]),
                  reads=[XB[c][g] for c in range(DC) for g in range(NG)])
        ost = [sb("ost%d" % i, [128, D], F32) for i in range(2)]
        OST = [Buf("ost%d" % i) for i in range(2)]
        for t in range(NT):
            s = t % 2
            for half in range(2):
                pb = 1 + (2 * t + half) % 2
                for cc in range(4):
                    c = half * 4 + cc
                    P.op("pe", lambda t=t, c=c, cc=cc, pb=pb: nc.tensor.transpose(
                        out=psum[pb][:, cc * 128:(cc + 1) * 128], in_=xT[:, c, t * 128:(t + 1) * 128],
                        identity=IDF), reads=[XB[c][t // 4], IDFB], writes=[PB[pb]])
                if half == 0:
                    P.op("act", lambda s=s, pb=pb: nc.scalar.copy(out=ost[s][:, 0:512], in_=psum[pb][:]),
                         reads=[PB[pb]], writes=[OST[s]])
                else:
                    P.op("act", lambda s=s, pb=pb: nc.scalar.copy(out=ost[s][:, 512:1024], in_=psum[pb][:]),
                         reads=[PB[pb]], writes=[OST[s]])
            P.dma("sp", lambda t=t, s=s: nc.sync.dma_start(out=y_d[t * 128:(t + 1) * 128, :], in_=ost[s][:]),
                  reads=[OST[s]])
        P.emit()
        build.stats = P.stats
    return nc


def make_pvec(cond, b_mod, norm_g, dn_conv_w, cv_dw_w, cv_dw_b, cv_ln_g, cv_ln_b, dn_norm_g):
    pv = np.zeros((384, 128), np.float32)
    pv[0:8] = cond.reshape(8, 128)
    pv[8:80] = b_mod.reshape(72, 128)
    pv[80:128] = norm_g.reshape(48, 128)
    pv[128:164] = dn_conv_w.reshape(36, 128)
    pv[164:288] = cv_dw_w.reshape(124, 128)
    pv[288:292] = cv_dw_b.reshape(4, 128)
    pv[292:296] = cv_ln_g.reshape(4, 128)
    pv[296:300] = cv_ln_b.reshape(4, 128)
    pv[300] = dn_norm_g.reshape(128)
    return pv


_NC_CACHE = {}


def kernel(x_prompt, x_sample, state_delta, c, c_ctx, w_mod, b_mod, norm_g, ffn1_w_in,
           ffn1_w_out, w_in, dn_conv_w, dn_a_log, dn_dt_bias, dn_norm_g, cv_dw_w, cv_dw_b,
           cv_ln_g, cv_ln_b, w_out, ffn2_w_in, ffn2_w_out, _debug=None,
           _stages=("ffn1", "mixer", "ffn2")):
    f = lambda a: np.ascontiguousarray(np.asarray(a, dtype=np.float32))
    x_prompt, x_sample = f(x_prompt), f(x_sample)
    key = (_debug, _stages)
    if key not in _NC_CACHE:
        _NC_CACHE[key] = build(debug=_debug, stages=_stages)
    nc = _NC_CACHE[key]
    cf, cb = make_consts()
    rep = lambda v: np.ascontiguousarray(np.broadcast_to(np.asarray(v, np.float32).reshape(1, -1), (128, np.size(v))))
    gconst = rep(np.concatenate([f(dn_a_log)[0].reshape(8), f(dn_dt_bias)[0].reshape(8)]))
    r8 = np.arange(8)
    r32 = np.arange(32)
    link8_p = (r8 % 4 != 0).astype(np.float32)
    link32_p = (r32 % 4 != 0).astype(np.float32)
    sd = f(state_delta)
    in_maps = []
    for core in range(8):
        if core < 4 or core >= 6:
            cp = core if core < 4 else 0
            xc = x_prompt[8 * cp:8 * cp + 8].reshape(T, D)
            cond = f(c_ctx)
            link8, link32v, carry = link8_p, link32_p, 0.0
            s0 = np.zeros((2, 128, 512), np.float32)
        else:
            b = core - 4
            xc = x_sample[b]
            cond = f(c)[b]
            link8, link32v, carry = np.zeros(8, np.float32), np.zeros(32, np.float32), 1.0
            s0 = np.ascontiguousarray(sd[b, 0].transpose(0, 2, 1, 3)).reshape(2, 128, 512)
        in_maps.append({
            "x": np.ascontiguousarray(xc),
            "pvec": make_pvec(cond, f(b_mod)[0], f(norm_g)[0], f(dn_conv_w)[0], f(cv_dw_w)[0], f(cv_dw_b)[0],
                              f(cv_ln_g)[0], f(cv_ln_b)[0], f(dn_norm_g)[0]),
            "w_mod": f(w_mod)[0], "ffn1_w_in": f(ffn1_w_in)[0], "ffn1_w_out": f(ffn1_w_out)[0],
            "ffn2_w_in": f(ffn2_w_in)[0], "ffn2_w_out": f(ffn2_w_out)[0],
            "w_in": f(w_in)[0], "w_out": f(w_out)[0],
            "cstf": cf, "cstb": cb, "gconst": gconst, "nlink": rep(link8 - 1.0), "link32": rep(link32v),
            "carry": np.full((128, 1), carry, np.float32), "s0": s0,
        })
    res = run_bass_kernel_spmd(nc, in_maps, core_ids=list(range(8)))
    r = res.results
    y_p = np.concatenate([r[i]["y"].reshape(8, 256, D) for i in range(4)], axis=0)
    y_s = np.stack([r[4]["y"], r[5]["y"]], axis=0)
    ns = np.concatenate([r[i]["st"].reshape(8, 2, 128, 4, 128).transpose(0, 1, 3, 2, 4) for i in range(4)], axis=0)
    ns = np.ascontiguousarray(ns.reshape(32, 1, 2, 4, 128, 128))
    if _debug:
        return (y_p, y_s, ns), [r[i]["dbg"] for i in range(8)]
    return (y_p, y_s, ns)
```
